# Optimizing a Trainium2 kernel written in Bass

```python
import math
import jax, jax.numpy as jnp
from jax import lax
import numpy as np

D_MODEL = 2048
BATCH = 4
SEQ = 8192
DEPTH = 1

SSM_EXPAND = 2
SSM_D_INNER = SSM_EXPAND * D_MODEL
SSM_HEAD_DIM = 64
SSM_HEADS = SSM_D_INNER // SSM_HEAD_DIM
SSM_GROUPS = 8
SSM_STATE = 128
SSM_CONV = 4
SSM_CHUNK = 256
SSM_CONV_DIM = SSM_D_INNER + 2 * SSM_GROUPS * SSM_STATE

ATTN_HEADS = 16
ATTN_HEAD_DIM = 128
ATTN_D = ATTN_HEADS * ATTN_HEAD_DIM
MOBA_BLOCK = 256
MOBA_TOPK = 3
Q_CHUNK = 16

N_BRANCH = 2
COL_SIZES = (SSM_D_INNER, SSM_CONV_DIM, SSM_HEADS, ATTN_D, ATTN_D, ATTN_D, ATTN_D, N_BRANCH * D_MODEL)
IN_COLS = sum(COL_SIZES)
EPS = 1e-6
NEG = -1e30

kernel_name = "hybrid_ssd_moba_gated_block"


def rmsnorm(x, w):
    xf = x.astype(jnp.float32)
    y = xf * lax.rsqrt(jnp.mean(xf * xf, axis=-1, keepdims=True) + EPS)
    return (y * w.astype(jnp.float32)).astype(x.dtype)


def group_rmsnorm(x, w, groups):
    shp = x.shape
    xf = x.astype(jnp.float32).reshape(shp[:-1] + (groups, shp[-1] // groups))
    y = xf * lax.rsqrt(jnp.mean(xf * xf, axis=-1, keepdims=True) + EPS)
    return (y.reshape(shp) * w.astype(jnp.float32)).astype(x.dtype)


def causal_dwconv(u, w, bias):
    k = w.shape[0]
    out = lax.conv_general_dilated(
        u, w[:, None, :].astype(u.dtype), window_strides=(1,), padding=[(k - 1, 0)],
        dimension_numbers=('NWC', 'WIO', 'NWC'), feature_group_count=u.shape[-1])
    return out + bias.astype(u.dtype)


def ssd_scan(xh, dt, A, Bm, Cm):
    b, s, h, p = xh.shape
    g, n = Bm.shape[2], Bm.shape[3]
    e = h // g
    L = SSM_CHUNK
    pad = (-s) % L
    if pad:
        xh = jnp.pad(xh, ((0, 0), (0, pad), (0, 0), (0, 0)))
        dt = jnp.pad(dt, ((0, 0), (0, pad), (0, 0)))
        Bm = jnp.pad(Bm, ((0, 0), (0, pad), (0, 0), (0, 0)))
        Cm = jnp.pad(Cm, ((0, 0), (0, pad), (0, 0), (0, 0)))
    nc = (s + pad) // L
    xc = xh.reshape(b, nc, L, g, e, p).transpose(1, 0, 2, 3, 4, 5)
    dtc = dt.reshape(b, nc, L, g, e).transpose(1, 0, 2, 3, 4)
    Bc = Bm.reshape(b, nc, L, g, n).transpose(1, 0, 2, 3, 4)
    Cc = Cm.reshape(b, nc, L, g, n).transpose(1, 0, 2, 3, 4)
    Ah = A.reshape(g, e)
    causal = jnp.tril(jnp.ones((L, L), dtype=bool))

    def step(state, inp):
        x_, dt_, B_, C_ = inp
        cum = jnp.cumsum(dt_ * Ah, axis=1)
        cum_t = cum.transpose(0, 2, 3, 1)
        seg = cum_t[..., :, None] - cum_t[..., None, :]
        decay = jnp.exp(jnp.where(causal, seg, -jnp.inf))
        cb = jnp.einsum('blgn,bsgn->bgls', C_, B_)
        w = cb[:, :, None] * decay * dt_.transpose(0, 2, 3, 1)[..., None, :]
        y_diag = jnp.einsum('bgels,bsgep->blgep', w, x_)
        y_off = jnp.einsum('blgn,bgepn->blgep', C_, state) * jnp.exp(cum)[..., None]
        decay_end = jnp.exp(cum[:, -1:] - cum) * dt_
        new_state = state * jnp.exp(cum[:, -1])[..., None, None] + \
            jnp.einsum('blge,blgep,blgn->bgepn', decay_end, x_, B_)
        return new_state, y_diag + y_off

    state0 = jnp.zeros((b, g, e, p, n), jnp.float32)
    _, ys = lax.scan(step, state0, (xc, dtc, Bc, Cc))
    ys = ys.transpose(1, 0, 2, 3, 4, 5).reshape(b, nc * L, h, p)[:, :s]
    return ys.astype(xh.dtype)


def gather_blocks(kv, idx):
    return jax.vmap(jax.vmap(lambda a, i: a[i]))(kv, idx)


def moba_attention(q, k, v):
    b, s, h, dh = q.shape
    bs = MOBA_BLOCK
    pad = (-s) % bs
    nb = (s + pad) // bs
    kp = jnp.pad(k, ((0, 0), (0, pad), (0, 0), (0, 0)))
    vp = jnp.pad(v, ((0, 0), (0, pad), (0, 0), (0, 0)))
    kt = kp.transpose(0, 2, 1, 3).reshape(b, h, nb, bs, dh)
    vt = vp.transpose(0, 2, 1, 3).reshape(b, h, nb, bs, dh)
    k_mean = jnp.mean(kt.astype(jnp.float32), axis=3)
    topk = min(MOBA_TOPK, nb)
    scale = dh ** -0.5
    nq = s // Q_CHUNK
    qc = q.transpose(0, 2, 1, 3).reshape(b, h, nq, Q_CHUNK, dh).transpose(2, 0, 1, 3, 4)
    starts = jnp.arange(nq, dtype=jnp.int32) * Q_CHUNK
    blk_ids = jnp.arange(nb, dtype=jnp.int32)

    def one_chunk(args):
        q_, t0 = args
        blk = t0 // bs
        qpos = t0 + jnp.arange(Q_CHUNK, dtype=jnp.int32)
        gate = jnp.einsum('bhqd,bhnd->bhqn', q_.astype(jnp.float32), k_mean)
        gate = jnp.where(blk_ids < blk, gate, NEG)
        _, idx = lax.top_k(gate, topk)
        valid = jnp.arange(topk, dtype=jnp.int32) < blk
        idx = jnp.where(valid, idx, 0)
        k_sel = gather_blocks(kt, idx)
        v_sel = gather_blocks(vt, idx)
        s_sel = jnp.einsum('bhqd,bhqjtd->bhqjt', q_, k_sel).astype(jnp.float32) * scale
        s_sel = jnp.where(valid[:, None], s_sel, NEG)
        k_own = lax.dynamic_index_in_dim(kt, blk, axis=2, keepdims=False)
        v_own = lax.dynamic_index_in_dim(vt, blk, axis=2, keepdims=False)
        s_own = jnp.einsum('bhqd,bhtd->bhqt', q_, k_own).astype(jnp.float32) * scale
        kpos = blk * bs + jnp.arange(bs, dtype=jnp.int32)
        s_own = jnp.where(kpos[None, :] <= qpos[:, None], s_own, NEG)
        scores = jnp.concatenate([s_sel.reshape(b, h, Q_CHUNK, topk * bs), s_own], axis=-1)
        pr = jax.nn.softmax(scores, axis=-1)
        p_sel = pr[..., :topk * bs].reshape(b, h, Q_CHUNK, topk, bs).astype(v.dtype)
        p_own = pr[..., topk * bs:].astype(v.dtype)
        return jnp.einsum('bhqjt,bhqjtd->bhqd', p_sel, v_sel) + jnp.einsum('bhqt,bhtd->bhqd', p_own, v_own)

    o = lax.map(one_chunk, (qc, starts))
    return o.transpose(1, 0, 3, 2, 4).reshape(b, s, h * dh)


def setup_inputs(seed: int = 0) -> dict:
    key = jax.random.key(seed)
    ks = jax.random.split(key, 16)
    f32 = jnp.float32
    x = jax.random.normal(ks[0], (BATCH, SEQ, D_MODEL), f32)
    norm_w = 1.0 + 0.02 * jax.random.normal(ks[1], (DEPTH, D_MODEL), f32)
    w_in = jax.random.normal(ks[2], (DEPTH, D_MODEL, IN_COLS), f32) * D_MODEL ** -0.5
    conv_w = jax.random.normal(ks[3], (DEPTH, SSM_CONV, SSM_CONV_DIM), f32) * SSM_CONV ** -0.5
    conv_b = 0.02 * jax.random.normal(ks[4], (DEPTH, SSM_CONV_DIM), f32)
    dt0 = jnp.exp(jax.random.uniform(ks[5], (DEPTH, SSM_HEADS), f32, math.log(1e-3), math.log(1e-1)))
    dt_bias = dt0 + jnp.log(-jnp.expm1(-dt0))
    A_log = jnp.log(jax.random.uniform(ks[6], (DEPTH, SSM_HEADS), f32, 1.0, 16.0))
    D_skip = 1.0 + 0.1 * jax.random.normal(ks[7], (DEPTH, SSM_HEADS), f32)
    ssm_norm_w = 1.0 + 0.02 * jax.random.normal(ks[8], (DEPTH, SSM_D_INNER), f32)
    w_ssm_proj = jax.random.normal(ks[9], (DEPTH, SSM_D_INNER, D_MODEL), f32) * SSM_D_INNER ** -0.5
    w_attn_proj = jax.random.normal(ks[10], (DEPTH, ATTN_D, D_MODEL), f32) * ATTN_D ** -0.5
    gate_bias = 0.02 * jax.random.normal(ks[11], (DEPTH, N_BRANCH * D_MODEL), f32)
    w_out = jax.random.normal(ks[12], (DEPTH, D_MODEL, D_MODEL), f32) * D_MODEL ** -0.5
    final_norm_w = 1.0 + 0.02 * jax.random.normal(ks[13], (D_MODEL,), f32)
    return {"x": x, "norm_w": norm_w, "w_in": w_in, "conv_w": conv_w, "conv_b": conv_b,
            "dt_bias": dt_bias, "A_log": A_log, "D_skip": D_skip, "ssm_norm_w": ssm_norm_w,
            "w_ssm_proj": w_ssm_proj, "w_attn_proj": w_attn_proj, "gate_bias": gate_bias,
            "w_out": w_out, "final_norm_w": final_norm_w}


def reference(x, norm_w, w_in, conv_w, conv_b, dt_bias, A_log, D_skip, ssm_norm_w,
              w_ssm_proj, w_attn_proj, gate_bias, w_out, final_norm_w):
    b, s, _ = x.shape
    split_pts = list(np.cumsum(COL_SIZES)[:-1])
    xbc_pts = [SSM_D_INNER, SSM_D_INNER + SSM_GROUPS * SSM_STATE]
    for l in range(DEPTH):
        h = rmsnorm(x, norm_w[l])
        proj = h @ w_in[l]
        z, xbc, dt_raw, q, k, v, g_attn, g_merge = jnp.split(proj, split_pts, axis=-1)

        xbc = jax.nn.silu(causal_dwconv(xbc, conv_w[l], conv_b[l]))
        xs, Bm, Cm = jnp.split(xbc, xbc_pts, axis=-1)
        xh = xs.reshape(b, s, SSM_HEADS, SSM_HEAD_DIM)
        dt = jax.nn.softplus(dt_raw.astype(jnp.float32) + dt_bias[l].astype(jnp.float32))
        A = -jnp.exp(A_log[l].astype(jnp.float32))
        y = ssd_scan(xh, dt, A,
                     Bm.reshape(b, s, SSM_GROUPS, SSM_STATE), Cm.reshape(b, s, SSM_GROUPS, SSM_STATE))
        y = y + D_skip[l].astype(y.dtype)[:, None] * xh
        y = y.reshape(b, s, SSM_D_INNER) * jax.nn.silu(z)
        y = group_rmsnorm(y, ssm_norm_w[l], SSM_GROUPS)
        y_ssm = y @ w_ssm_proj[l]

        o = moba_attention(q.reshape(b, s, ATTN_HEADS, ATTN_HEAD_DIM),
                           k.reshape(b, s, ATTN_HEADS, ATTN_HEAD_DIM),
                           v.reshape(b, s, ATTN_HEADS, ATTN_HEAD_DIM))
        y_attn = (o * jax.nn.silu(g_attn)) @ w_attn_proj[l]

        gates = jax.nn.sigmoid(g_merge + gate_bias[l].astype(g_merge.dtype))
        g_ssm, g_att = jnp.split(gates, 2, axis=-1)
        mixed = g_ssm * y_ssm + g_att * y_attn
        x = x + mixed @ w_out[l]
    return rmsnorm(x, final_norm_w)
```

```python
import numpy as np
from contextlib import ExitStack
import concourse.bass as bass
import concourse.mybir as mybir
from concourse.bass_utils import run_bass_kernel_spmd
import ml_dtypes

F32 = mybir.dt.float32
BF16 = mybir.dt.bfloat16
AF = mybir.ActivationFunctionType
ALU = mybir.AluOpType
AX = mybir.AxisListType

D = 2048
EPS = 1e-6
NEGBIG = -1.0e30


class Sched:
    def __init__(self, nc, es):
        self.nc = nc
        self.es = es
        self.E = {'pe': nc.tensor, 'act': nc.scalar, 'dve': nc.vector, 'pool': nc.gpsimd, 'sp': nc.sync}
        self.sems = {}
        self.val = {}
        for k in ('pe', 'act', 'dve', 'pool'):
            self.sems[k] = es.enter_context(nc.semaphore('c_' + k))
            self.val[k] = 0
        self.seen = {e: {} for e in self.E}
        self.w = {}
        self.r = {}
        self.nsem = 4

    def _wait(self, e, toks):
        need = {}
        for (k, v) in toks:
            if not isinstance(k, str):
                v = self.val[k]
            if need.get(k, 0) < v:
                need[k] = v
        for k, v in need.items():
            if k == e and e == 'pe':
                continue
            if self.seen[e].get(k, 0) >= v:
                continue
            self.E[e].wait_ge(self.sems[k], v)
            self.seen[e][k] = v

    def _key(self, r):
        if isinstance(r, tuple):
            return tuple(self._key(x) for x in r)
        if isinstance(r, (str, int)):
            return r
        return ('id', id(r))

    def _deps(self, e, reads, writes):
        toks = []
        for r in reads:
            toks += self.w.get(r, [])
        for w_ in writes:
            toks += self.w.get(w_, [])
            toks += self.r.get(w_, [])
        return toks

    def op(self, e, fn, reads=(), writes=(), signal=True):
        reads = [self._key(r) for r in reads]
        writes = [self._key(r) for r in writes]
        self._wait(e, self._deps(e, reads, writes))
        inst = fn()
        if signal:
            self.val[e] += 1
            inst.then_inc(self.sems[e], 1)
            tok = (e, self.val[e])
            for w_ in writes:
                self.w[w_] = [tok]
                self.r[w_] = []
        else:
            tok = (e, self.val[e] + 1)
            for w_ in writes:
                self.w[w_] = [tok]
                self.r[w_] = []
        for r in reads:
            self.r.setdefault(r, []).append(tok)
        return tok

    def dma(self, q, out, in_, reads=(), writes=(), slot=None):
        reads = [self._key(r) for r in reads]
        writes = [self._key(r) for r in writes]
        self._wait(q, self._deps(q, reads, writes))
        key = ('d', slot)
        if key not in self.sems:
            self.sems[key] = self.es.enter_context(self.nc.semaphore('d%d' % self.nsem))
            self.nsem += 1
            self.val[key] = 0
        self.val[key] += 16
        self.E[q].dma_start(out=out, in_=in_).then_inc(self.sems[key], 16)
        tok = (key, self.val[key])
        for w_ in writes:
            self.w[w_] = [tok]
            self.r[w_] = []
        for r in reads:
            self.r.setdefault(r, []).append(tok)
        return tok

    def barrier(self):
        for e in ('pe', 'act', 'dve', 'pool', 'sp'):
            for k, v in self.val.items():
                if v > 0 and k != e and self.seen[e].get(k, 0) < v:
                    self.E[e].wait_ge(self.sems[k], v)
                    self.seen[e][k] = v
        self.w.clear()
        self.r.clear()


class Ring:
    def __init__(self, tiles):
        self.t = tiles
        self.i = -1

    def next(self):
        self.i = (self.i + 1) % len(self.t)
        return self.t[self.i]


class _Stop(Exception):
    pass


def build(T, pair_groups, stop=None, debug=False):
    holder = {}
    try:
        _build(T, pair_groups, stop, holder, debug)
    except _Stop:
        holder['es'].close()
    return holder['nc']


def _build(T, pair_groups, stop, holder, debug):
    assert T % 512 == 0
    NT = T // 128
    NCH = T // 256
    TS = min(T, 2048)
    NSB = T // TS
    TH = T // 2
    nc = bass.Bass("TRN2", target_bir_lowering=False)

    def din(name, shape, dt=F32):
        return nc.dram_tensor(name, shape, dt, kind="ExternalInput")

    x_d = din("x", [T, D])
    wfm_d = din("wfm", [40, 128, 2048])
    wtm_d = din("wtm", [8, 128, 8192])
    wdt_d = din("wdt", [128, 512])
    wgm_d = din("wgm", [32, 128, 2048])
    wssm_d = din("wssm", [16, 128, 4096])
    wattn_d = din("wattn", [16, 128, 2048])
    wout_d = din("wout", [4, 128, 8192])
    cst_d = din("cst", [128, 1024])
    dsk_d = din("dsk", [128, 2048])
    fnw_d = din("fnw", [128, 2048])
    xf_d = din("xf", [T // 2, D])
    cstb_d = din("cstb", [128, 1024], BF16)
    out_d = nc.dram_tensor("out", [TH, D], F32, kind="ExternalOutput")

    def scr(name, shape, dt=BF16):
        if debug and not name.startswith(("ynT", "ogT")):
            return nc.dram_tensor(name, shape, dt, kind="ExternalOutput")
        return nc.dram_tensor(name, shape, dt)

    hT_s = scr("hT_s", [16, 128, T])
    wfm_s = scr("wfm_s", [40, 128, 2048])
    wtm_s = scr("wtm_s", [8, 128, 8192])
    wdt_s = scr("wdt_s", [128, 512])
    wgm_s = scr("wgm_s", [32, 128, 2048])
    wssm_s = scr("wssm_s", [16, 128, 4096])
    wattn_s = scr("wattn_s", [16, 128, 2048])
    wout_s = scr("wout_s", [4, 128, 8192])
    xtm_s = scr("xtm_s", [T, 2048])
    btm_s = scr("btm_s", [T, 512])
    bT_s = scr("bT_s", [512, T])
    cT_s = scr("cT_s", [512, T])
    qT_s = scr("qT_s", [1024, T])
    kT_s = scr("kT_s", [1024, T])
    zs_s = scr("zs_s", [T, 2048])
    v_s = scr("v_s", [T, 1024])
    gs_s = scr("gs_s", [T, 1024])
    dt_s = scr("dt_s", [T, 32], F32)
    ynT_s = scr("ynT_s", [16, 128, T])
    ynT_g = scr("ynT_g", [16, 256, T])
    ogT_s = scr("ogT_s", [8, 128, T])
    ogT_g = scr("ogT_g", [8, 256, T])

    es = ExitStack()
    holder['nc'] = nc
    holder['es'] = es
    S = Sched(nc, es)

    def chk(name):
        if stop == name:
            S.barrier()
            raise _Stop()
    cc_sem = es.enter_context(nc.semaphore("cc"))
    n_cc = [0]

    uid = [0]

    def sb(es_, shape, dt, name=None):
        uid[0] += 1
        return es_.enter_context(nc.sbuf_tensor("%s%d" % (name or "t", uid[0]), shape, dt))

    def ps(es_, shape, dt, name=None):
        uid[0] += 1
        return es_.enter_context(nc.psum_tensor("%s%d" % (name or "p", uid[0]), shape, dt))

    def ring(es_, n, shape, dt, name=None, psum=False):
        return Ring([(ps if psum else sb)(es_, shape, dt, name) for _ in range(n)])

    cst = sb(es, [128, 1024], F32, "cst")
    cstb = sb(es, [128, 1024], BF16, "cstb")
    S.dma('sp', cst[:], cst_d[:, :], writes=[cst], slot='cst')
    S.dma('sp', cstb[:], cstb_d[:, :], writes=[cstb], slot='cstb')
    C_NORMW = 0
    C_CW = 16
    C_CB = 112
    C_DTB = 136
    C_ALOG = 168
    C_GNW = 200
    C_GB = 216
    C_BL = 248
    C_U = 256
    C_L0 = 512
    C_ONES = 640
    C_IDF = 768
    B_ID = 0
    B_CM0 = 128
    B_CM1 = 384
    identb = cstb[:, B_ID:B_ID + 128]
    identf = cst[:, C_IDF:C_IDF + 128]
    S.barrier()
    S.op('act', lambda: nc.scalar.activation(out=cst[:, C_ALOG:C_ALOG + 32], in_=cst[:, C_ALOG:C_ALOG + 32], func=AF.Exp),
         reads=[cst], writes=[cst])
    S.op('dve', lambda: nc.vector.tensor_scalar(out=cst[:, C_ALOG:C_ALOG + 32], in0=cst[:, C_ALOG:C_ALOG + 32],
                                                 scalar1=-1.0, scalar2=None, op0=ALU.mult),
         reads=[cst], writes=[cst])
    S.barrier()
    A_bc = cst[:, C_ALOG:C_ALOG + 32]

    with ExitStack() as pes:
        win = ring(pes, 3, [128, 2048], F32, "win")
        wo = ring(pes, 3, [128, 2048], BF16, "wo")
        normw_b = cst[:, C_NORMW:C_NORMW + 16]
        cnt = [0]

        def conv_tile(src_ap, dst_ap, kcols, fold, k0=0, nk=16):
            a = win.next()
            o = wo.next()
            n = nk * kcols
            S.dma('sp', a[:, :n], src_ap, writes=[a], slot=('win', win.i))
            if fold:
                S.op('dve', lambda: nc.vector.tensor_tensor(
                    out=o[:, :n].rearrange("p (k c) -> p k c", k=nk),
                    in0=a[:, :n].rearrange("p (k c) -> p k c", k=nk),
                    in1=normw_b[:, k0:k0 + nk].unsqueeze(2).broadcast_to([128, nk, kcols]), op=ALU.mult),
                    reads=[a], writes=[o])
            else:
                cnt[0] += 1
                if cnt[0] % 2:
                    S.op('act', lambda: nc.scalar.copy(out=o[:, :n], in_=a[:, :n]), reads=[a], writes=[o])
                else:
                    S.op('dve', lambda: nc.vector.tensor_copy(out=o[:, :n], in_=a[:, :n]), reads=[a], writes=[o])
            S.dma('pool', dst_ap, o[:, :n], reads=[o], slot=('wo', wo.i))

        for c in range(40):
            conv_tile(wfm_d[c], wfm_s[c], 128, True)
        for c in range(8):
            for kq in range(4):
                conv_tile(wtm_d[c][:, kq * 2048:(kq + 1) * 2048], wtm_s[c][:, kq * 2048:(kq + 1) * 2048], 512, True, k0=kq * 4, nk=4)
        conv_tile(wdt_d[:, :], wdt_s[:, :], 32, True)
        for c in range(32):
            conv_tile(wgm_d[c], wgm_s[c], 128, True)
        for c in range(16):
            for hq in range(2):
                conv_tile(wssm_d[c][:, hq * 2048:(hq + 1) * 2048], wssm_s[c][:, hq * 2048:(hq + 1) * 2048], 128, False)
        for c in range(16):
            conv_tile(wattn_d[c], wattn_s[c], 128, False)
        for c in range(4):
            for kq in range(4):
                conv_tile(wout_d[c][:, kq * 2048:(kq + 1) * 2048], wout_s[c][:, kq * 2048:(kq + 1) * 2048], 512, False, nk=4)
        S.barrier()
    chk('W')

    with ExitStack() as pes:
        xring = ring(pes, 2, [128, 2048], F32, "xt")
        junk = sb(pes, [128, 2048], BF16, "junk")
        hbr = ring(pes, 2, [128, 2048], BF16, "hb")
        stg = ring(pes, 2, [128, 16, 512], BF16, "hst")
        smr = ring(pes, 4, [128, 4], F32, "sm")
        ptr = ring(pes, 2, [128, 16, 128], BF16, "ptr", psum=True)
        for i4 in range(T // 512):
            stage = stg.next()
            for ii in range(4):
                i = i4 * 4 + ii
                xt = xring.next()
                S.dma('sp', xt[:], x_d[i * 128:(i + 1) * 128, :], writes=[xt], slot=('xt', xring.i))
                sm = smr.next()
                S.op('act', lambda: nc.scalar.activation(out=junk[:], in_=xt[:], func=AF.Square, accum_out=sm[:, 0:1]),
                     reads=[xt], writes=[junk, sm])
                S.op('dve', lambda: nc.vector.tensor_scalar(out=sm[:, 1:2], in0=sm[:, 0:1], scalar1=1.0 / D, scalar2=EPS,
                                                             op0=ALU.mult, op1=ALU.add), reads=[sm], writes=[sm])
                S.op('act', lambda: nc.scalar.activation(out=sm[:, 2:3], in_=sm[:, 1:2], func=AF.Sqrt), reads=[sm], writes=[sm])
                S.op('dve', lambda: nc.vector.reciprocal(out=sm[:, 3:4], in_=sm[:, 2:3]), reads=[sm], writes=[sm])
                hb = hbr.next()
                S.op('dve', lambda: nc.vector.tensor_scalar(out=hb[:], in0=xt[:], scalar1=sm[:, 3:4], scalar2=None, op0=ALU.mult),
                     reads=[xt, sm], writes=[hb])
                pt = ptr.next()
                for k in range(16):
                    S.op('pe', lambda: nc.tensor.transpose(out=pt[:, k, :], in_=hb[:, k * 128:(k + 1) * 128], identity=identb),
                         reads=[hb], writes=[pt], signal=(k == 15))
                S.op('act', lambda: nc.scalar.copy(out=stage[:, :, ii * 128:(ii + 1) * 128], in_=pt[:]), reads=[pt], writes=[stage])
            S.dma('pool', hT_s[:, :, i4 * 512:(i4 + 1) * 512].rearrange("k p t -> p k t"), stage[:], reads=[stage], slot=('hst', stg.i))
        S.barrier()
    chk('A')

    with ExitStack() as pes:
        hT = sb(pes, [128, 16, TS], BF16, "hT")
        wfr = ring(pes, 3, [128, 16, 128], BF16, "wf")
        wtr = ring(pes, 2, [128, 16, 512], BF16, "wt")
        wdt = sb(pes, [128, 16, 32], BF16, "wdt")
        halo = sb(pes, [128, 24, 4], F32, "halo")
        xsr = ring(pes, 2, [128, 516], F32, "xs")
        accr = ring(pes, 2, [128, 512], F32, "acc")
        fmo = ring(pes, 3, [128, 512], BF16, "fmo")
        tmo = ring(pes, 3, [128, 512], BF16, "tmo")
        tms = ring(pes, 2, [128, 4, 128], BF16, "tms")
        dtr = ring(pes, 2, [128, 32], F32, "dtr")
        pacc = ring(pes, 4, [128, 512], F32, "pacc", psum=True)
        ptp = ring(pes, 2, [128, 4, 128], BF16, "ptp", psum=True)
        S.op('dve', lambda: nc.vector.memset(halo[:], 0.0), writes=[halo])
        S.dma('sp', wdt[:], wdt_s[:, :].rearrange("p (k c) -> p k c", k=16), writes=[wdt], slot='wdt')
        for sbi in range(NSB):
            t0 = sbi * TS
            for k in range(16):
                S.dma('sp', hT[:, k, :], hT_s[k][:, t0:t0 + TS], writes=[hT], slot='hT')
            for c in range(40):
                wf = wfr.next()
                S.dma('sp', wf[:], wfm_s[c].rearrange("p (k c) -> p k c", k=16), writes=[wf], slot=('wf', wfr.i))
                for tt in range(TS // 512):
                    tok0 = t0 + tt * 512
                    pa = pacc.next()
                    for k in range(16):
                        S.op('pe', lambda: nc.tensor.matmul(pa[:], lhsT=wf[:, k, :], rhs=hT[:, k, tt * 512:(tt + 1) * 512],
                                                            start=(k == 0), stop=(k == 15)),
                             reads=[wf, hT], writes=[pa], signal=(k == 15))
                    if c < 24:
                        xs = xsr.next()
                        S.op('act', lambda: nc.scalar.copy(out=xs[:, 3:515], in_=pa[:]), reads=[pa], writes=[(xs, 'b')])
                        S.op('dve', lambda: nc.vector.tensor_copy(out=xs[:, 0:3], in_=halo[:, c, 0:3]), reads=[halo], writes=[(xs, 'h')])
                        S.op('dve', lambda: nc.vector.tensor_copy(out=halo[:, c, 0:3], in_=xs[:, 512:515]), reads=[(xs, 'b')], writes=[halo])
                        acc = accr.next()
                        cw = C_CW + c * 4
                        S.op('dve', lambda: nc.vector.tensor_scalar(out=acc[:], in0=xs[:, 0:512], scalar1=cst[:, cw:cw + 1],
                                                                     scalar2=cst[:, C_CB + c:C_CB + c + 1], op0=ALU.mult, op1=ALU.add),
                             reads=[(xs, 'b'), (xs, 'h')], writes=[acc])
                        for k in range(1, 4):
                            S.op('dve', lambda: nc.vector.scalar_tensor_tensor(out=acc[:], in0=xs[:, k:k + 512], scalar=cst[:, cw + k:cw + k + 1],
                                                                                in1=acc[:], op0=ALU.mult, op1=ALU.add),
                                 reads=[(xs, 'b'), (xs, 'h'), acc], writes=[acc])
                        fo = fmo.next()
                        S.op('act', lambda: nc.scalar.activation(out=fo[:], in_=acc[:], func=AF.Silu), reads=[acc], writes=[fo])
                        if c >= 16:
                            dst = (bT_s if c < 20 else cT_s)[((c - 16) % 4) * 128:((c - 16) % 4 + 1) * 128, tok0:tok0 + 512]
                            S.dma('pool', dst, fo[:], reads=[fo], slot=('fmo', fmo.i))
                        if c < 20:
                            pt = ptp.next()
                            for j in range(4):
                                S.op('pe', lambda: nc.tensor.transpose(out=pt[:, j, :], in_=fo[:, j * 128:(j + 1) * 128], identity=identb),
                                     reads=[fo], writes=[pt], signal=(j == 3))
                            ts_ = tms.next()
                            S.op('act', lambda: nc.scalar.copy(out=ts_[:], in_=pt[:]), reads=[pt], writes=[ts_])
                            if c < 16:
                                dst = xtm_s[tok0:tok0 + 512, c * 128:(c + 1) * 128]
                            else:
                                dst = btm_s[tok0:tok0 + 512, (c - 16) * 128:(c - 15) * 128]
                            S.dma('pool', dst.rearrange("(j p) c -> p j c", p=128), ts_[:], reads=[ts_], slot=('tms', tms.i))
                    else:
                        fo = fmo.next()
                        S.op('act', lambda: nc.scalar.copy(out=fo[:], in_=pa[:]), reads=[pa], writes=[fo])
                        cc_ = c - 24
                        dst = (qT_s if cc_ < 8 else kT_s)[(cc_ % 8) * 128:(cc_ % 8 + 1) * 128, tok0:tok0 + 512]
                        S.dma('pool', dst, fo[:], reads=[fo], slot=('fmo', fmo.i))
            for blk in range(8):
                wt = wtr.next()
                S.dma('sp', wt[:], wtm_s[blk].rearrange("p (k c) -> p k c", k=16), writes=[wt], slot=('wt', wtr.i))
                for tt in range(TS // 128):
                    tok0 = t0 + tt * 128
                    pa = pacc.next()
                    for k in range(16):
                        S.op('pe', lambda: nc.tensor.matmul(pa[:], lhsT=hT[:, k, tt * 128:(tt + 1) * 128], rhs=wt[:, k, :],
                                                            start=(k == 0), stop=(k == 15)),
                             reads=[wt, hT], writes=[pa], signal=(k == 15))
                    to = tmo.next()
                    if blk < 4:
                        S.op('act', lambda: nc.scalar.activation(out=to[:], in_=pa[:], func=AF.Silu), reads=[pa], writes=[to])
                        dst = zs_s[tok0:tok0 + 128, blk * 512:(blk + 1) * 512]
                    elif blk < 6:
                        S.op('act', lambda: nc.scalar.copy(out=to[:], in_=pa[:]), reads=[pa], writes=[to])
                        dst = v_s[tok0:tok0 + 128, (blk - 4) * 512:(blk - 3) * 512]
                    else:
                        S.op('act', lambda: nc.scalar.activation(out=to[:], in_=pa[:], func=AF.Silu), reads=[pa], writes=[to])
                        dst = gs_s[tok0:tok0 + 128, (blk - 6) * 512:(blk - 5) * 512]
                    S.dma('pool', dst, to[:], reads=[to], slot=('tmo', tmo.i))
            for tt in range(TS // 128):
                tok0 = t0 + tt * 128
                pa = pacc.next()
                for k in range(16):
                    S.op('pe', lambda: nc.tensor.matmul(pa[:, 0:32], lhsT=hT[:, k, tt * 128:(tt + 1) * 128], rhs=wdt[:, k, :],
                                                        start=(k == 0), stop=(k == 15)),
                         reads=[wdt, hT], writes=[pa], signal=(k == 15))
                d_ = dtr.next()
                S.op('dve', lambda: nc.vector.tensor_tensor(out=d_[:], in0=pa[:, 0:32], in1=cst[:, C_DTB:C_DTB + 32], op=ALU.add),
                     reads=[pa], writes=[d_])
                S.op('act', lambda: nc.scalar.activation(out=d_[:], in_=d_[:], func=AF.Exp), reads=[d_], writes=[d_])
                S.op('act', lambda: nc.scalar.activation(out=d_[:], in_=d_[:], func=AF.Ln, bias=1.0, scale=1.0), reads=[d_], writes=[d_])
                S.dma('pool', dt_s[tok0:tok0 + 128, :], d_[:], reads=[d_], slot=('dtr', dtr.i))
            S.barrier()
    chk('P')

    with ExitStack() as pes:
        xr = ring(pes, 2, [128, 2, 2048], BF16, "sx")
        br = ring(pes, 2, [128, 2, 512], BF16, "sb")
        bTr = ring(pes, 2, [128, 4, 256], BF16, "sbT")
        cTr = ring(pes, 2, [128, 4, 256], BF16, "scT")
        dtr2 = ring(pes, 2, [128, 2, 32], F32, "sdt")
        zr = ring(pes, 2, [128, 2, 2048], BF16, "sz")
        state = sb(pes, [128, 4, 512], F32, "state")
        stbf = sb(pes, [128, 4, 512], BF16, "stbf")
        dtA = sb(pes, [128, 2, 32], F32, "dtA")
        cumT = sb(pes, [128, 2, 32], F32, "cumT")
        cend = sb(pes, [128, 32], F32, "cend")
        ecum = sb(pes, [128, 2, 32], F32, "ecum")
        dend = sb(pes, [128, 2, 32], F32, "dend")
        sdec = sb(pes, [128, 32], F32, "sdec")
        cbm = ring(pes, 2, [128, 384], F32, "cbm")
        lhr = ring(pes, 3, [128, 384], F32, "lh")
        er = ring(pes, 3, [128, 384], BF16, "er")
        wr = ring(pes, 3, [128, 384], BF16, "wr")
        t1r = ring(pes, 2, [128, 512], F32, "t1")
        t2r = ring(pes, 2, [128, 512], F32, "t2")
        smr = ring(pes, 4, [128, 4], F32, "ssm")
        ynr = ring(pes, 2, [128, 512], BF16, "yn")
        yts = ring(pes, 2, [128, 4, 128], BF16, "yts")
        xdr = ring(pes, 2, [128, 2, 512], BF16, "xd")
        p_cum = ps(pes, [128, 512], F32, "pcum")
        p_cb = ring(pes, 1, [128, 512], F32, "pcb", psum=True)
        p_seg = ring(pes, 2, [128, 512], F32, "pseg", psum=True)
        p_yd = ring(pes, 1, [128, 2, 512], F32, "pyd", psum=True)
        p_yo = ring(pes, 1, [128, 512], F32, "pyo", psum=True)
        p_tr = ring(pes, 1, [128, 4, 128], BF16, "ptr2", psum=True)
        junk = sb(pes, [128, 512], BF16, "junk2")
        dsk = sb(pes, [128, 2048], F32, "dsk")
        S.dma('sp', dsk[:], dsk_d[:, :], writes=[dsk], slot='dsk')
        U2 = cst[:, C_U:C_U + 256]
        U1 = cst[:, C_U:C_U + 128]
        L0 = cst[:, C_L0:C_L0 + 128]
        ONES = cst[:, C_ONES:C_ONES + 128]
        S.op('dve', lambda: nc.vector.memset(state[:], 0.0), writes=[state])
        S.op('dve', lambda: nc.vector.memset(stbf[:], 0.0), writes=[stbf])
        for c in range(NCH):
            tok0 = c * 256
            x2 = xr.next(); b2 = br.next(); bT = bTr.next(); cT = cTr.next(); dt2 = dtr2.next(); z2 = zr.next()
            S.dma('sp', x2[:], xtm_s[tok0:tok0 + 256, :].rearrange("(j p) c -> p j c", p=128), writes=[x2], slot=('sx', xr.i))
            S.dma('sp', b2[:], btm_s[tok0:tok0 + 256, :].rearrange("(j p) c -> p j c", p=128), writes=[b2], slot=('sb', br.i))
            S.dma('sp', bT[:], bT_s[:, tok0:tok0 + 256].rearrange("(g p) t -> p g t", p=128), writes=[bT], slot=('sbT', bTr.i))
            S.dma('sp', cT[:], cT_s[:, tok0:tok0 + 256].rearrange("(g p) t -> p g t", p=128), writes=[cT], slot=('scT', cTr.i))
            S.dma('sp', dt2[:], dt_s[tok0:tok0 + 256, :].rearrange("(j p) c -> p j c", p=128), writes=[dt2], slot=('sdt', dtr2.i))
            S.dma('sp', z2[:], zs_s[tok0:tok0 + 256, :].rearrange("(j p) c -> p j c", p=128), writes=[z2], slot=('sz', zr.i))
            S.op('dve', lambda: nc.vector.tensor_tensor(out=dtA[:], in0=dt2[:], in1=A_bc.unsqueeze(1).broadcast_to([128, 2, 32]), op=ALU.mult),
                 reads=[dt2], writes=[dtA])
            S.op('pe', lambda: nc.tensor.matmul(p_cum[:, 0:32], lhsT=U1, rhs=dtA[:, 0, :], start=True, stop=True), reads=[dtA], writes=[p_cum], signal=False)
            S.op('pe', lambda: nc.tensor.matmul(p_cum[:, 32:64], lhsT=ONES, rhs=dtA[:, 0, :], start=True, stop=False), reads=[dtA], writes=[p_cum], signal=False)
            S.op('pe', lambda: nc.tensor.matmul(p_cum[:, 32:64], lhsT=U1, rhs=dtA[:, 1, :], start=False, stop=True), reads=[dtA], writes=[p_cum], signal=False)
            S.op('pe', lambda: nc.tensor.matmul(p_cum[:, 64:96], lhsT=ONES, rhs=dtA[:, 0, :], start=True, stop=False), reads=[dtA], writes=[p_cum], signal=False)
            S.op('pe', lambda: nc.tensor.matmul(p_cum[:, 64:96], lhsT=ONES, rhs=dtA[:, 1, :], start=False, stop=True), reads=[dtA], writes=[p_cum])
            S.op('dve', lambda: nc.vector.tensor_copy(out=cumT[:].rearrange("p j h -> p (j h)"), in_=p_cum[:, 0:64]), reads=[p_cum], writes=[cumT])
            S.op('dve', lambda: nc.vector.tensor_copy(out=cend[:], in_=p_cum[:, 64:96]), reads=[p_cum], writes=[cend])
            S.op('act', lambda: nc.scalar.activation(out=ecum[:], in_=cumT[:], func=AF.Exp), reads=[cumT], writes=[ecum])
            S.op('act', lambda: nc.scalar.activation(out=sdec[:], in_=cend[:], func=AF.Exp), reads=[cend], writes=[sdec])
            S.op('dve', lambda: nc.vector.tensor_tensor(out=dend[:], in0=cend[:].unsqueeze(1).broadcast_to([128, 2, 32]), in1=cumT[:], op=ALU.subtract),
                 reads=[cend, cumT], writes=[dend])
            S.op('act', lambda: nc.scalar.activation(out=dend[:], in_=dend[:], func=AF.Exp), reads=[dend], writes=[dend])
            S.op('dve', lambda: nc.vector.tensor_tensor(out=dend[:], in0=dend[:], in1=dt2[:], op=ALU.mult), reads=[dend, dt2], writes=[dend])
            for g in range(4):
                pcb = p_cb.next()
                S.op('pe', lambda: nc.tensor.matmul(pcb[:, 0:256], lhsT=bT[:, g, 0:128], rhs=cT[:, g, 0:256], start=True, stop=True),
                     reads=[bT, cT], writes=[pcb], signal=False)
                S.op('pe', lambda: nc.tensor.matmul(pcb[:, 256:384], lhsT=bT[:, g, 128:256], rhs=cT[:, g, 128:256], start=True, stop=True),
                     reads=[bT, cT], writes=[pcb])
                cb_ = cbm.next()
                S.op('dve', lambda: nc.vector.tensor_tensor(out=cb_[:, 0:256], in0=pcb[:, 0:256], in1=U2, op=ALU.mult), reads=[pcb], writes=[(cb_, 0)])
                S.op('dve', lambda: nc.vector.tensor_tensor(out=cb_[:, 256:384], in0=pcb[:, 256:384], in1=U1, op=ALU.mult), reads=[pcb], writes=[(cb_, 1)])
                pyd = p_yd.next()
                for e in range(8):
                    h = g * 8 + e
                    lh = lhr.next()
                    S.op('dve', lambda: nc.vector.tensor_scalar(out=lh[:, 0:128], in0=L0, scalar1=dtA[:, 0, h:h + 1], scalar2=None, op0=ALU.mult),
                         reads=[dtA], writes=[(lh, 0)])
                    S.op('dve', lambda: nc.vector.tensor_scalar(out=lh[:, 128:256], in0=ONES, scalar1=dtA[:, 1, h:h + 1], scalar2=None, op0=ALU.mult),
                         reads=[dtA], writes=[(lh, 1)])
                    S.op('dve', lambda: nc.vector.tensor_scalar(out=lh[:, 256:384], in0=L0, scalar1=dtA[:, 1, h:h + 1], scalar2=None, op0=ALU.mult),
                         reads=[dtA], writes=[(lh, 2)])
                    pseg = p_seg.next()
                    S.op('pe', lambda: nc.tensor.matmul(pseg[:, 0:256], lhsT=lh[:, 0:128], rhs=U2, start=True, stop=False),
                         reads=[(lh, 0)], writes=[pseg], signal=False)
                    S.op('pe', lambda: nc.tensor.matmul(pseg[:, 128:256], lhsT=lh[:, 128:256], rhs=U1, start=False, stop=True),
                         reads=[(lh, 1)], writes=[pseg], signal=False)
                    S.op('pe', lambda: nc.tensor.matmul(pseg[:, 256:384], lhsT=lh[:, 256:384], rhs=U1, start=True, stop=True),
                         reads=[(lh, 2)], writes=[pseg])
                    ee = er.next()
                    S.op('act', lambda: nc.scalar.activation(out=ee[:], in_=pseg[:, 0:384], func=AF.Exp), reads=[pseg], writes=[ee])
                    ww = wr.next()
                    S.op('dve', lambda: nc.vector.scalar_tensor_tensor(out=ww[:, 0:256], in0=ee[:, 0:256], scalar=dt2[:, 0, h:h + 1], in1=cb_[:, 0:256],
                                                                        op0=ALU.mult, op1=ALU.mult), reads=[ee, dt2, (cb_, 0)], writes=[(ww, 0)])
                    S.op('dve', lambda: nc.vector.scalar_tensor_tensor(out=ww[:, 256:384], in0=ee[:, 256:384], scalar=dt2[:, 1, h:h + 1], in1=cb_[:, 256:384],
                                                                        op0=ALU.mult, op1=ALU.mult), reads=[ee, dt2, (cb_, 1)], writes=[(ww, 1)])
                    xs0 = x2[:, 0, h * 64:(h + 1) * 64]
                    xs1 = x2[:, 1, h * 64:(h + 1) * 64]
                    S.op('pe', lambda: nc.tensor.matmul(pyd[:, 0, e * 64:(e + 1) * 64], lhsT=ww[:, 0:128], rhs=xs0, start=True, stop=True),
                         reads=[(ww, 0), x2], writes=[pyd], signal=False)
                    S.op('pe', lambda: nc.tensor.matmul(pyd[:, 1, e * 64:(e + 1) * 64], lhsT=ww[:, 128:256], rhs=xs0, start=True, stop=False),
                         reads=[(ww, 0), x2], writes=[pyd], signal=False)
                    S.op('pe', lambda: nc.tensor.matmul(pyd[:, 1, e * 64:(e + 1) * 64], lhsT=ww[:, 256:384], rhs=xs1, start=False, stop=True),
                         reads=[(ww, 1), x2], writes=[pyd], signal=(e == 7))
                pyo_l = []
                for lt in range(2):
                    pyo = p_yo.next()
                    S.op('pe', lambda: nc.tensor.matmul(pyo[:], lhsT=cT[:, g, lt * 128:(lt + 1) * 128], rhs=stbf[:, g, :], start=True, stop=True),
                         reads=[cT, stbf], writes=[pyo])
                    t1 = t1r.next(); t2 = t2r.next()
                    S.op('act', lambda: nc.scalar.copy(out=t1[:], in_=pyd[:, lt, :]), reads=[pyd], writes=[t1])
                    S.op('dve', lambda: nc.vector.tensor_tensor(out=t2[:].rearrange("p (e q) -> p e q", e=8), in0=pyo[:].rearrange("p (e q) -> p e q", e=8),
                                                                 in1=ecum[:, lt, g * 8:(g + 1) * 8].unsqueeze(2).broadcast_to([128, 8, 64]), op=ALU.mult),
                         reads=[pyo, ecum], writes=[t2])
                    S.op('dve', lambda: nc.vector.tensor_tensor(out=t1[:], in0=t1[:], in1=t2[:], op=ALU.add), reads=[t1, t2], writes=[t1])
                    S.op('dve', lambda: nc.vector.tensor_tensor(out=t2[:], in0=x2[:, lt, g * 512:(g + 1) * 512],
                                                                 in1=dsk[:, g * 512:(g + 1) * 512], op=ALU.mult), reads=[x2, dsk], writes=[t2])
                    S.op('dve', lambda: nc.vector.tensor_tensor(out=t1[:], in0=t1[:], in1=t2[:], op=ALU.add), reads=[t1, t2], writes=[t1])
                    S.op('dve', lambda: nc.vector.tensor_tensor(out=t1[:], in0=t1[:], in1=z2[:, lt, g * 512:(g + 1) * 512], op=ALU.mult),
                         reads=[t1, z2], writes=[t1])
                    sm = smr.next()
                    S.op('act', lambda: nc.scalar.activation(out=junk[:], in_=t1[:], func=AF.Square, accum_out=sm[:, 0:1]), reads=[t1], writes=[junk, sm])
                    S.op('dve', lambda: nc.vector.tensor_scalar(out=sm[:, 1:2], in0=sm[:, 0:1], scalar1=1.0 / 512, scalar2=EPS, op0=ALU.mult, op1=ALU.add),
                         reads=[sm], writes=[sm])
                    S.op('act', lambda: nc.scalar.activation(out=sm[:, 2:3], in_=sm[:, 1:2], func=AF.Sqrt), reads=[sm], writes=[sm])
                    S.op('dve', lambda: nc.vector.reciprocal(out=sm[:, 3:4], in_=sm[:, 2:3]), reads=[sm], writes=[sm])
                    yn = ynr.next()
                    S.op('dve', lambda: nc.vector.tensor_scalar(out=yn[:], in0=t1[:], scalar1=sm[:, 3:4], scalar2=None, op0=ALU.mult), reads=[t1, sm], writes=[yn])
                    ptr_ = p_tr.next()
                    for j in range(4):
                        S.op('pe', lambda: nc.tensor.transpose(out=ptr_[:, j, :], in_=yn[:, j * 128:(j + 1) * 128], identity=identb),
                             reads=[yn], writes=[ptr_], signal=(j == 3))
                    yt = yts.next()
                    S.op('dve', lambda: nc.vector.tensor_tensor(out=yt[:], in0=ptr_[:], in1=cst[:, C_GNW + g * 4:C_GNW + g * 4 + 4].unsqueeze(2).broadcast_to([128, 4, 128]),
                                                                 op=ALU.mult), reads=[ptr_], writes=[yt])
                    S.dma('pool', ynT_s[g * 4:(g + 1) * 4, :, tok0 + lt * 128:tok0 + (lt + 1) * 128].rearrange("j p t -> p j t"), yt[:], reads=[yt], slot=('yts', yts.i))
                xd = xdr.next()
                S.op('dve', lambda: nc.vector.tensor_tensor(out=xd[:].rearrange("p j (e q) -> p j e q", e=8),
                                                             in0=x2[:, :, g * 512:(g + 1) * 512].rearrange("p j (e q) -> p j e q", e=8),
                                                             in1=dend[:, :, g * 8:(g + 1) * 8].unsqueeze(3).broadcast_to([128, 2, 8, 64]), op=ALU.mult),
                     reads=[x2, dend], writes=[xd])
                pst = p_yo.next()
                S.op('pe', lambda: nc.tensor.matmul(pst[:], lhsT=b2[:, 0, g * 128:(g + 1) * 128], rhs=xd[:, 0, :], start=True, stop=False), reads=[b2, xd], writes=[pst], signal=False)
                S.op('pe', lambda: nc.tensor.matmul(pst[:], lhsT=b2[:, 1, g * 128:(g + 1) * 128], rhs=xd[:, 1, :], start=False, stop=True), reads=[b2, xd], writes=[pst])
                S.op('dve', lambda: nc.vector.tensor_tensor(out=state[:, g, :].rearrange("p (e q) -> p e q", e=8), in0=state[:, g, :].rearrange("p (e q) -> p e q", e=8),
                                                             in1=sdec[:, g * 8:(g + 1) * 8].unsqueeze(2).broadcast_to([128, 8, 64]), op=ALU.mult),
                     reads=[(state, g), sdec], writes=[(state, g)])
                S.op('dve', lambda: nc.vector.tensor_tensor(out=state[:, g, :], in0=state[:, g, :], in1=pst[:], op=ALU.add), reads=[(state, g), pst], writes=[(state, g)])
                S.op('act', lambda: nc.scalar.copy(out=stbf[:, g, :], in_=state[:, g, :]), reads=[(state, g)], writes=[stbf])
        S.barrier()
    chk('S')

    def gather(src, dst):
        nc.gpsimd.collective_compute("AllGather", ALU.bypass, replica_groups=pair_groups,
                                     ins=[src.opt()], outs=[dst.opt()]).then_inc(cc_sem)
        n_cc[0] += 1

    for j in range(16):
        for tq in range(0, T, 8192):
            gather(ynT_s[j], ynT_g[j])

    with ExitStack() as pes:
        qTr = ring(pes, 2, [128, T], BF16, "mq")
        kTr = ring(pes, 2, [128, T], BF16, "mk")
        var = ring(pes, 2, [128, NT, 130], BF16, "mv")
        gsr = ring(pes, 2, [128, NT, 128], BF16, "mg")
        kmf = sb(pes, [128, 32], F32, "kmf")
        kmb = sb(pes, [128, 32], BF16, "kmb")
        gbuf = ring(pes, 2, [128, 32], F32, "gbuf")
        t8r = ring(pes, 2, [128, 8], F32, "t8")
        selr = ring(pes, 4, [128, 32], F32, "sel")
        ptr_ = ring(pes, 4, [128, 256], BF16, "mp")
        oacc = ring(pes, 4, [128, 132], F32, "oacc")
        rdr = ring(pes, 4, [128, 2], F32, "rd")
        ogr = ring(pes, 2, [128, 128], BF16, "og")
        ogT = ring(pes, 2, [128, 256], BF16, "ogT")
        p_g = ring(pes, 1, [128, 512], F32, "pg", psum=True)
        p_s = ring(pes, 3, [128, 512], F32, "pS", psum=True)
        p_o = ring(pes, 2, [128, 2, 256], F32, "pO", psum=True)
        p_t = ring(pes, 1, [128, 2, 128], BF16, "pT", psum=True)
        scale = 128.0 ** -0.5
        NBLK = T // 256
        for hd in range(8):
            qT = qTr.next(); kT = kTr.next(); va = var.next(); gs = gsr.next()
            S.dma('sp', qT[:], qT_s[hd * 128:(hd + 1) * 128, :], writes=[qT], slot=('mq', qTr.i))
            S.dma('sp', kT[:], kT_s[hd * 128:(hd + 1) * 128, :], writes=[kT], slot=('mk', kTr.i))
            S.dma('sp', va[:, :, 0:128], v_s[:, hd * 128:(hd + 1) * 128].rearrange("(j p) c -> p j c", p=128), writes=[(va, 'v')], slot=('mv', var.i))
            S.op('dve', lambda: nc.vector.memset(va[:, :, 128:130], 1.0), writes=[(va, 'o')])
            S.dma('sp', gs[:], gs_s[:, hd * 128:(hd + 1) * 128].rearrange("(j p) c -> p j c", p=128), writes=[gs], slot=('mg', gsr.i))
            S.op('dve', lambda: nc.vector.tensor_reduce(out=kmf[:, 0:NBLK], in_=kT[:].rearrange("p (n t) -> p n t", t=256), op=ALU.add, axis=AX.X),
                 reads=[kT], writes=[kmf])
            S.op('act', lambda: nc.scalar.activation(out=kmb[:, 0:NBLK], in_=kmf[:, 0:NBLK], func=AF.Copy, scale=1.0 / 256), reads=[kmf], writes=[kmb])
            gb = [gbuf.next(), gbuf.next()]
            for qt in range(2):
                S.op('dve', lambda: nc.vector.memset(gb[qt][:], NEGBIG), writes=[gb[qt]])
            for i in range(NBLK):
                sels = [None, None]
                if i >= 1:
                    pg = p_g.next()
                    for qt in range(2):
                        q0 = i * 256 + qt * 128
                        S.op('pe', lambda: nc.tensor.matmul(pg[:, qt * 32:qt * 32 + NBLK], lhsT=qT[:, q0:q0 + 128], rhs=kmb[:, 0:NBLK], start=True, stop=True),
                             reads=[qT, kmb], writes=[pg], signal=(qt == 1))
                    for qt in range(2):
                        S.op('dve', lambda: nc.vector.tensor_copy(out=gb[qt][:, 0:i], in_=pg[:, qt * 32:qt * 32 + i]), reads=[pg], writes=[gb[qt]])
                        t8 = t8r.next()
                        S.op('dve', lambda: nc.vector.max(out=t8[:], in_=gb[qt][:]), reads=[gb[qt]], writes=[t8])
                        sel = selr.next()
                        S.op('dve', lambda: nc.vector.tensor_scalar(out=sel[:], in0=gb[qt][:], scalar1=t8[:, 2:3], scalar2=None, op0=ALU.is_ge),
                             reads=[gb[qt], t8], writes=[sel])
                        sels[qt] = sel
                oa = [oacc.next(), oacc.next()]
                for j in [i] + list(range(i)):
                    pS = p_s.next()
                    pts = []
                    for kt in range(2):
                        k0 = j * 256 + kt * 128
                        S.op('pe', lambda: nc.tensor.matmul(pS[:, kt * 256:(kt + 1) * 256], lhsT=kT[:, k0:k0 + 128], rhs=qT[:, i * 256:(i + 1) * 256], start=True, stop=True),
                             reads=[kT, qT], writes=[pS], signal=(kt == 1))
                    for kt in range(2):
                        pT = ptr_.next()
                        S.op('act', lambda: nc.scalar.activation(out=pT[:], in_=pS[:, kt * 256:(kt + 1) * 256], func=AF.Exp, scale=scale), reads=[pS], writes=[pT])
                        if j == i:
                            cm = cstb[:, (B_CM0 if kt == 0 else B_CM1):(B_CM0 if kt == 0 else B_CM1) + 256]
                            S.op('dve', lambda: nc.vector.tensor_tensor(out=pT[:], in0=pT[:], in1=cm, op=ALU.mult), reads=[pT], writes=[pT])
                        pts.append(pT)
                    pO = p_o.next()
                    for qt in range(2):
                        for kt in range(2):
                            S.op('pe', lambda: nc.tensor.matmul(pO[:, qt, 0:130], lhsT=pts[kt][:, qt * 128:(qt + 1) * 128], rhs=va[:, j * 2 + kt, :],
                                                                start=(kt == 0), stop=(kt == 1)),
                                 reads=[pts[kt], (va, 'v'), (va, 'o')], writes=[pO], signal=(qt == 1 and kt == 1))
                    for qt in range(2):
                        if j == i:
                            S.op('dve', lambda: nc.vector.tensor_copy(out=oa[qt][:, 0:130], in_=pO[:, qt, 0:130]), reads=[pO], writes=[oa[qt]])
                        else:
                            S.op('dve', lambda: nc.vector.scalar_tensor_tensor(out=oa[qt][:, 0:130], in0=pO[:, qt, 0:130], scalar=sels[qt][:, j:j + 1],
                                                                                in1=oa[qt][:, 0:130], op0=ALU.mult, op1=ALU.add),
                                 reads=[pO, sels[qt], oa[qt]], writes=[oa[qt]])
                pt_ = p_t.next()
                for qt in range(2):
                    rd = rdr.next()
                    S.op('dve', lambda: nc.vector.reciprocal(out=rd[:, 0:1], in_=oa[qt][:, 128:129]), reads=[oa[qt]], writes=[rd])
                    og = ogr.next()
                    S.op('dve', lambda: nc.vector.scalar_tensor_tensor(out=og[:], in0=oa[qt][:, 0:128], scalar=rd[:, 0:1], in1=gs[:, i * 2 + qt, :],
                                                                        op0=ALU.mult, op1=ALU.mult), reads=[oa[qt], rd, gs], writes=[og])
                    S.op('pe', lambda: nc.tensor.transpose(out=pt_[:, qt, :], in_=og[:], identity=identb), reads=[og],
                         writes=[pt_], signal=(qt == 1))
                ot = ogT.next()
                S.op('act', lambda: nc.scalar.copy(out=ot[:], in_=pt_[:].rearrange("p a b -> p (a b)")), reads=[pt_], writes=[ot])
                S.dma('pool', ogT_s[hd][:, i * 256:(i + 1) * 256], ot[:], reads=[ot], writes=[('ogT_s', hd)] if i == NBLK - 1 else [], slot=('ogT', ogT.i))
            S._wait('pool', [(k, v) for k, v in S.val.items() if not isinstance(k, str) and k[1] in (('ogT', 0), ('ogT', 1))])
            gather(ogT_s[hd], ogT_g[hd])
        S.barrier()
    nc.sync.wait_ge(cc_sem, n_cc[0])
    nc.gpsimd.wait_ge(cc_sem, n_cc[0])
    chk('M')

    with ExitStack() as pes:
        TB = 256
        hTf = sb(pes, [128, 16, TB], BF16, "fh")
        ynf = sb(pes, [128, 32, TB], BF16, "fy")
        ogf = sb(pes, [128, 16, TB], BF16, "fo")
        mixT = sb(pes, [128, 16, TB], BF16, "fm")
        stgr = ring(pes, 2, [128, 16, TB], BF16, "fst")
        wr1 = ring(pes, 6, [128, 16, 128], BF16, "fw1")
        wor = ring(pes, 2, [128, 16, 512], BF16, "fwo")
        g1r = ring(pes, 2, [128, TB], F32, "fg1")
        g2r = ring(pes, 2, [128, TB], F32, "fg2")
        xr = ring(pes, 2, [128, 2048], F32, "fx")
        xor_ = ring(pes, 2, [128, 2048], F32, "fxo")
        junk = sb(pes, [128, 2048], BF16, "fj")
        fnw = sb(pes, [128, 2048], F32, "fnw")
        smr = ring(pes, 4, [128, 4], F32, "fsm")
        psr = ring(pes, 6, [128, 512], F32, "fps", psum=True)
        pdr = ring(pes, 2, [128, 512], F32, "fpd", psum=True)
        S.dma('sp', fnw[:], fnw_d[:, :], writes=[fnw], slot='fnw')
        s_lo = cst[:, C_BL:C_BL + 1]
        s_hi = cst[:, C_BL + 1:C_BL + 2]

        def blend_load(dst, n, lo_src, hi_src, key):
            S.dma('sp', dst, lo_src, writes=[key], slot=key)
            st = stgr.next()
            S.dma('sp', st[:, :n, :], hi_src, writes=[st], slot=('fst', stgr.i))
            S.op('dve', lambda: nc.vector.tensor_scalar(out=dst, in0=dst, scalar1=s_lo, scalar2=None, op0=ALU.mult), reads=[key], writes=[key])
            S.op('dve', lambda: nc.vector.scalar_tensor_tensor(out=dst, in0=st[:, :n, :], scalar=s_hi, in1=dst, op0=ALU.mult, op1=ALU.add),
                 reads=[st, key], writes=[key])

        for bi in range(TH // TB):
            lo = bi * TB
            hi = TH + bi * TB
            blend_load(hTf[:], 16, hT_s[:, :, lo:lo + TB].rearrange("k p t -> p k t"), hT_s[:, :, hi:hi + TB].rearrange("k p t -> p k t"), ('fh', 0))
            for r in range(2):
                blend_load(ynf[:, r * 16:(r + 1) * 16, :], 16, ynT_g[:, r * 128:(r + 1) * 128, lo:lo + TB].rearrange("j p t -> p j t"),
                           ynT_g[:, r * 128:(r + 1) * 128, hi:hi + TB].rearrange("j p t -> p j t"), ('fy', r))
            for r in range(2):
                blend_load(ogf[:, r * 8:(r + 1) * 8, :], 8, ogT_g[:, r * 128:(r + 1) * 128, lo:lo + TB].rearrange("j p t -> p j t"),
                           ogT_g[:, r * 128:(r + 1) * 128, hi:hi + TB].rearrange("j p t -> p j t"), ('fo', r))
            for mc in range(16):
                srcs = [wgm_s[mc], wgm_s[16 + mc], wssm_s[mc][:, 0:2048], wssm_s[mc][:, 2048:4096], wattn_s[mc]]
                wch = []
                for s_ in srcs:
                    wt_ = wr1.next()
                    S.dma('sp', wt_[:], s_.rearrange("p (k c) -> p k c", k=16), writes=[wt_], slot=('fw1', wr1.i))
                    wch.append(wt_)
                pg1 = psr.next(); pg2 = psr.next(); pys = psr.next(); pya = psr.next()
                for k in range(16):
                    S.op('pe', lambda: nc.tensor.matmul(pg1[:, :TB], lhsT=wch[0][:, k, :], rhs=hTf[:, k, :], start=(k == 0), stop=(k == 15)),
                         reads=[wch[0], ('fh', 0)], writes=[pg1], signal=(k == 15))
                for k in range(16):
                    S.op('pe', lambda: nc.tensor.matmul(pg2[:, :TB], lhsT=wch[1][:, k, :], rhs=hTf[:, k, :], start=(k == 0), stop=(k == 15)),
                         reads=[wch[1], ('fh', 0)], writes=[pg2], signal=(k == 15))
                for k in range(32):
                    S.op('pe', lambda: nc.tensor.matmul(pys[:, :TB], lhsT=wch[2 + k // 16][:, k % 16, :], rhs=ynf[:, k, :], start=(k == 0), stop=(k == 31)),
                         reads=[wch[2 + k // 16], ('fy', k // 16)], writes=[pys], signal=(k == 31))
                for k in range(16):
                    S.op('pe', lambda: nc.tensor.matmul(pya[:, :TB], lhsT=wch[4][:, k, :], rhs=ogf[:, k, :], start=(k == 0), stop=(k == 15)),
                         reads=[wch[4], ('fo', k // 8)], writes=[pya], signal=(k == 15))
                g1 = g1r.next(); g2 = g2r.next()
                S.op('act', lambda: nc.scalar.activation(out=g1[:], in_=pg1[:, :TB], func=AF.Sigmoid, bias=cst[:, C_GB + mc:C_GB + mc + 1], scale=1.0),
                     reads=[pg1], writes=[g1])
                S.op('act', lambda: nc.scalar.activation(out=g2[:], in_=pg2[:, :TB], func=AF.Sigmoid, bias=cst[:, C_GB + 16 + mc:C_GB + 17 + mc], scale=1.0),
                     reads=[pg2], writes=[g2])
                S.op('dve', lambda: nc.vector.tensor_tensor(out=g1[:], in0=g1[:], in1=pys[:, :TB], op=ALU.mult), reads=[g1, pys], writes=[g1])
                S.op('dve', lambda: nc.vector.tensor_tensor(out=g2[:], in0=g2[:], in1=pya[:, :TB], op=ALU.mult), reads=[g2, pya], writes=[g2])
                S.op('dve', lambda: nc.vector.tensor_tensor(out=mixT[:, mc, :], in0=g1[:], in1=g2[:], op=ALU.add), reads=[g1, g2], writes=[mixT])
            xts = []
            xos = []
            for tt in range(TB // 128):
                xt = xr.next()
                S.dma('sp', xt[:], xf_d[lo + tt * 128:lo + (tt + 1) * 128, :], writes=[xt], slot=('fx', xr.i))
                xts.append(xt)
                xos.append(xor_.next())
            for dblk in range(4):
                wo = wor.next()
                S.dma('sp', wo[:], wout_s[dblk].rearrange("p (k c) -> p k c", k=16), writes=[wo], slot=('fwo', wor.i))
                for tt in range(TB // 128):
                    pd = pdr.next()
                    for k in range(16):
                        S.op('pe', lambda: nc.tensor.matmul(pd[:], lhsT=mixT[:, k, tt * 128:(tt + 1) * 128], rhs=wo[:, k, :], start=(k == 0), stop=(k == 15)),
                             reads=[wo, mixT], writes=[pd], signal=(k == 15))
                    S.op('dve', lambda: nc.vector.tensor_tensor(out=xos[tt][:, dblk * 512:(dblk + 1) * 512], in0=pd[:], in1=xts[tt][:, dblk * 512:(dblk + 1) * 512], op=ALU.add),
                         reads=[pd, xts[tt]], writes=[(xos[tt], dblk)])
            for tt in range(TB // 128):
                sm = smr.next()
                xo = xos[tt]
                S.op('act', lambda: nc.scalar.activation(out=junk[:], in_=xo[:], func=AF.Square, accum_out=sm[:, 0:1]),
                     reads=[(xo, q) for q in range(4)], writes=[junk, sm])
                S.op('dve', lambda: nc.vector.tensor_scalar(out=sm[:, 1:2], in0=sm[:, 0:1], scalar1=1.0 / D, scalar2=EPS, op0=ALU.mult, op1=ALU.add),
                     reads=[sm], writes=[sm])
                S.op('act', lambda: nc.scalar.activation(out=sm[:, 2:3], in_=sm[:, 1:2], func=AF.Sqrt), reads=[sm], writes=[sm])
                S.op('dve', lambda: nc.vector.reciprocal(out=sm[:, 3:4], in_=sm[:, 2:3]), reads=[sm], writes=[sm])
                S.op('dve', lambda: nc.vector.scalar_tensor_tensor(out=xts[tt][:], in0=xo[:], scalar=sm[:, 3:4], in1=fnw[:], op0=ALU.mult, op1=ALU.mult),
                     reads=[(xo, q) for q in range(4)] + [sm, fnw], writes=[xts[tt]])
                S.dma('pool', out_d[lo + tt * 128:lo + (tt + 1) * 128, :], xts[tt][:], reads=[xts[tt]], slot=('fxs', xr.t.index(xts[tt])))
        S.barrier()
    es.close()
    return nc


_OFFS = np.cumsum([0, 4096, 6144, 64, 2048, 2048, 2048, 2048, 4096])


def _chunk(w, kc, ncol_blk):
    nblk = w.shape[1] // ncol_blk
    return np.ascontiguousarray(w.reshape(kc, 128, nblk, ncol_blk).transpose(2, 1, 0, 3)).reshape(nblk, 128, kc * ncol_blk)


def _shared_consts():
    t = np.arange(128)
    cst = np.zeros((128, 1024), np.float32)
    U = (t[:, None] <= t[None, :]).astype(np.float32)
    cst[:, 256:384] = U
    cst[:, 384:512] = 1.0
    cst[:, 512:640] = (t[:, None] > t[None, :]).astype(np.float32)
    cst[:, 640:768] = 1.0
    cst[:, 768:896] = np.eye(128, dtype=np.float32)
    cstb = np.zeros((128, 1024), np.float32)
    cstb[:, 0:128] = np.eye(128)
    col = np.arange(256)
    cstb[:, 128:384] = (t[:, None] <= col[None, :])
    cstb[:, 384:640] = ((128 + t[:, None]) <= col[None, :])
    return cst, cstb.astype(ml_dtypes.bfloat16)


def _prep_core(inp, b, hh, T, cst0, cstb):
    f = np.float32
    x = np.ascontiguousarray(inp["x"][b, :T]).astype(f, copy=False)
    w_in = inp["w_in"][0]
    z0, xbc0, dt0, q0, k0, v0, g0, gm0 = [int(v) for v in _OFFS[:8]]

    def cols(a, n):
        return w_in[:, a:a + n]

    Wz = cols(z0 + hh * 2048, 2048)
    Wx = cols(xbc0 + hh * 2048, 2048)
    WB = cols(xbc0 + 4096 + hh * 512, 512)
    WC = cols(xbc0 + 5120 + hh * 512, 512)
    Wdt = cols(dt0 + hh * 32, 32)
    Wq = cols(q0 + hh * 1024, 1024)
    Wk = cols(k0 + hh * 1024, 1024)
    Wv = cols(v0 + hh * 1024, 1024)
    Wg = cols(g0 + hh * 1024, 1024)
    Wgm = cols(gm0, 4096)
    wfm = _chunk(np.concatenate([Wx, WB, WC, Wq, Wk], 1), 16, 128)
    wtm = _chunk(np.concatenate([Wz, Wv, Wg], 1), 16, 512)
    wdt = _chunk(Wdt, 16, 32)[0]
    wgm = _chunk(Wgm, 16, 128)
    wssm = _chunk(inp["w_ssm_proj"][0], 32, 128)
    wattn = _chunk(inp["w_attn_proj"][0], 16, 128)
    wout = _chunk(inp["w_out"][0], 16, 512)
    cst = cst0.copy()
    cst[:, 0:16] = inp["norm_w"][0].reshape(16, 128).T
    conv_w = inp["conv_w"][0]
    conv_b = inp["conv_b"][0]
    chans = np.concatenate([hh * 2048 + np.arange(2048), 4096 + hh * 512 + np.arange(512), 5120 + hh * 512 + np.arange(512)])
    cw = conv_w[:, chans]
    cst[:, 16:112] = cw.reshape(4, 24, 128).transpose(2, 1, 0).reshape(128, 96)
    cst[:, 112:136] = conv_b[chans].reshape(24, 128).T
    cst[:, 136:168] = np.broadcast_to(inp["dt_bias"][0][hh * 32:(hh + 1) * 32], (128, 32))
    cst[:, 168:200] = np.broadcast_to(inp["A_log"][0][hh * 32:(hh + 1) * 32], (128, 32))
    cst[:, 200:216] = inp["ssm_norm_w"][0][hh * 2048:(hh + 1) * 2048].reshape(16, 128).T
    cst[:, 216:248] = inp["gate_bias"][0].reshape(32, 128).T
    cst[:, 248] = 1.0 - hh
    cst[:, 249] = float(hh)
    dsk = np.ascontiguousarray(np.broadcast_to(np.repeat(inp["D_skip"][0][hh * 32:(hh + 1) * 32], 64), (128, 2048))).astype(f)
    fnw = np.ascontiguousarray(np.broadcast_to(inp["final_norm_w"], (128, 2048))).astype(f)
    TH = T // 2
    return {"x": x, "wfm": wfm, "wtm": wtm, "wdt": np.ascontiguousarray(wdt), "wgm": wgm, "wssm": wssm, "wattn": wattn,
            "wout": wout, "cst": cst, "dsk": dsk, "fnw": fnw, "xf": np.ascontiguousarray(x[hh * TH:(hh + 1) * TH]), "cstb": cstb}


def run(inputs, T, nb, stop=None, debug=False):
    ncores = 2 * nb
    groups = [[2 * i, 2 * i + 1] for i in range(nb)]
    nc = build(T, groups, stop, debug)
    cst0, cstb = _shared_consts()
    in_maps = [_prep_core(inputs, c // 2, c % 2, T, cst0, cstb) for c in range(ncores)]
    res = run_bass_kernel_spmd(nc, in_maps, core_ids=list(range(ncores)))
    if debug:
        return res.results, in_maps
    out = np.empty((nb, T, D), np.float32)
    TH = T // 2
    for c in range(ncores):
        out[c // 2, (c % 2) * TH:(c % 2 + 1) * TH] = res.results[c]["out"]
    return out


def kernel(**inputs):
    inputs = {k: np.asarray(v) for k, v in inputs.items()}
    return run(inputs, 8192, 4)
```

```python
import numpy as np
from contextlib import ExitStack
import concourse.bass as bass
import concourse.mybir as mybir
from concourse.bass_utils import run_bass_kernel_spmd
import ml_dtypes

F32 = mybir.dt.float32
BF16 = mybir.dt.bfloat16
AF = mybir.ActivationFunctionType
ALU = mybir.AluOpType
AX = mybir.AxisListType

D = 2048
EPS = 1e-6
NEGBIG = -1.0e30


class Sched:
    def __init__(self, nc, es):
        self.nc = nc
        self.es = es
        self.E = {'pe': nc.tensor, 'act': nc.scalar, 'dve': nc.vector, 'pool': nc.gpsimd, 'sp': nc.sync}
        self.sems = {}
        self.val = {}
        for k in ('pe', 'act', 'dve', 'pool'):
            self.sems[k] = es.enter_context(nc.semaphore('c_' + k))
            self.val[k] = 0
        self.seen = {e: {} for e in self.E}
        self.w = {}
        self.r = {}
        self.nsem = 4

    def _wait(self, e, toks):
        need = {}
        for (k, v) in toks:
            if not isinstance(k, str):
                v = self.val[k]
            if need.get(k, 0) < v:
                need[k] = v
        for k, v in need.items():
            if k == e and e == 'pe':
                continue
            if self.seen[e].get(k, 0) >= v:
                continue
            self.E[e].wait_ge(self.sems[k], v)
            self.seen[e][k] = v

    def _key(self, r):
        if isinstance(r, tuple):
            return tuple(self._key(x) for x in r)
        if isinstance(r, (str, int)):
            return r
        return ('id', id(r))

    def _deps(self, e, reads, writes):
        toks = []
        for r in reads:
            toks += self.w.get(r, [])
        for w_ in writes:
            toks += self.w.get(w_, [])
            toks += self.r.get(w_, [])
        return toks

    def op(self, e, fn, reads=(), writes=(), signal=True):
        reads = [self._key(r) for r in reads]
        writes = [self._key(r) for r in writes]
        self._wait(e, self._deps(e, reads, writes))
        inst = fn()
        if signal:
            self.val[e] += 1
            inst.then_inc(self.sems[e], 1)
            tok = (e, self.val[e])
            for w_ in writes:
                self.w[w_] = [tok]
                self.r[w_] = []
        else:
            tok = (e, self.val[e] + 1)
            for w_ in writes:
                self.w[w_] = [tok]
                self.r[w_] = []
        for r in reads:
            self.r.setdefault(r, []).append(tok)
        return tok

    def dma(self, q, out, in_, reads=(), writes=(), slot=None):
        reads = [self._key(r) for r in reads]
        writes = [self._key(r) for r in writes]
        self._wait(q, self._deps(q, reads, writes))
        key = ('d', slot)
        if key not in self.sems:
            self.sems[key] = self.es.enter_context(self.nc.semaphore('d%d' % self.nsem))
            self.nsem += 1
            self.val[key] = 0
        self.val[key] += 16
        self.E[q].dma_start(out=out, in_=in_).then_inc(self.sems[key], 16)
        tok = (key, self.val[key])
        for w_ in writes:
            self.w[w_] = [tok]
            self.r[w_] = []
        for r in reads:
            self.r.setdefault(r, []).append(tok)
        return tok

    def barrier(self):
        for e in ('pe', 'act', 'dve', 'pool', 'sp'):
            for k, v in self.val.items():
                if v > 0 and k != e and self.seen[e].get(k, 0) < v:
                    self.E[e].wait_ge(self.sems[k], v)
                    self.seen[e][k] = v
        self.w.clear()
        self.r.clear()


class Ring:
    def __init__(self, tiles):
        self.t = tiles
        self.i = -1

    def next(self):
        self.i = (self.i + 1) % len(self.t)
        return self.t[self.i]


class _Stop(Exception):
    pass


def build(T, pair_groups, stop=None, debug=False):
    holder = {}
    try:
        _build(T, pair_groups, stop, holder, debug)
    except _Stop:
        holder['es'].close()
    return holder['nc']


def _build(T, pair_groups, stop, holder, debug):
    assert T % 512 == 0
    NT = T // 128
    NCH = T // 256
    TS = min(T, 2048)
    NSB = T // TS
    TH = T // 2
    nc = bass.Bass("TRN2", target_bir_lowering=False)

    def din(name, shape, dt=F32):
        return nc.dram_tensor(name, shape, dt, kind="ExternalInput")

    x_d = din("x", [T, D])
    wfm_d = din("wfm", [40, 128, 2048])
    wtm_d = din("wtm", [8, 128, 8192])
    wdt_d = din("wdt", [128, 512])
    wgm_d = din("wgm", [32, 128, 2048])
    wssm_d = din("wssm", [16, 128, 4096])
    wattn_d = din("wattn", [16, 128, 2048])
    wout_d = din("wout", [4, 128, 8192])
    cst_d = din("cst", [128, 1024])
    dsk_d = din("dsk", [128, 2048])
    fnw_d = din("fnw", [128, 2048])
    xf_d = din("xf", [T // 2, D])
    cstb_d = din("cstb", [128, 1024], BF16)
    out_d = nc.dram_tensor("out", [TH, D], F32, kind="ExternalOutput")

    def scr(name, shape, dt=BF16):
        if debug and not name.startswith(("ynT", "ogT")):
            return nc.dram_tensor(name, shape, dt, kind="ExternalOutput")
        return nc.dram_tensor(name, shape, dt)

    hT_s = scr("hT_s", [16, 128, T])
    wfm_s = scr("wfm_s", [40, 128, 2048])
    wtm_s = scr("wtm_s", [8, 128, 8192])
    wdt_s = scr("wdt_s", [128, 512])
    wgm_s = scr("wgm_s", [32, 128, 2048])
    wssm_s = scr("wssm_s", [16, 128, 4096])
    wattn_s = scr("wattn_s", [16, 128, 2048])
    wout_s = scr("wout_s", [4, 128, 8192])
    xtm_s = scr("xtm_s", [T, 2048])
    btm_s = scr("btm_s", [T, 512])
    bT_s = scr("bT_s", [512, T])
    cT_s = scr("cT_s", [512, T])
    qT_s = scr("qT_s", [1024, T])
    kT_s = scr("kT_s", [1024, T])
    zs_s = scr("zs_s", [T, 2048])
    v_s = scr("v_s", [T, 1024])
    gs_s = scr("gs_s", [T, 1024])
    dt_s = scr("dt_s", [T, 32], F32)
    ynT_s = scr("ynT_s", [16, 128, T])
    ynT_g = scr("ynT_g", [16, 256, T])
    ogT_s = scr("ogT_s", [8, 128, T])
    ogT_g = scr("ogT_g", [8, 256, T])

    es = ExitStack()
    holder['nc'] = nc
    holder['es'] = es
    S = Sched(nc, es)

    def chk(name):
        if stop == name:
            S.barrier()
            raise _Stop()
    cc_sem = es.enter_context(nc.semaphore("cc"))
    n_cc = [0]

    uid = [0]

    def sb(es_, shape, dt, name=None):
        uid[0] += 1
        return es_.enter_context(nc.sbuf_tensor("%s%d" % (name or "t", uid[0]), shape, dt))

    def ps(es_, shape, dt, name=None):
        uid[0] += 1
        return es_.enter_context(nc.psum_tensor("%s%d" % (name or "p", uid[0]), shape, dt))

    def ring(es_, n, shape, dt, name=None, psum=False):
        return Ring([(ps if psum else sb)(es_, shape, dt, name) for _ in range(n)])

    cst = sb(es, [128, 1024], F32, "cst")
    cstb = sb(es, [128, 1024], BF16, "cstb")
    S.dma('sp', cst[:], cst_d[:, :], writes=[cst], slot='cst')
    S.dma('sp', cstb[:], cstb_d[:, :], writes=[cstb], slot='cstb')
    C_NORMW = 0
    C_CW = 16
    C_CB = 112
    C_DTB = 136
    C_ALOG = 168
    C_GNW = 200
    C_GB = 216
    C_BL = 248
    C_U = 256
    C_L0 = 512
    C_ONES = 640
    C_IDF = 768
    B_ID = 0
    B_CM0 = 128
    B_CM1 = 384
    identb = cstb[:, B_ID:B_ID + 128]
    identf = cst[:, C_IDF:C_IDF + 128]
    S.barrier()
    S.op('act', lambda: nc.scalar.activation(out=cst[:, C_ALOG:C_ALOG + 32], in_=cst[:, C_ALOG:C_ALOG + 32], func=AF.Exp),
         reads=[cst], writes=[cst])
    S.op('dve', lambda: nc.vector.tensor_scalar(out=cst[:, C_ALOG:C_ALOG + 32], in0=cst[:, C_ALOG:C_ALOG + 32],
                                                 scalar1=-1.0, scalar2=None, op0=ALU.mult),
         reads=[cst], writes=[cst])
    S.barrier()
    A_bc = cst[:, C_ALOG:C_ALOG + 32]

    with ExitStack() as pes:
        win = ring(pes, 3, [128, 2048], F32, "win")
        wo = ring(pes, 3, [128, 2048], BF16, "wo")
        normw_b = cst[:, C_NORMW:C_NORMW + 16]
        cnt = [0]

        def conv_tile(src_ap, dst_ap, kcols, fold, k0=0, nk=16):
            a = win.next()
            o = wo.next()
            n = nk * kcols
            S.dma('sp', a[:, :n], src_ap, writes=[a], slot=('win', win.i))
            if fold:
                S.op('dve', lambda: nc.vector.tensor_tensor(
                    out=o[:, :n].rearrange("p (k c) -> p k c", k=nk),
                    in0=a[:, :n].rearrange("p (k c) -> p k c", k=nk),
                    in1=normw_b[:, k0:k0 + nk].unsqueeze(2).broadcast_to([128, nk, kcols]), op=ALU.mult),
                    reads=[a], writes=[o])
            else:
                cnt[0] += 1
                if cnt[0] % 2:
                    S.op('act', lambda: nc.scalar.copy(out=o[:, :n], in_=a[:, :n]), reads=[a], writes=[o])
                else:
                    S.op('dve', lambda: nc.vector.tensor_copy(out=o[:, :n], in_=a[:, :n]), reads=[a], writes=[o])
            S.dma('pool', dst_ap, o[:, :n], reads=[o], slot=('wo', wo.i))

        for c in range(40):
            conv_tile(wfm_d[c], wfm_s[c], 128, True)
        for c in range(8):
            for kq in range(4):
                conv_tile(wtm_d[c][:, kq * 2048:(kq + 1) * 2048], wtm_s[c][:, kq * 2048:(kq + 1) * 2048], 512, True, k0=kq * 4, nk=4)
        conv_tile(wdt_d[:, :], wdt_s[:, :], 32, True)
        for c in range(32):
            conv_tile(wgm_d[c], wgm_s[c], 128, True)
        for c in range(16):
            for hq in range(2):
                conv_tile(wssm_d[c][:, hq * 2048:(hq + 1) * 2048], wssm_s[c][:, hq * 2048:(hq + 1) * 2048], 128, False)
        for c in range(16):
            conv_tile(wattn_d[c], wattn_s[c], 128, False)
        for c in range(4):
            for kq in range(4):
                conv_tile(wout_d[c][:, kq * 2048:(kq + 1) * 2048], wout_s[c][:, kq * 2048:(kq + 1) * 2048], 512, False, nk=4)
        S.barrier()
    chk('W')

    with ExitStack() as pes:
        xring = ring(pes, 2, [128, 2048], F32, "xt")
        junk = sb(pes, [128, 2048], BF16, "junk")
        hbr = ring(pes, 2, [128, 2048], BF16, "hb")
        stg = ring(pes, 2, [128, 16, 512], BF16, "hst")
        smr = ring(pes, 4, [128, 4], F32, "sm")
        ptr = ring(pes, 2, [128, 16, 128], BF16, "ptr", psum=True)
        for i4 in range(T // 512):
            stage = stg.next()
            for ii in range(4):
                i = i4 * 4 + ii
                xt = xring.next()
                S.dma('sp', xt[:], x_d[i * 128:(i + 1) * 128, :], writes=[xt], slot=('xt', xring.i))
                sm = smr.next()
                S.op('act', lambda: nc.scalar.activation(out=junk[:], in_=xt[:], func=AF.Square, accum_out=sm[:, 0:1]),
                     reads=[xt], writes=[junk, sm])
                S.op('dve', lambda: nc.vector.tensor_scalar(out=sm[:, 1:2], in0=sm[:, 0:1], scalar1=1.0 / D, scalar2=EPS,
                                                             op0=ALU.mult, op1=ALU.add), reads=[sm], writes=[sm])
                S.op('act', lambda: nc.scalar.activation(out=sm[:, 2:3], in_=sm[:, 1:2], func=AF.Sqrt), reads=[sm], writes=[sm])
                S.op('dve', lambda: nc.vector.reciprocal(out=sm[:, 3:4], in_=sm[:, 2:3]), reads=[sm], writes=[sm])
                hb = hbr.next()
                S.op('dve', lambda: nc.vector.tensor_scalar(out=hb[:], in0=xt[:], scalar1=sm[:, 3:4], scalar2=None, op0=ALU.mult),
                     reads=[xt, sm], writes=[hb])
                pt = ptr.next()
                for k in range(16):
                    S.op('pe', lambda: nc.tensor.transpose(out=pt[:, k, :], in_=hb[:, k * 128:(k + 1) * 128], identity=identb),
                         reads=[hb], writes=[pt], signal=(k == 15))
                S.op('act', lambda: nc.scalar.copy(out=stage[:, :, ii * 128:(ii + 1) * 128], in_=pt[:]), reads=[pt], writes=[stage])
            S.dma('pool', hT_s[:, :, i4 * 512:(i4 + 1) * 512].rearrange("k p t -> p k t"), stage[:], reads=[stage], slot=('hst', stg.i))
        S.barrier()
    chk('A')

    with ExitStack() as pes:
        hT = sb(pes, [128, 16, TS], BF16, "hT")
        wfr = ring(pes, 3, [128, 16, 128], BF16, "wf")
        wtr = ring(pes, 2, [128, 16, 512], BF16, "wt")
        wdt = sb(pes, [128, 16, 32], BF16, "wdt")
        halo = sb(pes, [128, 24, 4], F32, "halo")
        xsr = ring(pes, 2, [128, 516], F32, "xs")
        accr = ring(pes, 2, [128, 512], F32, "acc")
        fmo = ring(pes, 3, [128, 512], BF16, "fmo")
        tmo = ring(pes, 3, [128, 512], BF16, "tmo")
        tms = ring(pes, 2, [128, 4, 128], BF16, "tms")
        dtr = ring(pes, 2, [128, 32], F32, "dtr")
        pacc = ring(pes, 4, [128, 512], F32, "pacc", psum=True)
        ptp = ring(pes, 2, [128, 4, 128], BF16, "ptp", psum=True)
        S.op('dve', lambda: nc.vector.memset(halo[:], 0.0), writes=[halo])
        S.dma('sp', wdt[:], wdt_s[:, :].rearrange("p (k c) -> p k c", k=16), writes=[wdt], slot='wdt')
        for sbi in range(NSB):
            t0 = sbi * TS
            for k in range(16):
                S.dma('sp', hT[:, k, :], hT_s[k][:, t0:t0 + TS], writes=[hT], slot='hT')
            pend = []
            for c in range(40):
                wf = wfr.next()
                S.dma('sp', wf[:], wfm_s[c].rearrange("p (k c) -> p k c", k=16), writes=[wf], slot=('wf', wfr.i))
                for tt in range(TS // 512):
                    tok0 = t0 + tt * 512
                    pa = pacc.next()
                    for k in range(16):
                        S.op('pe', lambda: nc.tensor.matmul(pa[:], lhsT=wf[:, k, :], rhs=hT[:, k, tt * 512:(tt + 1) * 512],
                                                            start=(k == 0), stop=(k == 15)),
                             reads=[wf, hT], writes=[pa], signal=(k == 15))
                    while pend:
                        pend.pop(0)()
                    if c < 24:
                        xs = xsr.next()
                        S.op('act', lambda: nc.scalar.copy(out=xs[:, 3:515], in_=pa[:]), reads=[pa], writes=[(xs, 'b')])
                        S.op('dve', lambda: nc.vector.tensor_copy(out=xs[:, 0:3], in_=halo[:, c, 0:3]), reads=[halo], writes=[(xs, 'h')])
                        S.op('dve', lambda: nc.vector.tensor_copy(out=halo[:, c, 0:3], in_=xs[:, 512:515]), reads=[(xs, 'b')], writes=[halo])
                        acc = accr.next()
                        cw = C_CW + c * 4
                        S.op('dve', lambda: nc.vector.tensor_scalar(out=acc[:], in0=xs[:, 0:512], scalar1=cst[:, cw:cw + 1],
                                                                     scalar2=cst[:, C_CB + c:C_CB + c + 1], op0=ALU.mult, op1=ALU.add),
                             reads=[(xs, 'b'), (xs, 'h')], writes=[acc])
                        for k in range(1, 4):
                            S.op('dve', lambda: nc.vector.scalar_tensor_tensor(out=acc[:], in0=xs[:, k:k + 512], scalar=cst[:, cw + k:cw + k + 1],
                                                                                in1=acc[:], op0=ALU.mult, op1=ALU.add),
                                 reads=[(xs, 'b'), (xs, 'h'), acc], writes=[acc])
                        fo = fmo.next()
                        S.op('act', lambda: nc.scalar.activation(out=fo[:], in_=acc[:], func=AF.Silu), reads=[acc], writes=[fo])
                        if c >= 16:
                            dst = (bT_s if c < 20 else cT_s)[((c - 16) % 4) * 128:((c - 16) % 4 + 1) * 128, tok0:tok0 + 512]
                            S.dma('pool', dst, fo[:], reads=[fo], slot=('fmo', fmo.i))
                        if c < 20:
                            def _tr(fo=fo, c=c, tok0=tok0):
                                pt = ptp.next()
                                for j in range(4):
                                    S.op('pe', lambda: nc.tensor.transpose(out=pt[:, j, :], in_=fo[:, j * 128:(j + 1) * 128], identity=identb),
                                         reads=[fo], writes=[pt], signal=(j == 3))
                                ts_ = tms.next()
                                S.op('act', lambda: nc.scalar.copy(out=ts_[:], in_=pt[:]), reads=[pt], writes=[ts_])
                                if c < 16:
                                    dst = xtm_s[tok0:tok0 + 512, c * 128:(c + 1) * 128]
                                else:
                                    dst = btm_s[tok0:tok0 + 512, (c - 16) * 128:(c - 15) * 128]
                                S.dma('pool', dst.rearrange("(j p) c -> p j c", p=128), ts_[:], reads=[ts_], slot=('tms', tms.i))
                            pend.append(_tr)
                    else:
                        fo = fmo.next()
                        S.op('act', lambda: nc.scalar.copy(out=fo[:], in_=pa[:]), reads=[pa], writes=[fo])
                        cc_ = c - 24
                        dst = (qT_s if cc_ < 8 else kT_s)[(cc_ % 8) * 128:(cc_ % 8 + 1) * 128, tok0:tok0 + 512]
                        S.dma('pool', dst, fo[:], reads=[fo], slot=('fmo', fmo.i))
            while pend:
                pend.pop(0)()
            for blk in range(8):
                wt = wtr.next()
                S.dma('sp', wt[:], wtm_s[blk].rearrange("p (k c) -> p k c", k=16), writes=[wt], slot=('wt', wtr.i))
                for tt in range(TS // 128):
                    tok0 = t0 + tt * 128
                    pa = pacc.next()
                    for k in range(16):
                        S.op('pe', lambda: nc.tensor.matmul(pa[:], lhsT=hT[:, k, tt * 128:(tt + 1) * 128], rhs=wt[:, k, :],
                                                            start=(k == 0), stop=(k == 15)),
                             reads=[wt, hT], writes=[pa], signal=(k == 15))
                    to = tmo.next()
                    if blk < 4:
                        S.op('act', lambda: nc.scalar.activation(out=to[:], in_=pa[:], func=AF.Silu), reads=[pa], writes=[to])
                        dst = zs_s[tok0:tok0 + 128, blk * 512:(blk + 1) * 512]
                    elif blk < 6:
                        S.op('act', lambda: nc.scalar.copy(out=to[:], in_=pa[:]), reads=[pa], writes=[to])
                        dst = v_s[tok0:tok0 + 128, (blk - 4) * 512:(blk - 3) * 512]
                    else:
                        S.op('act', lambda: nc.scalar.activation(out=to[:], in_=pa[:], func=AF.Silu), reads=[pa], writes=[to])
                        dst = gs_s[tok0:tok0 + 128, (blk - 6) * 512:(blk - 5) * 512]
                    S.dma('pool', dst, to[:], reads=[to], slot=('tmo', tmo.i))
            for tt in range(TS // 128):
                tok0 = t0 + tt * 128
                pa = pacc.next()
                for k in range(16):
                    S.op('pe', lambda: nc.tensor.matmul(pa[:, 0:32], lhsT=hT[:, k, tt * 128:(tt + 1) * 128], rhs=wdt[:, k, :],
                                                        start=(k == 0), stop=(k == 15)),
                         reads=[wdt, hT], writes=[pa], signal=(k == 15))
                d_ = dtr.next()
                S.op('dve', lambda: nc.vector.tensor_tensor(out=d_[:], in0=pa[:, 0:32], in1=cst[:, C_DTB:C_DTB + 32], op=ALU.add),
                     reads=[pa], writes=[d_])
                S.op('act', lambda: nc.scalar.activation(out=d_[:], in_=d_[:], func=AF.Exp), reads=[d_], writes=[d_])
                S.op('act', lambda: nc.scalar.activation(out=d_[:], in_=d_[:], func=AF.Ln, bias=1.0, scale=1.0), reads=[d_], writes=[d_])
                S.dma('pool', dt_s[tok0:tok0 + 128, :], d_[:], reads=[d_], slot=('dtr', dtr.i))
            S.barrier()
    chk('P')

    with ExitStack() as pes:
        xr = ring(pes, 2, [128, 2, 2048], BF16, "sx")
        br = ring(pes, 2, [128, 2, 512], BF16, "sb")
        bTr = ring(pes, 2, [128, 4, 256], BF16, "sbT")
        cTr = ring(pes, 2, [128, 4, 256], BF16, "scT")
        dtr2 = ring(pes, 2, [128, 2, 32], F32, "sdt")
        zr = ring(pes, 2, [128, 2, 2048], BF16, "sz")
        state = sb(pes, [128, 4, 512], F32, "state")
        stbf = sb(pes, [128, 4, 512], BF16, "stbf")
        dtA = sb(pes, [128, 2, 32], F32, "dtA")
        cumT = sb(pes, [128, 2, 32], F32, "cumT")
        cend = sb(pes, [128, 32], F32, "cend")
        ecum = sb(pes, [128, 2, 32], F32, "ecum")
        dend = sb(pes, [128, 2, 32], F32, "dend")
        sdec = sb(pes, [128, 32], F32, "sdec")
        cbm = ring(pes, 2, [128, 384], F32, "cbm")
        lhr = ring(pes, 3, [128, 384], F32, "lh")
        er = ring(pes, 3, [128, 384], BF16, "er")
        wr = ring(pes, 3, [128, 384], BF16, "wr")
        t1r = ring(pes, 4, [128, 512], F32, "t1")
        t2r = ring(pes, 2, [128, 512], F32, "t2")
        smr = ring(pes, 4, [128, 4], F32, "ssm")
        ynr = ring(pes, 2, [128, 512], BF16, "yn")
        yts = ring(pes, 2, [128, 4, 128], BF16, "yts")
        xdr = ring(pes, 2, [128, 2, 512], BF16, "xd")
        p_cum = ps(pes, [128, 512], F32, "pcum")
        p_cb = ring(pes, 1, [128, 512], F32, "pcb", psum=True)
        p_seg = ring(pes, 2, [128, 512], F32, "pseg", psum=True)
        p_yd = ring(pes, 1, [128, 2, 512], F32, "pyd", psum=True)
        p_yo = ring(pes, 1, [128, 512], F32, "pyo", psum=True)
        p_tr = ring(pes, 1, [128, 4, 128], BF16, "ptr2", psum=True)
        junk = sb(pes, [128, 512], BF16, "junk2")
        dsk = sb(pes, [128, 2048], F32, "dsk")
        S.dma('sp', dsk[:], dsk_d[:, :], writes=[dsk], slot='dsk')
        U2 = cst[:, C_U:C_U + 256]
        U1 = cst[:, C_U:C_U + 128]
        L0 = cst[:, C_L0:C_L0 + 128]
        ONES = cst[:, C_ONES:C_ONES + 128]
        S.op('dve', lambda: nc.vector.memset(state[:], 0.0), writes=[(state, 0), (state, 1), (state, 2), (state, 3)])
        S.op('dve', lambda: nc.vector.memset(stbf[:], 0.0), writes=[(stbf, 0), (stbf, 1), (stbf, 2), (stbf, 3)])
        for c in range(NCH):
            tok0 = c * 256
            x2 = xr.next(); b2 = br.next(); bT = bTr.next(); cT = cTr.next(); dt2 = dtr2.next(); z2 = zr.next()
            cbs = {}; pyds = {}; wws = {}; pend = []
            S.dma('sp', x2[:], xtm_s[tok0:tok0 + 256, :].rearrange("(j p) c -> p j c", p=128), writes=[x2], slot=('sx', xr.i))
            S.dma('sp', b2[:], btm_s[tok0:tok0 + 256, :].rearrange("(j p) c -> p j c", p=128), writes=[b2], slot=('sb', br.i))
            S.dma('sp', bT[:], bT_s[:, tok0:tok0 + 256].rearrange("(g p) t -> p g t", p=128), writes=[bT], slot=('sbT', bTr.i))
            S.dma('sp', cT[:], cT_s[:, tok0:tok0 + 256].rearrange("(g p) t -> p g t", p=128), writes=[cT], slot=('scT', cTr.i))
            S.dma('sp', dt2[:], dt_s[tok0:tok0 + 256, :].rearrange("(j p) c -> p j c", p=128), writes=[dt2], slot=('sdt', dtr2.i))
            S.dma('sp', z2[:], zs_s[tok0:tok0 + 256, :].rearrange("(j p) c -> p j c", p=128), writes=[z2], slot=('sz', zr.i))
            S.op('dve', lambda: nc.vector.tensor_tensor(out=dtA[:], in0=dt2[:], in1=A_bc.unsqueeze(1).broadcast_to([128, 2, 32]), op=ALU.mult),
                 reads=[dt2], writes=[dtA])
            S.op('pe', lambda: nc.tensor.matmul(p_cum[:, 0:32], lhsT=U1, rhs=dtA[:, 0, :], start=True, stop=True), reads=[dtA], writes=[p_cum], signal=False)
            S.op('pe', lambda: nc.tensor.matmul(p_cum[:, 32:64], lhsT=ONES, rhs=dtA[:, 0, :], start=True, stop=False), reads=[dtA], writes=[p_cum], signal=False)
            S.op('pe', lambda: nc.tensor.matmul(p_cum[:, 32:64], lhsT=U1, rhs=dtA[:, 1, :], start=False, stop=True), reads=[dtA], writes=[p_cum], signal=False)
            S.op('pe', lambda: nc.tensor.matmul(p_cum[:, 64:96], lhsT=ONES, rhs=dtA[:, 0, :], start=True, stop=False), reads=[dtA], writes=[p_cum], signal=False)
            S.op('pe', lambda: nc.tensor.matmul(p_cum[:, 64:96], lhsT=ONES, rhs=dtA[:, 1, :], start=False, stop=True), reads=[dtA], writes=[p_cum])
            S.op('dve', lambda: nc.vector.tensor_copy(out=cumT[:].rearrange("p j h -> p (j h)"), in_=p_cum[:, 0:64]), reads=[p_cum], writes=[cumT])
            S.op('dve', lambda: nc.vector.tensor_copy(out=cend[:], in_=p_cum[:, 64:96]), reads=[p_cum], writes=[cend])
            S.op('act', lambda: nc.scalar.activation(out=ecum[:], in_=cumT[:], func=AF.Exp), reads=[cumT], writes=[ecum])
            S.op('act', lambda: nc.scalar.activation(out=sdec[:], in_=cend[:], func=AF.Exp), reads=[cend], writes=[sdec])
            S.op('dve', lambda: nc.vector.tensor_tensor(out=dend[:], in0=cend[:].unsqueeze(1).broadcast_to([128, 2, 32]), in1=cumT[:], op=ALU.subtract),
                 reads=[cend, cumT], writes=[dend])
            S.op('act', lambda: nc.scalar.activation(out=dend[:], in_=dend[:], func=AF.Exp), reads=[dend], writes=[dend])
            S.op('dve', lambda: nc.vector.tensor_tensor(out=dend[:], in0=dend[:], in1=dt2[:], op=ALU.mult), reads=[dend, dt2], writes=[dend])
            def prologue(g):
                pcb = p_cb.next()
                S.op('pe', lambda: nc.tensor.matmul(pcb[:, 0:256], lhsT=bT[:, g, 0:128], rhs=cT[:, g, 0:256], start=True, stop=True),
                     reads=[bT, cT], writes=[pcb], signal=False)
                S.op('pe', lambda: nc.tensor.matmul(pcb[:, 256:384], lhsT=bT[:, g, 128:256], rhs=cT[:, g, 128:256], start=True, stop=True),
                     reads=[bT, cT], writes=[pcb])
                cb_ = cbm.next()
                S.op('dve', lambda: nc.vector.tensor_tensor(out=cb_[:, 0:256], in0=pcb[:, 0:256], in1=U2, op=ALU.mult), reads=[pcb], writes=[(cb_, 0)])
                S.op('dve', lambda: nc.vector.tensor_tensor(out=cb_[:, 256:384], in0=pcb[:, 256:384], in1=U1, op=ALU.mult), reads=[pcb], writes=[(cb_, 1)])
                cbs[g] = cb_
                pyds[g] = p_yd.next()

            def stageA(g, e):
                if e == 0:
                    prologue(g)
                cb_ = cbs[g]
                h = g * 8 + e
                lh = lhr.next()
                S.op('dve', lambda: nc.vector.tensor_scalar(out=lh[:, 0:128], in0=L0, scalar1=dtA[:, 0, h:h + 1], scalar2=None, op0=ALU.mult),
                     reads=[dtA], writes=[(lh, 0)])
                S.op('dve', lambda: nc.vector.tensor_scalar(out=lh[:, 128:256], in0=ONES, scalar1=dtA[:, 1, h:h + 1], scalar2=None, op0=ALU.mult),
                     reads=[dtA], writes=[(lh, 1)])
                S.op('dve', lambda: nc.vector.tensor_scalar(out=lh[:, 256:384], in0=L0, scalar1=dtA[:, 1, h:h + 1], scalar2=None, op0=ALU.mult),
                     reads=[dtA], writes=[(lh, 2)])
                pseg = p_seg.next()
                S.op('pe', lambda: nc.tensor.matmul(pseg[:, 0:256], lhsT=lh[:, 0:128], rhs=U2, start=True, stop=False),
                     reads=[(lh, 0)], writes=[pseg], signal=False)
                S.op('pe', lambda: nc.tensor.matmul(pseg[:, 128:256], lhsT=lh[:, 128:256], rhs=U1, start=False, stop=True),
                     reads=[(lh, 1)], writes=[pseg], signal=False)
                S.op('pe', lambda: nc.tensor.matmul(pseg[:, 256:384], lhsT=lh[:, 256:384], rhs=U1, start=True, stop=True),
                     reads=[(lh, 2)], writes=[pseg])
                ee = er.next()
                S.op('act', lambda: nc.scalar.activation(out=ee[:], in_=pseg[:, 0:384], func=AF.Exp), reads=[pseg], writes=[ee])
                ww = wr.next()
                S.op('dve', lambda: nc.vector.scalar_tensor_tensor(out=ww[:, 0:256], in0=ee[:, 0:256], scalar=dt2[:, 0, h:h + 1], in1=cb_[:, 0:256],
                                                                    op0=ALU.mult, op1=ALU.mult), reads=[ee, dt2, (cb_, 0)], writes=[(ww, 0)])
                S.op('dve', lambda: nc.vector.scalar_tensor_tensor(out=ww[:, 256:384], in0=ee[:, 256:384], scalar=dt2[:, 1, h:h + 1], in1=cb_[:, 256:384],
                                                                    op0=ALU.mult, op1=ALU.mult), reads=[ee, dt2, (cb_, 1)], writes=[(ww, 1)])
                wws[(g, e)] = ww

            def stageB(g, e):
                h = g * 8 + e
                ww = wws.pop((g, e))
                pyd = pyds[g]
                xs0 = x2[:, 0, h * 64:(h + 1) * 64]
                xs1 = x2[:, 1, h * 64:(h + 1) * 64]
                S.op('pe', lambda: nc.tensor.matmul(pyd[:, 0, e * 64:(e + 1) * 64], lhsT=ww[:, 0:128], rhs=xs0, start=True, stop=True),
                     reads=[(ww, 0), x2], writes=[pyd], signal=False)
                S.op('pe', lambda: nc.tensor.matmul(pyd[:, 1, e * 64:(e + 1) * 64], lhsT=ww[:, 128:256], rhs=xs0, start=True, stop=False),
                     reads=[(ww, 0), x2], writes=[pyd], signal=False)
                S.op('pe', lambda: nc.tensor.matmul(pyd[:, 1, e * 64:(e + 1) * 64], lhsT=ww[:, 256:384], rhs=xs1, start=False, stop=True),
                     reads=[(ww, 1), x2], writes=[pyd], signal=(e == 7))
                if e == 7:
                    epilogue(g)

            def epilogue(g):
                pyd = pyds.pop(g)
                t1s = []
                for lt in range(2):
                    t1 = t1r.next()
                    S.op('act', lambda: nc.scalar.copy(out=t1[:], in_=pyd[:, lt, :]), reads=[pyd], writes=[t1])
                    t1s.append(t1)
                for lt in range(2):
                    pyo = p_yo.next()
                    S.op('pe', lambda: nc.tensor.matmul(pyo[:], lhsT=cT[:, g, lt * 128:(lt + 1) * 128], rhs=stbf[:, g, :], start=True, stop=True),
                         reads=[cT, (stbf, g)], writes=[pyo])
                    t1 = t1s[lt]
                    t2 = t2r.next()
                    S.op('dve', lambda: nc.vector.tensor_tensor(out=t2[:].rearrange("p (e q) -> p e q", e=8), in0=pyo[:].rearrange("p (e q) -> p e q", e=8),
                                                                 in1=ecum[:, lt, g * 8:(g + 1) * 8].unsqueeze(2).broadcast_to([128, 8, 64]), op=ALU.mult),
                         reads=[pyo, ecum], writes=[t2])
                    S.op('dve', lambda: nc.vector.tensor_tensor(out=t1[:], in0=t1[:], in1=t2[:], op=ALU.add), reads=[t1, t2], writes=[t1])
                    S.op('dve', lambda: nc.vector.tensor_tensor(out=t2[:], in0=x2[:, lt, g * 512:(g + 1) * 512],
                                                                 in1=dsk[:, g * 512:(g + 1) * 512], op=ALU.mult), reads=[x2, dsk], writes=[t2])
                    S.op('dve', lambda: nc.vector.tensor_tensor(out=t1[:], in0=t1[:], in1=t2[:], op=ALU.add), reads=[t1, t2], writes=[t1])
                    S.op('dve', lambda: nc.vector.tensor_tensor(out=t1[:], in0=t1[:], in1=z2[:, lt, g * 512:(g + 1) * 512], op=ALU.mult),
                         reads=[t1, z2], writes=[t1])
                    sm = smr.next()
                    S.op('act', lambda: nc.scalar.activation(out=junk[:], in_=t1[:], func=AF.Square, accum_out=sm[:, 0:1]), reads=[t1], writes=[junk, sm])
                    S.op('dve', lambda: nc.vector.tensor_scalar(out=sm[:, 1:2], in0=sm[:, 0:1], scalar1=1.0 / 512, scalar2=EPS, op0=ALU.mult, op1=ALU.add),
                         reads=[sm], writes=[sm])
                    S.op('act', lambda: nc.scalar.activation(out=sm[:, 2:3], in_=sm[:, 1:2], func=AF.Sqrt), reads=[sm], writes=[sm])
                    S.op('dve', lambda: nc.vector.reciprocal(out=sm[:, 3:4], in_=sm[:, 2:3]), reads=[sm], writes=[sm])
                    yn = ynr.next()
                    S.op('act', lambda: nc.scalar.activation(out=yn[:], in_=t1[:], func=AF.Copy, scale=sm[:, 3:4]), reads=[t1, sm], writes=[yn])

                    def _tr(yn=yn, lt=lt, g=g):
                        ptr_ = p_tr.next()
                        for j in range(4):
                            S.op('pe', lambda: nc.tensor.transpose(out=ptr_[:, j, :], in_=yn[:, j * 128:(j + 1) * 128], identity=identb),
                                 reads=[yn], writes=[ptr_], signal=(j == 3))
                        yt = yts.next()
                        S.op('dve', lambda: nc.vector.tensor_tensor(out=yt[:], in0=ptr_[:], in1=cst[:, C_GNW + g * 4:C_GNW + g * 4 + 4].unsqueeze(2).broadcast_to([128, 4, 128]),
                                                                     op=ALU.mult), reads=[ptr_], writes=[yt])
                        S.dma('pool', ynT_s[g * 4:(g + 1) * 4, :, tok0 + lt * 128:tok0 + (lt + 1) * 128].rearrange("j p t -> p j t"), yt[:], reads=[yt], slot=('yts', yts.i))
                    pend.append(_tr)
                xd = xdr.next()
                S.op('dve', lambda: nc.vector.tensor_tensor(out=xd[:].rearrange("p j (e q) -> p j e q", e=8),
                                                             in0=x2[:, :, g * 512:(g + 1) * 512].rearrange("p j (e q) -> p j e q", e=8),
                                                             in1=dend[:, :, g * 8:(g + 1) * 8].unsqueeze(3).broadcast_to([128, 2, 8, 64]), op=ALU.mult),
                     reads=[x2, dend], writes=[xd])

                def _st(xd=xd, g=g):
                    pst = p_yo.next()
                    S.op('pe', lambda: nc.tensor.matmul(pst[:], lhsT=b2[:, 0, g * 128:(g + 1) * 128], rhs=xd[:, 0, :], start=True, stop=False), reads=[b2, xd], writes=[pst], signal=False)
                    S.op('pe', lambda: nc.tensor.matmul(pst[:], lhsT=b2[:, 1, g * 128:(g + 1) * 128], rhs=xd[:, 1, :], start=False, stop=True), reads=[b2, xd], writes=[pst])
                    S.op('dve', lambda: nc.vector.tensor_tensor(out=state[:, g, :].rearrange("p (e q) -> p e q", e=8), in0=state[:, g, :].rearrange("p (e q) -> p e q", e=8),
                                                                 in1=sdec[:, g * 8:(g + 1) * 8].unsqueeze(2).broadcast_to([128, 8, 64]), op=ALU.mult),
                         reads=[(state, g), sdec], writes=[(state, g)])
                    S.op('dve', lambda: nc.vector.tensor_tensor(out=state[:, g, :], in0=state[:, g, :], in1=pst[:], op=ALU.add), reads=[(state, g), pst], writes=[(state, g)])
                    S.op('act', lambda: nc.scalar.copy(out=stbf[:, g, :], in_=state[:, g, :]), reads=[(state, g)], writes=[(stbf, g)])
                pend.append(_st)

            hitems = [(g, e) for g in range(4) for e in range(8)]
            LAS = 2
            for n in range(len(hitems) + LAS):
                if n < len(hitems):
                    stageA(*hitems[n])
                    if pend and hitems[n][1] >= 1:
                        pend.pop(0)()
                if n - LAS >= 0:
                    stageB(*hitems[n - LAS])
            while pend:
                pend.pop(0)()
        S.barrier()
    chk('S')

    def gather(src, dst):
        nc.gpsimd.collective_compute("AllGather", ALU.bypass, replica_groups=pair_groups,
                                     ins=[src.opt()], outs=[dst.opt()]).then_inc(cc_sem)
        n_cc[0] += 1

    for j in range(16):
        for tq in range(0, T, 8192):
            gather(ynT_s[j], ynT_g[j])

    with ExitStack() as pes:
        qTr = ring(pes, 2, [128, T], BF16, "mq")
        kTr = ring(pes, 2, [128, T], BF16, "mk")
        var = ring(pes, 2, [128, NT, 130], BF16, "mv")
        gsr = ring(pes, 2, [128, NT, 128], BF16, "mg")
        kmf = sb(pes, [128, 32], F32, "kmf")
        kmb = sb(pes, [128, 32], BF16, "kmb")
        gbuf = ring(pes, 2, [128, 32], F32, "gbuf")
        t8r = ring(pes, 2, [128, 8], F32, "t8")
        selr = ring(pes, 6, [128, 32], F32, "sel")
        ptr_ = ring(pes, 4, [128, 512], BF16, "mp")
        oacc = ring(pes, 4, [128, 132], F32, "oacc")
        rdr = ring(pes, 4, [128, 2], F32, "rd")
        ogr = ring(pes, 2, [128, 128], BF16, "og")
        ogT = ring(pes, 2, [128, 256], BF16, "ogT")
        p_g = ring(pes, 1, [128, 512], F32, "pg", psum=True)
        p_s = ring(pes, 4, [128, 512], F32, "pS", psum=True)
        p_o = ring(pes, 2, [128, 2, 256], F32, "pO", psum=True)
        p_t = ring(pes, 1, [128, 2, 128], BF16, "pT", psum=True)
        scale = 128.0 ** -0.5
        NBLK = T // 256
        LA = 3
        cmask = cstb[:, B_CM0:B_CM0 + 512]
        for hd in range(8):
            qT = qTr.next(); kT = kTr.next(); va = var.next(); gs = gsr.next()
            S.dma('sp', qT[:], qT_s[hd * 128:(hd + 1) * 128, :], writes=[qT], slot=('mq', qTr.i))
            S.dma('sp', kT[:], kT_s[hd * 128:(hd + 1) * 128, :], writes=[kT], slot=('mk', kTr.i))
            S.dma('sp', va[:, :, 0:128], v_s[:, hd * 128:(hd + 1) * 128].rearrange("(j p) c -> p j c", p=128), writes=[(va, 'v')], slot=('mv', var.i))
            S.op('dve', lambda: nc.vector.memset(va[:, :, 128:130], 1.0), writes=[(va, 'o')])
            S.dma('sp', gs[:], gs_s[:, hd * 128:(hd + 1) * 128].rearrange("(j p) c -> p j c", p=128), writes=[gs], slot=('mg', gsr.i))
            S.op('dve', lambda: nc.vector.tensor_reduce(out=kmf[:, 0:NBLK], in_=kT[:].rearrange("p (n t) -> p n t", t=256), op=ALU.add, axis=AX.X),
                 reads=[kT], writes=[kmf])
            S.op('act', lambda: nc.scalar.activation(out=kmb[:, 0:NBLK], in_=kmf[:, 0:NBLK], func=AF.Copy, scale=1.0 / 256), reads=[kmf], writes=[kmb])
            gb = [gbuf.next(), gbuf.next()]
            for qt in range(2):
                S.op('dve', lambda: nc.vector.memset(gb[qt][:], NEGBIG), writes=[gb[qt]])
            items = [(i, j) for i in range(NBLK) for j in [i] + list(range(i))]
            selm = {}
            pSm = {}
            oam = {}

            def emit_qk(n):
                i, j = items[n]
                if j == i and i >= 1:
                    pg = p_g.next()
                    for qt in range(2):
                        q0 = i * 256 + qt * 128
                        S.op('pe', lambda: nc.tensor.matmul(pg[:, qt * 32:qt * 32 + NBLK], lhsT=qT[:, q0:q0 + 128], rhs=kmb[:, 0:NBLK], start=True, stop=True),
                             reads=[qT, kmb], writes=[pg], signal=(qt == 1))
                    sl = []
                    for qt in range(2):
                        S.op('dve', lambda: nc.vector.tensor_copy(out=gb[qt][:, 0:i], in_=pg[:, qt * 32:qt * 32 + i]), reads=[pg], writes=[gb[qt]])
                        t8 = t8r.next()
                        S.op('dve', lambda: nc.vector.max(out=t8[:], in_=gb[qt][:]), reads=[gb[qt]], writes=[t8])
                        sel = selr.next()
                        S.op('dve', lambda: nc.vector.tensor_scalar(out=sel[:], in0=gb[qt][:], scalar1=t8[:, 2:3], scalar2=None, op0=ALU.is_ge),
                             reads=[gb[qt], t8], writes=[sel])
                        sl.append(sel)
                    selm[i] = sl
                pS = p_s.next()
                for kt in range(2):
                    k0 = j * 256 + kt * 128
                    S.op('pe', lambda: nc.tensor.matmul(pS[:, kt * 256:(kt + 1) * 256], lhsT=kT[:, k0:k0 + 128], rhs=qT[:, i * 256:(i + 1) * 256], start=True, stop=True),
                         reads=[kT, qT], writes=[pS], signal=(kt == 1))
                pSm[n] = pS

            def emit_rest(n):
                i, j = items[n]
                pS = pSm.pop(n)
                if j == i:
                    oam[i] = [oacc.next(), oacc.next()]
                oa = oam[i]
                pT = ptr_.next()
                S.op('act', lambda: nc.scalar.activation(out=pT[:], in_=pS[:], func=AF.Exp, scale=scale), reads=[pS], writes=[pT])
                if j == i:
                    S.op('dve', lambda: nc.vector.tensor_tensor(out=pT[:], in0=pT[:], in1=cmask, op=ALU.mult), reads=[pT], writes=[pT])
                pO = p_o.next()
                for qt in range(2):
                    for kt in range(2):
                        S.op('pe', lambda: nc.tensor.matmul(pO[:, qt, 0:130], lhsT=pT[:, kt * 256 + qt * 128:kt * 256 + (qt + 1) * 128], rhs=va[:, j * 2 + kt, :],
                                                            start=(kt == 0), stop=(kt == 1)),
                             reads=[pT, (va, 'v'), (va, 'o')], writes=[pO], signal=(qt == 1 and kt == 1))
                for qt in range(2):
                    if j == i:
                        S.op('dve', lambda: nc.vector.tensor_copy(out=oa[qt][:, 0:130], in_=pO[:, qt, 0:130]), reads=[pO], writes=[oa[qt]])
                    else:
                        S.op('dve', lambda: nc.vector.scalar_tensor_tensor(out=oa[qt][:, 0:130], in0=pO[:, qt, 0:130], scalar=selm[i][qt][:, j:j + 1],
                                                                            in1=oa[qt][:, 0:130], op0=ALU.mult, op1=ALU.add),
                             reads=[pO, selm[i][qt], oa[qt]], writes=[oa[qt]])
                if (j == i - 1) or (i == 0):
                    pt_ = p_t.next()
                    for qt in range(2):
                        rd = rdr.next()
                        S.op('dve', lambda: nc.vector.reciprocal(out=rd[:, 0:1], in_=oa[qt][:, 128:129]), reads=[oa[qt]], writes=[rd])
                        og = ogr.next()
                        S.op('dve', lambda: nc.vector.scalar_tensor_tensor(out=og[:], in0=oa[qt][:, 0:128], scalar=rd[:, 0:1], in1=gs[:, i * 2 + qt, :],
                                                                            op0=ALU.mult, op1=ALU.mult), reads=[oa[qt], rd, gs], writes=[og])
                        S.op('pe', lambda: nc.tensor.transpose(out=pt_[:, qt, :], in_=og[:], identity=identb), reads=[og],
                             writes=[pt_], signal=(qt == 1))
                    ot = ogT.next()
                    S.op('act', lambda: nc.scalar.copy(out=ot[:], in_=pt_[:].rearrange("p a b -> p (a b)")), reads=[pt_], writes=[ot])
                    S.dma('pool', ogT_s[hd][:, i * 256:(i + 1) * 256], ot[:], reads=[ot], slot=('ogT', ogT.i))
                    oam.pop(i)
                    selm.pop(i, None)

            NI = len(items)
            for n in range(NI + LA):
                if n < NI:
                    emit_qk(n)
                if n - LA >= 0:
                    emit_rest(n - LA)
            S._wait('pool', [(k, v) for k, v in S.val.items() if not isinstance(k, str) and k[1] in (('ogT', 0), ('ogT', 1))])
            gather(ogT_s[hd], ogT_g[hd])
        S.barrier()
    nc.sync.wait_ge(cc_sem, n_cc[0])
    nc.gpsimd.wait_ge(cc_sem, n_cc[0])
    chk('M')

    with ExitStack() as pes:
        TB = 256
        hTf = sb(pes, [128, 16, TB], BF16, "fh")
        ynf = sb(pes, [128, 32, TB], BF16, "fy")
        ogf = sb(pes, [128, 16, TB], BF16, "fo")
        mixT = sb(pes, [128, 16, TB], BF16, "fm")
        stgr = ring(pes, 2, [128, 16, TB], BF16, "fst")
        wr1 = ring(pes, 6, [128, 16, 128], BF16, "fw1")
        wor = ring(pes, 2, [128, 16, 512], BF16, "fwo")
        g1r = ring(pes, 2, [128, TB], F32, "fg1")
        g2r = ring(pes, 2, [128, TB], F32, "fg2")
        xr = ring(pes, 2, [128, 2048], F32, "fx")
        xor_ = ring(pes, 2, [128, 2048], F32, "fxo")
        junk = sb(pes, [128, 2048], BF16, "fj")
        fnw = sb(pes, [128, 2048], F32, "fnw")
        smr = ring(pes, 4, [128, 4], F32, "fsm")
        psr = ring(pes, 6, [128, 512], F32, "fps", psum=True)
        pdr = ring(pes, 2, [128, 512], F32, "fpd", psum=True)
        S.dma('sp', fnw[:], fnw_d[:, :], writes=[fnw], slot='fnw')
        s_lo = cst[:, C_BL:C_BL + 1]
        s_hi = cst[:, C_BL + 1:C_BL + 2]

        def blend_load(dst, n, lo_src, hi_src, key):
            S.dma('sp', dst, lo_src, writes=[key], slot=key)
            st = stgr.next()
            S.dma('sp', st[:, :n, :], hi_src, writes=[st], slot=('fst', stgr.i))
            S.op('dve', lambda: nc.vector.tensor_scalar(out=dst, in0=dst, scalar1=s_lo, scalar2=None, op0=ALU.mult), reads=[key], writes=[key])
            S.op('dve', lambda: nc.vector.scalar_tensor_tensor(out=dst, in0=st[:, :n, :], scalar=s_hi, in1=dst, op0=ALU.mult, op1=ALU.add),
                 reads=[st, key], writes=[key])

        for bi in range(TH // TB):
            lo = bi * TB
            hi = TH + bi * TB
            blend_load(hTf[:], 16, hT_s[:, :, lo:lo + TB].rearrange("k p t -> p k t"), hT_s[:, :, hi:hi + TB].rearrange("k p t -> p k t"), ('fh', 0))
            for r in range(2):
                blend_load(ynf[:, r * 16:(r + 1) * 16, :], 16, ynT_g[:, r * 128:(r + 1) * 128, lo:lo + TB].rearrange("j p t -> p j t"),
                           ynT_g[:, r * 128:(r + 1) * 128, hi:hi + TB].rearrange("j p t -> p j t"), ('fy', r))
            for r in range(2):
                blend_load(ogf[:, r * 8:(r + 1) * 8, :], 8, ogT_g[:, r * 128:(r + 1) * 128, lo:lo + TB].rearrange("j p t -> p j t"),
                           ogT_g[:, r * 128:(r + 1) * 128, hi:hi + TB].rearrange("j p t -> p j t"), ('fo', r))
            for mc in range(16):
                srcs = [wgm_s[mc], wgm_s[16 + mc], wssm_s[mc][:, 0:2048], wssm_s[mc][:, 2048:4096], wattn_s[mc]]
                wch = []
                for s_ in srcs:
                    wt_ = wr1.next()
                    S.dma('sp', wt_[:], s_.rearrange("p (k c) -> p k c", k=16), writes=[wt_], slot=('fw1', wr1.i))
                    wch.append(wt_)
                pg1 = psr.next(); pg2 = psr.next(); pys = psr.next(); pya = psr.next()
                for k in range(16):
                    S.op('pe', lambda: nc.tensor.matmul(pg1[:, :TB], lhsT=wch[0][:, k, :], rhs=hTf[:, k, :], start=(k == 0), stop=(k == 15)),
                         reads=[wch[0], ('fh', 0)], writes=[pg1], signal=(k == 15))
                for k in range(16):
                    S.op('pe', lambda: nc.tensor.matmul(pg2[:, :TB], lhsT=wch[1][:, k, :], rhs=hTf[:, k, :], start=(k == 0), stop=(k == 15)),
                         reads=[wch[1], ('fh', 0)], writes=[pg2], signal=(k == 15))
                for k in range(32):
                    S.op('pe', lambda: nc.tensor.matmul(pys[:, :TB], lhsT=wch[2 + k // 16][:, k % 16, :], rhs=ynf[:, k, :], start=(k == 0), stop=(k == 31)),
                         reads=[wch[2 + k // 16], ('fy', k // 16)], writes=[pys], signal=(k == 31))
                for k in range(16):
                    S.op('pe', lambda: nc.tensor.matmul(pya[:, :TB], lhsT=wch[4][:, k, :], rhs=ogf[:, k, :], start=(k == 0), stop=(k == 15)),
                         reads=[wch[4], ('fo', k // 8)], writes=[pya], signal=(k == 15))
                g1 = g1r.next(); g2 = g2r.next()
                S.op('act', lambda: nc.scalar.activation(out=g1[:], in_=pg1[:, :TB], func=AF.Sigmoid, bias=cst[:, C_GB + mc:C_GB + mc + 1], scale=1.0),
                     reads=[pg1], writes=[g1])
                S.op('act', lambda: nc.scalar.activation(out=g2[:], in_=pg2[:, :TB], func=AF.Sigmoid, bias=cst[:, C_GB + 16 + mc:C_GB + 17 + mc], scale=1.0),
                     reads=[pg2], writes=[g2])
                S.op('dve', lambda: nc.vector.tensor_tensor(out=g1[:], in0=g1[:], in1=pys[:, :TB], op=ALU.mult), reads=[g1, pys], writes=[g1])
                S.op('dve', lambda: nc.vector.tensor_tensor(out=g2[:], in0=g2[:], in1=pya[:, :TB], op=ALU.mult), reads=[g2, pya], writes=[g2])
                S.op('dve', lambda: nc.vector.tensor_tensor(out=mixT[:, mc, :], in0=g1[:], in1=g2[:], op=ALU.add), reads=[g1, g2], writes=[mixT])
            xts = []
            xos = []
            for tt in range(TB // 128):
                xt = xr.next()
                S.dma('sp', xt[:], xf_d[lo + tt * 128:lo + (tt + 1) * 128, :], writes=[xt], slot=('fx', xr.i))
                xts.append(xt)
                xos.append(xor_.next())
            for dblk in range(4):
                wo = wor.next()
                S.dma('sp', wo[:], wout_s[dblk].rearrange("p (k c) -> p k c", k=16), writes=[wo], slot=('fwo', wor.i))
                for tt in range(TB // 128):
                    pd = pdr.next()
                    for k in range(16):
                        S.op('pe', lambda: nc.tensor.matmul(pd[:], lhsT=mixT[:, k, tt * 128:(tt + 1) * 128], rhs=wo[:, k, :], start=(k == 0), stop=(k == 15)),
                             reads=[wo, mixT], writes=[pd], signal=(k == 15))
                    S.op('dve', lambda: nc.vector.tensor_tensor(out=xos[tt][:, dblk * 512:(dblk + 1) * 512], in0=pd[:], in1=xts[tt][:, dblk * 512:(dblk + 1) * 512], op=ALU.add),
                         reads=[pd, xts[tt]], writes=[(xos[tt], dblk)])
            for tt in range(TB // 128):
                sm = smr.next()
                xo = xos[tt]
                S.op('act', lambda: nc.scalar.activation(out=junk[:], in_=xo[:], func=AF.Square, accum_out=sm[:, 0:1]),
                     reads=[(xo, q) for q in range(4)], writes=[junk, sm])
                S.op('dve', lambda: nc.vector.tensor_scalar(out=sm[:, 1:2], in0=sm[:, 0:1], scalar1=1.0 / D, scalar2=EPS, op0=ALU.mult, op1=ALU.add),
                     reads=[sm], writes=[sm])
                S.op('act', lambda: nc.scalar.activation(out=sm[:, 2:3], in_=sm[:, 1:2], func=AF.Sqrt), reads=[sm], writes=[sm])
                S.op('dve', lambda: nc.vector.reciprocal(out=sm[:, 3:4], in_=sm[:, 2:3]), reads=[sm], writes=[sm])
                S.op('dve', lambda: nc.vector.scalar_tensor_tensor(out=xts[tt][:], in0=xo[:], scalar=sm[:, 3:4], in1=fnw[:], op0=ALU.mult, op1=ALU.mult),
                     reads=[(xo, q) for q in range(4)] + [sm, fnw], writes=[xts[tt]])
                S.dma('pool', out_d[lo + tt * 128:lo + (tt + 1) * 128, :], xts[tt][:], reads=[xts[tt]], slot=('fxs', xr.t.index(xts[tt])))
        S.barrier()
    es.close()
    return nc


_OFFS = np.cumsum([0, 4096, 6144, 64, 2048, 2048, 2048, 2048, 4096])


def _chunk(w, kc, ncol_blk):
    nblk = w.shape[1] // ncol_blk
    return np.ascontiguousarray(w.reshape(kc, 128, nblk, ncol_blk).transpose(2, 1, 0, 3)).reshape(nblk, 128, kc * ncol_blk)


def _shared_consts():
    t = np.arange(128)
    cst = np.zeros((128, 1024), np.float32)
    U = (t[:, None] <= t[None, :]).astype(np.float32)
    cst[:, 256:384] = U
    cst[:, 384:512] = 1.0
    cst[:, 512:640] = (t[:, None] > t[None, :]).astype(np.float32)
    cst[:, 640:768] = 1.0
    cst[:, 768:896] = np.eye(128, dtype=np.float32)
    cstb = np.zeros((128, 1024), np.float32)
    cstb[:, 0:128] = np.eye(128)
    col = np.arange(256)
    cstb[:, 128:384] = (t[:, None] <= col[None, :])
    cstb[:, 384:640] = ((128 + t[:, None]) <= col[None, :])
    return cst, cstb.astype(ml_dtypes.bfloat16)


def _prep_core(inp, b, hh, T, cst0, cstb):
    f = np.float32
    x = np.ascontiguousarray(inp["x"][b, :T]).astype(f, copy=False)
    w_in = inp["w_in"][0]
    z0, xbc0, dt0, q0, k0, v0, g0, gm0 = [int(v) for v in _OFFS[:8]]

    def cols(a, n):
        return w_in[:, a:a + n]

    Wz = cols(z0 + hh * 2048, 2048)
    Wx = cols(xbc0 + hh * 2048, 2048)
    WB = cols(xbc0 + 4096 + hh * 512, 512)
    WC = cols(xbc0 + 5120 + hh * 512, 512)
    Wdt = cols(dt0 + hh * 32, 32)
    Wq = cols(q0 + hh * 1024, 1024)
    Wk = cols(k0 + hh * 1024, 1024)
    Wv = cols(v0 + hh * 1024, 1024)
    Wg = cols(g0 + hh * 1024, 1024)
    Wgm = cols(gm0, 4096)
    wfm = _chunk(np.concatenate([Wx, WB, WC, Wq, Wk], 1), 16, 128)
    wtm = _chunk(np.concatenate([Wz, Wv, Wg], 1), 16, 512)
    wdt = _chunk(Wdt, 16, 32)[0]
    wgm = _chunk(Wgm, 16, 128)
    wssm = _chunk(inp["w_ssm_proj"][0], 32, 128)
    wattn = _chunk(inp["w_attn_proj"][0], 16, 128)
    wout = _chunk(inp["w_out"][0], 16, 512)
    cst = cst0.copy()
    cst[:, 0:16] = inp["norm_w"][0].reshape(16, 128).T
    conv_w = inp["conv_w"][0]
    conv_b = inp["conv_b"][0]
    chans = np.concatenate([hh * 2048 + np.arange(2048), 4096 + hh * 512 + np.arange(512), 5120 + hh * 512 + np.arange(512)])
    cw = conv_w[:, chans]
    cst[:, 16:112] = cw.reshape(4, 24, 128).transpose(2, 1, 0).reshape(128, 96)
    cst[:, 112:136] = conv_b[chans].reshape(24, 128).T
    cst[:, 136:168] = np.broadcast_to(inp["dt_bias"][0][hh * 32:(hh + 1) * 32], (128, 32))
    cst[:, 168:200] = np.broadcast_to(inp["A_log"][0][hh * 32:(hh + 1) * 32], (128, 32))
    cst[:, 200:216] = inp["ssm_norm_w"][0][hh * 2048:(hh + 1) * 2048].reshape(16, 128).T
    cst[:, 216:248] = inp["gate_bias"][0].reshape(32, 128).T
    cst[:, 248] = 1.0 - hh
    cst[:, 249] = float(hh)
    dsk = np.ascontiguousarray(np.broadcast_to(np.repeat(inp["D_skip"][0][hh * 32:(hh + 1) * 32], 64), (128, 2048))).astype(f)
    fnw = np.ascontiguousarray(np.broadcast_to(inp["final_norm_w"], (128, 2048))).astype(f)
    TH = T // 2
    return {"x": x, "wfm": wfm, "wtm": wtm, "wdt": np.ascontiguousarray(wdt), "wgm": wgm, "wssm": wssm, "wattn": wattn,
            "wout": wout, "cst": cst, "dsk": dsk, "fnw": fnw, "xf": np.ascontiguousarray(x[hh * TH:(hh + 1) * TH]), "cstb": cstb}


def run(inputs, T, nb, stop=None, debug=False):
    ncores = 2 * nb
    groups = [[2 * i, 2 * i + 1] for i in range(nb)]
    nc = build(T, groups, stop, debug)
    cst0, cstb = _shared_consts()
    in_maps = [_prep_core(inputs, c // 2, c % 2, T, cst0, cstb) for c in range(ncores)]
    res = run_bass_kernel_spmd(nc, in_maps, core_ids=list(range(ncores)))
    if debug:
        return res.results, in_maps
    out = np.empty((nb, T, D), np.float32)
    TH = T // 2
    for c in range(ncores):
        out[c // 2, (c % 2) * TH:(c % 2 + 1) * TH] = res.results[c]["out"]
    return out


def kernel(**inputs):
    inputs = {k: np.asarray(v) for k, v in inputs.items()}
    return run(inputs, 8192, 4)
```

```python
import numpy as np
from contextlib import ExitStack
import concourse.bass as bass
import concourse.mybir as mybir
from concourse.bass_utils import run_bass_kernel_spmd
import ml_dtypes

F32 = mybir.dt.float32
BF16 = mybir.dt.bfloat16
AF = mybir.ActivationFunctionType
ALU = mybir.AluOpType
AX = mybir.AxisListType

D = 2048
EPS = 1e-6
NEGBIG = -1.0e30


class Sched:
    def __init__(self, nc, es):
        self.nc = nc
        self.es = es
        self.E = {'pe': nc.tensor, 'act': nc.scalar, 'dve': nc.vector, 'pool': nc.gpsimd, 'sp': nc.sync}
        self.sems = {}
        self.val = {}
        for k in ('pe', 'act', 'dve', 'pool'):
            self.sems[k] = es.enter_context(nc.semaphore('c_' + k))
            self.val[k] = 0
        self.seen = {e: {} for e in self.E}
        self.w = {}
        self.r = {}
        self.nsem = 4

    def _wait(self, e, toks):
        need = {}
        for (k, v) in toks:
            if not isinstance(k, str):
                v = self.val[k]
            if need.get(k, 0) < v:
                need[k] = v
        for k, v in need.items():
            if k == e and e == 'pe':
                continue
            if self.seen[e].get(k, 0) >= v:
                continue
            self.E[e].wait_ge(self.sems[k], v)
            self.seen[e][k] = v

    def _key(self, r):
        if isinstance(r, tuple):
            return tuple(self._key(x) for x in r)
        if isinstance(r, (str, int)):
            return r
        return ('id', id(r))

    def _deps(self, e, reads, writes):
        toks = []
        for r in reads:
            toks += self.w.get(r, [])
        for w_ in writes:
            toks += self.w.get(w_, [])
            toks += self.r.get(w_, [])
        return toks

    def op(self, e, fn, reads=(), writes=(), signal=True):
        reads = [self._key(r) for r in reads]
        writes = [self._key(r) for r in writes]
        self._wait(e, self._deps(e, reads, writes))
        inst = fn()
        if signal:
            self.val[e] += 1
            inst.then_inc(self.sems[e], 1)
            tok = (e, self.val[e])
            for w_ in writes:
                self.w[w_] = [tok]
                self.r[w_] = []
        else:
            tok = (e, self.val[e] + 1)
            for w_ in writes:
                self.w[w_] = [tok]
                self.r[w_] = []
        for r in reads:
            self.r.setdefault(r, []).append(tok)
        return tok

    def dma(self, q, out, in_, reads=(), writes=(), slot=None):
        reads = [self._key(r) for r in reads]
        writes = [self._key(r) for r in writes]
        self._wait(q, self._deps(q, reads, writes))
        key = ('d', slot)
        if key not in self.sems:
            self.sems[key] = self.es.enter_context(self.nc.semaphore('d%d' % self.nsem))
            self.nsem += 1
            self.val[key] = 0
        self.val[key] += 16
        self.E[q].dma_start(out=out, in_=in_).then_inc(self.sems[key], 16)
        tok = (key, self.val[key])
        for w_ in writes:
            self.w[w_] = [tok]
            self.r[w_] = []
        for r in reads:
            self.r.setdefault(r, []).append(tok)
        return tok

    def barrier(self):
        for e in ('pe', 'act', 'dve', 'pool', 'sp'):
            for k, v in self.val.items():
                if v > 0 and k != e and self.seen[e].get(k, 0) < v:
                    self.E[e].wait_ge(self.sems[k], v)
                    self.seen[e][k] = v
        self.w.clear()
        self.r.clear()


class Ring:
    def __init__(self, tiles):
        self.t = tiles
        self.i = -1

    def next(self):
        self.i = (self.i + 1) % len(self.t)
        return self.t[self.i]


class _Stop(Exception):
    pass


def build(T, pair_groups, stop=None, debug=False):
    holder = {}
    try:
        _build(T, pair_groups, stop, holder, debug)
    except _Stop:
        holder['es'].close()
    return holder['nc']


def _build(T, pair_groups, stop, holder, debug):
    assert T % 512 == 0
    NT = T // 128
    NCH = T // 256
    TS = min(T, 2048)
    NSB = T // TS
    TH = T // 2
    nc = bass.Bass("TRN2", target_bir_lowering=False)

    def din(name, shape, dt=F32):
        return nc.dram_tensor(name, shape, dt, kind="ExternalInput")

    x_d = din("x", [T, D])
    wfm_d = din("wfm", [40, 128, 2048])
    wtm_d = din("wtm", [8, 128, 8192])
    wdt_d = din("wdt", [128, 512])
    wgm_d = din("wgm", [32, 128, 2048])
    wssm_d = din("wssm", [16, 128, 4096])
    wattn_d = din("wattn", [16, 128, 2048])
    wout_d = din("wout", [4, 128, 8192])
    cst_d = din("cst", [128, 1024])
    dsk_d = din("dsk", [128, 2048])
    fnw_d = din("fnw", [128, 2048])
    xf_d = din("xf", [T // 2, D])
    cstb_d = din("cstb", [128, 1024], BF16)
    out_d = nc.dram_tensor("out", [TH, D], F32, kind="ExternalOutput")

    def scr(name, shape, dt=BF16):
        if debug and not name.startswith(("ynT", "ogT")):
            return nc.dram_tensor(name, shape, dt, kind="ExternalOutput")
        return nc.dram_tensor(name, shape, dt)

    hT_s = scr("hT_s", [16, 128, T])
    wfm_s = scr("wfm_s", [40, 128, 2048])
    wtm_s = scr("wtm_s", [8, 128, 8192])
    wdt_s = scr("wdt_s", [128, 512])
    wgm_s = scr("wgm_s", [32, 128, 2048])
    wssm_s = scr("wssm_s", [16, 128, 4096])
    wattn_s = scr("wattn_s", [16, 128, 2048])
    wout_s = scr("wout_s", [4, 128, 8192])
    xtm_s = scr("xtm_s", [T, 2048])
    btm_s = scr("btm_s", [T, 512])
    bT_s = scr("bT_s", [512, T])
    cT_s = scr("cT_s", [512, T])
    qT_s = scr("qT_s", [1024, T])
    kT_s = scr("kT_s", [1024, T])
    zs_s = scr("zs_s", [T, 2048])
    v_s = scr("v_s", [T, 1024])
    gs_s = scr("gs_s", [T, 1024])
    dt_s = scr("dt_s", [T, 32], F32)
    ynT_s = scr("ynT_s", [16, 128, T])
    ynT_g = scr("ynT_g", [16, 256, T])
    ogT_s = scr("ogT_s", [8, 128, T])
    ogT_g = scr("ogT_g", [8, 256, T])

    es = ExitStack()
    holder['nc'] = nc
    holder['es'] = es
    S = Sched(nc, es)

    def chk(name):
        if stop == name:
            S.barrier()
            raise _Stop()
    cc_sem = es.enter_context(nc.semaphore("cc"))
    n_cc = [0]

    uid = [0]

    def sb(es_, shape, dt, name=None):
        uid[0] += 1
        return es_.enter_context(nc.sbuf_tensor("%s%d" % (name or "t", uid[0]), shape, dt))

    def ps(es_, shape, dt, name=None):
        uid[0] += 1
        return es_.enter_context(nc.psum_tensor("%s%d" % (name or "p", uid[0]), shape, dt))

    def ring(es_, n, shape, dt, name=None, psum=False):
        return Ring([(ps if psum else sb)(es_, shape, dt, name) for _ in range(n)])

    cst = sb(es, [128, 1024], F32, "cst")
    cstb = sb(es, [128, 1024], BF16, "cstb")
    S.dma('sp', cst[:], cst_d[:, :], writes=[cst], slot='cst')
    S.dma('sp', cstb[:], cstb_d[:, :], writes=[cstb], slot='cstb')
    C_NORMW = 0
    C_CW = 16
    C_CB = 112
    C_DTB = 136
    C_ALOG = 168
    C_GNW = 200
    C_GB = 216
    C_BL = 248
    C_U = 256
    C_L0 = 512
    C_ONES = 640
    C_IDF = 768
    B_ID = 0
    B_CM0 = 128
    B_CM1 = 384
    identb = cstb[:, B_ID:B_ID + 128]
    identf = cst[:, C_IDF:C_IDF + 128]
    S.barrier()
    S.op('act', lambda: nc.scalar.activation(out=cst[:, C_ALOG:C_ALOG + 32], in_=cst[:, C_ALOG:C_ALOG + 32], func=AF.Exp),
         reads=[cst], writes=[cst])
    S.op('dve', lambda: nc.vector.tensor_scalar(out=cst[:, C_ALOG:C_ALOG + 32], in0=cst[:, C_ALOG:C_ALOG + 32],
                                                 scalar1=-1.0, scalar2=None, op0=ALU.mult),
         reads=[cst], writes=[cst])
    S.barrier()
    A_bc = cst[:, C_ALOG:C_ALOG + 32]

    with ExitStack() as pes:
        win = ring(pes, 3, [128, 2048], F32, "win")
        wo = ring(pes, 3, [128, 2048], BF16, "wo")
        normw_b = cst[:, C_NORMW:C_NORMW + 16]
        cnt = [0]

        def conv_tile(src_ap, dst_ap, kcols, fold, k0=0, nk=16):
            a = win.next()
            o = wo.next()
            n = nk * kcols
            S.dma('sp', a[:, :n], src_ap, writes=[a], slot=('win', win.i))
            if fold:
                S.op('dve', lambda: nc.vector.tensor_tensor(
                    out=o[:, :n].rearrange("p (k c) -> p k c", k=nk),
                    in0=a[:, :n].rearrange("p (k c) -> p k c", k=nk),
                    in1=normw_b[:, k0:k0 + nk].unsqueeze(2).broadcast_to([128, nk, kcols]), op=ALU.mult),
                    reads=[a], writes=[o])
            else:
                cnt[0] += 1
                if cnt[0] % 2:
                    S.op('act', lambda: nc.scalar.copy(out=o[:, :n], in_=a[:, :n]), reads=[a], writes=[o])
                else:
                    S.op('dve', lambda: nc.vector.tensor_copy(out=o[:, :n], in_=a[:, :n]), reads=[a], writes=[o])
            S.dma('pool', dst_ap, o[:, :n], reads=[o], slot=('wo', wo.i))

        for c in range(40):
            conv_tile(wfm_d[c], wfm_s[c], 128, True)
        for c in range(8):
            for kq in range(4):
                conv_tile(wtm_d[c][:, kq * 2048:(kq + 1) * 2048], wtm_s[c][:, kq * 2048:(kq + 1) * 2048], 512, True, k0=kq * 4, nk=4)
        conv_tile(wdt_d[:, :], wdt_s[:, :], 32, True)
        for c in range(32):
            conv_tile(wgm_d[c], wgm_s[c], 128, True)
        for c in range(16):
            for hq in range(2):
                conv_tile(wssm_d[c][:, hq * 2048:(hq + 1) * 2048], wssm_s[c][:, hq * 2048:(hq + 1) * 2048], 128, False)
        for c in range(16):
            conv_tile(wattn_d[c], wattn_s[c], 128, False)
        for c in range(4):
            for kq in range(4):
                conv_tile(wout_d[c][:, kq * 2048:(kq + 1) * 2048], wout_s[c][:, kq * 2048:(kq + 1) * 2048], 512, False, nk=4)
        S.barrier()
    chk('W')

    with ExitStack() as pes:
        xring = ring(pes, 2, [128, 2048], F32, "xt")
        junk = sb(pes, [128, 2048], BF16, "junk")
        hbr = ring(pes, 2, [128, 2048], BF16, "hb")
        stg = ring(pes, 2, [128, 16, 512], BF16, "hst")
        smr = ring(pes, 4, [128, 4], F32, "sm")
        ptr = ring(pes, 2, [128, 16, 128], BF16, "ptr", psum=True)
        for i4 in range(T // 512):
            stage = stg.next()
            for ii in range(4):
                i = i4 * 4 + ii
                xt = xring.next()
                S.dma('sp', xt[:], x_d[i * 128:(i + 1) * 128, :], writes=[xt], slot=('xt', xring.i))
                sm = smr.next()
                S.op('act', lambda: nc.scalar.activation(out=junk[:], in_=xt[:], func=AF.Square, accum_out=sm[:, 0:1]),
                     reads=[xt], writes=[junk, sm])
                S.op('dve', lambda: nc.vector.tensor_scalar(out=sm[:, 1:2], in0=sm[:, 0:1], scalar1=1.0 / D, scalar2=EPS,
                                                             op0=ALU.mult, op1=ALU.add), reads=[sm], writes=[sm])
                S.op('act', lambda: nc.scalar.activation(out=sm[:, 2:3], in_=sm[:, 1:2], func=AF.Sqrt), reads=[sm], writes=[sm])
                S.op('dve', lambda: nc.vector.reciprocal(out=sm[:, 3:4], in_=sm[:, 2:3]), reads=[sm], writes=[sm])
                hb = hbr.next()
                S.op('dve', lambda: nc.vector.tensor_scalar(out=hb[:], in0=xt[:], scalar1=sm[:, 3:4], scalar2=None, op0=ALU.mult),
                     reads=[xt, sm], writes=[hb])
                pt = ptr.next()
                for k in range(16):
                    S.op('pe', lambda: nc.tensor.transpose(out=pt[:, k, :], in_=hb[:, k * 128:(k + 1) * 128], identity=identb),
                         reads=[hb], writes=[pt], signal=(k == 15))
                S.op('act', lambda: nc.scalar.copy(out=stage[:, :, ii * 128:(ii + 1) * 128], in_=pt[:]), reads=[pt], writes=[stage])
            S.dma('pool', hT_s[:, :, i4 * 512:(i4 + 1) * 512].rearrange("k p t -> p k t"), stage[:], reads=[stage], slot=('hst', stg.i))
        S.barrier()
    chk('A')

    with ExitStack() as pes:
        hT = sb(pes, [128, 16, TS], BF16, "hT")
        wfr = ring(pes, 3, [128, 16, 128], BF16, "wf")
        wtr = ring(pes, 2, [128, 16, 512], BF16, "wt")
        wdt = sb(pes, [128, 16, 32], BF16, "wdt")
        halo = sb(pes, [128, 24, 4], F32, "halo")
        xsr = ring(pes, 2, [128, 516], F32, "xs")
        accr = ring(pes, 2, [128, 512], F32, "acc")
        fmo = ring(pes, 3, [128, 512], BF16, "fmo")
        tmo = ring(pes, 3, [128, 512], BF16, "tmo")
        tms = ring(pes, 2, [128, 4, 128], BF16, "tms")
        dtr = ring(pes, 2, [128, 32], F32, "dtr")
        pacc = ring(pes, 4, [128, 512], F32, "pacc", psum=True)
        ptp = ring(pes, 2, [128, 4, 128], BF16, "ptp", psum=True)
        S.op('dve', lambda: nc.vector.memset(halo[:], 0.0), writes=[halo])
        S.dma('sp', wdt[:], wdt_s[:, :].rearrange("p (k c) -> p k c", k=16), writes=[wdt], slot='wdt')
        for sbi in range(NSB):
            t0 = sbi * TS
            for k in range(16):
                S.dma('sp', hT[:, k, :], hT_s[k][:, t0:t0 + TS], writes=[hT], slot='hT')
            pend = []
            for c in range(40):
                wf = wfr.next()
                S.dma('sp', wf[:], wfm_s[c].rearrange("p (k c) -> p k c", k=16), writes=[wf], slot=('wf', wfr.i))
                for tt in range(TS // 512):
                    tok0 = t0 + tt * 512
                    pa = pacc.next()
                    for k in range(16):
                        S.op('pe', lambda: nc.tensor.matmul(pa[:], lhsT=wf[:, k, :], rhs=hT[:, k, tt * 512:(tt + 1) * 512],
                                                            start=(k == 0), stop=(k == 15)),
                             reads=[wf, hT], writes=[pa], signal=(k == 15))
                    while pend:
                        pend.pop(0)()
                    if c < 24:
                        xs = xsr.next()
                        S.op('act', lambda: nc.scalar.copy(out=xs[:, 3:515], in_=pa[:]), reads=[pa], writes=[(xs, 'b')])
                        S.op('dve', lambda: nc.vector.tensor_copy(out=xs[:, 0:3], in_=halo[:, c, 0:3]), reads=[halo], writes=[(xs, 'h')])
                        S.op('dve', lambda: nc.vector.tensor_copy(out=halo[:, c, 0:3], in_=xs[:, 512:515]), reads=[(xs, 'b')], writes=[halo])
                        acc = accr.next()
                        cw = C_CW + c * 4
                        S.op('dve', lambda: nc.vector.tensor_scalar(out=acc[:], in0=xs[:, 0:512], scalar1=cst[:, cw:cw + 1],
                                                                     scalar2=cst[:, C_CB + c:C_CB + c + 1], op0=ALU.mult, op1=ALU.add),
                             reads=[(xs, 'b'), (xs, 'h')], writes=[acc])
                        for k in range(1, 4):
                            S.op('dve', lambda: nc.vector.scalar_tensor_tensor(out=acc[:], in0=xs[:, k:k + 512], scalar=cst[:, cw + k:cw + k + 1],
                                                                                in1=acc[:], op0=ALU.mult, op1=ALU.add),
                                 reads=[(xs, 'b'), (xs, 'h'), acc], writes=[acc])
                        fo = fmo.next()
                        S.op('act', lambda: nc.scalar.activation(out=fo[:], in_=acc[:], func=AF.Silu), reads=[acc], writes=[fo])
                        if c >= 16:
                            dst = (bT_s if c < 20 else cT_s)[((c - 16) % 4) * 128:((c - 16) % 4 + 1) * 128, tok0:tok0 + 512]
                            S.dma('pool', dst, fo[:], reads=[fo], slot=('fmo', fmo.i))
                        if c < 20:
                            def _tr(fo=fo, c=c, tok0=tok0):
                                pt = ptp.next()
                                for j in range(4):
                                    S.op('pe', lambda: nc.tensor.transpose(out=pt[:, j, :], in_=fo[:, j * 128:(j + 1) * 128], identity=identb),
                                         reads=[fo], writes=[pt], signal=(j == 3))
                                ts_ = tms.next()
                                S.op('act', lambda: nc.scalar.copy(out=ts_[:], in_=pt[:]), reads=[pt], writes=[ts_])
                                if c < 16:
                                    dst = xtm_s[tok0:tok0 + 512, c * 128:(c + 1) * 128]
                                else:
                                    dst = btm_s[tok0:tok0 + 512, (c - 16) * 128:(c - 15) * 128]
                                S.dma('pool', dst.rearrange("(j p) c -> p j c", p=128), ts_[:], reads=[ts_], slot=('tms', tms.i))
                            pend.append(_tr)
                    else:
                        fo = fmo.next()
                        S.op('act', lambda: nc.scalar.copy(out=fo[:], in_=pa[:]), reads=[pa], writes=[fo])
                        cc_ = c - 24
                        dst = (qT_s if cc_ < 8 else kT_s)[(cc_ % 8) * 128:(cc_ % 8 + 1) * 128, tok0:tok0 + 512]
                        S.dma('pool', dst, fo[:], reads=[fo], slot=('fmo', fmo.i))
            while pend:
                pend.pop(0)()
            for blk in range(8):
                wt = wtr.next()
                S.dma('sp', wt[:], wtm_s[blk].rearrange("p (k c) -> p k c", k=16), writes=[wt], slot=('wt', wtr.i))
                for tt in range(TS // 128):
                    tok0 = t0 + tt * 128
                    pa = pacc.next()
                    for k in range(16):
                        S.op('pe', lambda: nc.tensor.matmul(pa[:], lhsT=hT[:, k, tt * 128:(tt + 1) * 128], rhs=wt[:, k, :],
                                                            start=(k == 0), stop=(k == 15)),
                             reads=[wt, hT], writes=[pa], signal=(k == 15))
                    to = tmo.next()
                    if blk < 4:
                        S.op('act', lambda: nc.scalar.activation(out=to[:], in_=pa[:], func=AF.Silu), reads=[pa], writes=[to])
                        dst = zs_s[tok0:tok0 + 128, blk * 512:(blk + 1) * 512]
                    elif blk < 6:
                        S.op('act', lambda: nc.scalar.copy(out=to[:], in_=pa[:]), reads=[pa], writes=[to])
                        dst = v_s[tok0:tok0 + 128, (blk - 4) * 512:(blk - 3) * 512]
                    else:
                        S.op('act', lambda: nc.scalar.activation(out=to[:], in_=pa[:], func=AF.Silu), reads=[pa], writes=[to])
                        dst = gs_s[tok0:tok0 + 128, (blk - 6) * 512:(blk - 5) * 512]
                    S.dma('pool', dst, to[:], reads=[to], slot=('tmo', tmo.i))
            for tt in range(TS // 128):
                tok0 = t0 + tt * 128
                pa = pacc.next()
                for k in range(16):
                    S.op('pe', lambda: nc.tensor.matmul(pa[:, 0:32], lhsT=hT[:, k, tt * 128:(tt + 1) * 128], rhs=wdt[:, k, :],
                                                        start=(k == 0), stop=(k == 15)),
                         reads=[wdt, hT], writes=[pa], signal=(k == 15))
                d_ = dtr.next()
                S.op('dve', lambda: nc.vector.tensor_tensor(out=d_[:], in0=pa[:, 0:32], in1=cst[:, C_DTB:C_DTB + 32], op=ALU.add),
                     reads=[pa], writes=[d_])
                S.op('act', lambda: nc.scalar.activation(out=d_[:], in_=d_[:], func=AF.Exp), reads=[d_], writes=[d_])
                S.op('act', lambda: nc.scalar.activation(out=d_[:], in_=d_[:], func=AF.Ln, bias=1.0, scale=1.0), reads=[d_], writes=[d_])
                S.dma('pool', dt_s[tok0:tok0 + 128, :], d_[:], reads=[d_], slot=('dtr', dtr.i))
            S.barrier()
    chk('P')

    with ExitStack() as pes:
        xr = ring(pes, 2, [128, 2, 2048], BF16, "sx")
        br = ring(pes, 2, [128, 2, 512], BF16, "sb")
        bTr = ring(pes, 2, [128, 4, 256], BF16, "sbT")
        cTr = ring(pes, 2, [128, 4, 256], BF16, "scT")
        dtr2 = ring(pes, 2, [128, 2, 32], F32, "sdt")
        zr = ring(pes, 2, [128, 2, 2048], BF16, "sz")
        state = sb(pes, [128, 4, 512], F32, "state")
        stbf = sb(pes, [128, 4, 512], BF16, "stbf")
        dtA = sb(pes, [128, 2, 32], F32, "dtA")
        cumT = sb(pes, [128, 2, 32], F32, "cumT")
        cend = sb(pes, [128, 32], F32, "cend")
        ecum = sb(pes, [128, 2, 32], F32, "ecum")
        dend = sb(pes, [128, 2, 32], F32, "dend")
        sdec = sb(pes, [128, 32], F32, "sdec")
        cbm = ring(pes, 2, [128, 384], F32, "cbm")
        lhr = ring(pes, 3, [128, 384], F32, "lh")
        er = ring(pes, 3, [128, 384], BF16, "er")
        wr = ring(pes, 3, [128, 384], BF16, "wr")
        t1r = ring(pes, 4, [128, 512], F32, "t1")
        t2r = ring(pes, 2, [128, 512], F32, "t2")
        smr = ring(pes, 4, [128, 4], F32, "ssm")
        ynr = ring(pes, 2, [128, 512], BF16, "yn")
        yts = ring(pes, 2, [128, 4, 128], BF16, "yts")
        xdr = ring(pes, 2, [128, 2, 512], BF16, "xd")
        p_cum = ps(pes, [128, 512], F32, "pcum")
        p_cb = ring(pes, 1, [128, 512], F32, "pcb", psum=True)
        p_seg = ring(pes, 2, [128, 512], F32, "pseg", psum=True)
        p_yd = ring(pes, 1, [128, 2, 512], F32, "pyd", psum=True)
        p_yo = ring(pes, 1, [128, 512], F32, "pyo", psum=True)
        p_tr = ring(pes, 1, [128, 4, 128], BF16, "ptr2", psum=True)
        junk = sb(pes, [128, 512], BF16, "junk2")
        dsk = sb(pes, [128, 2048], F32, "dsk")
        S.dma('sp', dsk[:], dsk_d[:, :], writes=[dsk], slot='dsk')
        U2 = cst[:, C_U:C_U + 256]
        U1 = cst[:, C_U:C_U + 128]
        L0 = cst[:, C_L0:C_L0 + 128]
        ONES = cst[:, C_ONES:C_ONES + 128]
        S.op('dve', lambda: nc.vector.memset(state[:], 0.0), writes=[(state, 0), (state, 1), (state, 2), (state, 3)])
        S.op('dve', lambda: nc.vector.memset(stbf[:], 0.0), writes=[(stbf, 0), (stbf, 1), (stbf, 2), (stbf, 3)])
        for c in range(NCH):
            tok0 = c * 256
            x2 = xr.next(); b2 = br.next(); bT = bTr.next(); cT = cTr.next(); dt2 = dtr2.next(); z2 = zr.next()
            cbs = {}; pyds = {}; wws = {}; ees = {}; pend = []
            S.dma('sp', x2[:], xtm_s[tok0:tok0 + 256, :].rearrange("(j p) c -> p j c", p=128), writes=[x2], slot=('sx', xr.i))
            S.dma('sp', b2[:], btm_s[tok0:tok0 + 256, :].rearrange("(j p) c -> p j c", p=128), writes=[b2], slot=('sb', br.i))
            S.dma('sp', bT[:], bT_s[:, tok0:tok0 + 256].rearrange("(g p) t -> p g t", p=128), writes=[bT], slot=('sbT', bTr.i))
            S.dma('sp', cT[:], cT_s[:, tok0:tok0 + 256].rearrange("(g p) t -> p g t", p=128), writes=[cT], slot=('scT', cTr.i))
            S.dma('sp', dt2[:], dt_s[tok0:tok0 + 256, :].rearrange("(j p) c -> p j c", p=128), writes=[dt2], slot=('sdt', dtr2.i))
            S.dma('sp', z2[:], zs_s[tok0:tok0 + 256, :].rearrange("(j p) c -> p j c", p=128), writes=[z2], slot=('sz', zr.i))
            S.op('dve', lambda: nc.vector.tensor_tensor(out=dtA[:], in0=dt2[:], in1=A_bc.unsqueeze(1).broadcast_to([128, 2, 32]), op=ALU.mult),
                 reads=[dt2], writes=[dtA])
            S.op('pe', lambda: nc.tensor.matmul(p_cum[:, 0:32], lhsT=U1, rhs=dtA[:, 0, :], start=True, stop=True), reads=[dtA], writes=[p_cum], signal=False)
            S.op('pe', lambda: nc.tensor.matmul(p_cum[:, 32:64], lhsT=ONES, rhs=dtA[:, 0, :], start=True, stop=False), reads=[dtA], writes=[p_cum], signal=False)
            S.op('pe', lambda: nc.tensor.matmul(p_cum[:, 32:64], lhsT=U1, rhs=dtA[:, 1, :], start=False, stop=True), reads=[dtA], writes=[p_cum], signal=False)
            S.op('pe', lambda: nc.tensor.matmul(p_cum[:, 64:96], lhsT=ONES, rhs=dtA[:, 0, :], start=True, stop=False), reads=[dtA], writes=[p_cum], signal=False)
            S.op('pe', lambda: nc.tensor.matmul(p_cum[:, 64:96], lhsT=ONES, rhs=dtA[:, 1, :], start=False, stop=True), reads=[dtA], writes=[p_cum])
            S.op('dve', lambda: nc.vector.tensor_copy(out=cumT[:].rearrange("p j h -> p (j h)"), in_=p_cum[:, 0:64]), reads=[p_cum], writes=[cumT])
            S.op('dve', lambda: nc.vector.tensor_copy(out=cend[:], in_=p_cum[:, 64:96]), reads=[p_cum], writes=[cend])
            S.op('act', lambda: nc.scalar.activation(out=ecum[:], in_=cumT[:], func=AF.Exp), reads=[cumT], writes=[ecum])
            S.op('act', lambda: nc.scalar.activation(out=sdec[:], in_=cend[:], func=AF.Exp), reads=[cend], writes=[sdec])
            S.op('dve', lambda: nc.vector.tensor_tensor(out=dend[:], in0=cend[:].unsqueeze(1).broadcast_to([128, 2, 32]), in1=cumT[:], op=ALU.subtract),
                 reads=[cend, cumT], writes=[dend])
            S.op('act', lambda: nc.scalar.activation(out=dend[:], in_=dend[:], func=AF.Exp), reads=[dend], writes=[dend])
            S.op('dve', lambda: nc.vector.tensor_tensor(out=dend[:], in0=dend[:], in1=dt2[:], op=ALU.mult), reads=[dend, dt2], writes=[dend])
            def prologue(g):
                pcb = p_cb.next()
                S.op('pe', lambda: nc.tensor.matmul(pcb[:, 0:256], lhsT=bT[:, g, 0:128], rhs=cT[:, g, 0:256], start=True, stop=True),
                     reads=[bT, cT], writes=[pcb], signal=False)
                S.op('pe', lambda: nc.tensor.matmul(pcb[:, 256:384], lhsT=bT[:, g, 128:256], rhs=cT[:, g, 128:256], start=True, stop=True),
                     reads=[bT, cT], writes=[pcb])
                cb_ = cbm.next()
                S.op('dve', lambda: nc.vector.tensor_tensor(out=cb_[:, 0:256], in0=pcb[:, 0:256], in1=U2, op=ALU.mult), reads=[pcb], writes=[(cb_, 0)])
                S.op('dve', lambda: nc.vector.tensor_tensor(out=cb_[:, 256:384], in0=pcb[:, 256:384], in1=U1, op=ALU.mult), reads=[pcb], writes=[(cb_, 1)])
                cbs[g] = cb_
                pyds[g] = p_yd.next()

            def stageA(g, e):
                if e == 0:
                    prologue(g)
                cb_ = cbs[g]
                h = g * 8 + e
                lh = lhr.next()
                S.op('dve', lambda: nc.vector.tensor_scalar(out=lh[:, 0:128], in0=L0, scalar1=dtA[:, 0, h:h + 1], scalar2=None, op0=ALU.mult),
                     reads=[dtA], writes=[(lh, 0)])
                S.op('dve', lambda: nc.vector.tensor_scalar(out=lh[:, 128:256], in0=ONES, scalar1=dtA[:, 1, h:h + 1], scalar2=None, op0=ALU.mult),
                     reads=[dtA], writes=[(lh, 1)])
                S.op('dve', lambda: nc.vector.tensor_scalar(out=lh[:, 256:384], in0=L0, scalar1=dtA[:, 1, h:h + 1], scalar2=None, op0=ALU.mult),
                     reads=[dtA], writes=[(lh, 2)])
                pseg = p_seg.next()
                S.op('pe', lambda: nc.tensor.matmul(pseg[:, 0:256], lhsT=lh[:, 0:128], rhs=U2, start=True, stop=False),
                     reads=[(lh, 0)], writes=[pseg], signal=False)
                S.op('pe', lambda: nc.tensor.matmul(pseg[:, 128:256], lhsT=lh[:, 128:256], rhs=U1, start=False, stop=True),
                     reads=[(lh, 1)], writes=[pseg], signal=False)
                S.op('pe', lambda: nc.tensor.matmul(pseg[:, 256:384], lhsT=lh[:, 256:384], rhs=U1, start=True, stop=True),
                     reads=[(lh, 2)], writes=[pseg])
                ee = er.next()
                S.op('act', lambda: nc.scalar.activation(out=ee[:], in_=pseg[:, 0:384], func=AF.Exp), reads=[pseg], writes=[ee])
                ees[(g, e)] = ee

            def stageA2(g, e):
                cb_ = cbs[g]
                h = g * 8 + e
                ee = ees.pop((g, e))
                ww = wr.next()
                S.op('dve', lambda: nc.vector.scalar_tensor_tensor(out=ww[:, 0:256], in0=ee[:, 0:256], scalar=dt2[:, 0, h:h + 1], in1=cb_[:, 0:256],
                                                                    op0=ALU.mult, op1=ALU.mult), reads=[ee, dt2, (cb_, 0)], writes=[(ww, 0)])
                S.op('dve', lambda: nc.vector.scalar_tensor_tensor(out=ww[:, 256:384], in0=ee[:, 256:384], scalar=dt2[:, 1, h:h + 1], in1=cb_[:, 256:384],
                                                                    op0=ALU.mult, op1=ALU.mult), reads=[ee, dt2, (cb_, 1)], writes=[(ww, 1)])
                wws[(g, e)] = ww

            def stageB(g, e):
                h = g * 8 + e
                ww = wws.pop((g, e))
                pyd = pyds[g]
                xs0 = x2[:, 0, h * 64:(h + 1) * 64]
                xs1 = x2[:, 1, h * 64:(h + 1) * 64]
                S.op('pe', lambda: nc.tensor.matmul(pyd[:, 0, e * 64:(e + 1) * 64], lhsT=ww[:, 0:128], rhs=xs0, start=True, stop=True),
                     reads=[(ww, 0), x2], writes=[pyd], signal=False)
                S.op('pe', lambda: nc.tensor.matmul(pyd[:, 1, e * 64:(e + 1) * 64], lhsT=ww[:, 128:256], rhs=xs0, start=True, stop=False),
                     reads=[(ww, 0), x2], writes=[pyd], signal=False)
                S.op('pe', lambda: nc.tensor.matmul(pyd[:, 1, e * 64:(e + 1) * 64], lhsT=ww[:, 256:384], rhs=xs1, start=False, stop=True),
                     reads=[(ww, 1), x2], writes=[pyd], signal=(e == 7))
                if e == 7:
                    epilogue(g)

            def epilogue(g):
                pyd = pyds.pop(g)
                t1s = [t1r.next(), t1r.next()]

                def _c0():
                    for lt in range(2):
                        S.op('act', lambda: nc.scalar.copy(out=t1s[lt][:], in_=pyd[:, lt, :]), reads=[pyd], writes=[t1s[lt]])
                _c0()
                for lt in range(2):
                    t1 = t1s[lt]
                    t2 = t2r.next()
                    sm = smr.next()
                    yn = ynr.next()

                    def _c1(lt=lt, t1=t1, t2=t2):
                        pyo = p_yo.next()
                        S.op('pe', lambda: nc.tensor.matmul(pyo[:], lhsT=cT[:, g, lt * 128:(lt + 1) * 128], rhs=stbf[:, g, :], start=True, stop=True),
                             reads=[cT, (stbf, g)], writes=[pyo])
                        S.op('dve', lambda: nc.vector.tensor_tensor(out=t2[:].rearrange("p (e q) -> p e q", e=8), in0=pyo[:].rearrange("p (e q) -> p e q", e=8),
                                                                     in1=ecum[:, lt, g * 8:(g + 1) * 8].unsqueeze(2).broadcast_to([128, 8, 64]), op=ALU.mult),
                             reads=[pyo, ecum], writes=[t2])
                        S.op('dve', lambda: nc.vector.tensor_tensor(out=t1[:], in0=t1[:], in1=t2[:], op=ALU.add), reads=[t1, t2], writes=[t1])

                    def _c2(lt=lt, t1=t1, t2=t2, sm=sm):
                        S.op('dve', lambda: nc.vector.tensor_tensor(out=t2[:], in0=x2[:, lt, g * 512:(g + 1) * 512],
                                                                     in1=dsk[:, g * 512:(g + 1) * 512], op=ALU.mult), reads=[x2, dsk, t1], writes=[t2])
                        S.op('dve', lambda: nc.vector.tensor_tensor(out=t1[:], in0=t1[:], in1=t2[:], op=ALU.add), reads=[t1, t2], writes=[t1])
                        S.op('dve', lambda: nc.vector.tensor_tensor(out=t1[:], in0=t1[:], in1=z2[:, lt, g * 512:(g + 1) * 512], op=ALU.mult),
                             reads=[t1, z2], writes=[t1])
                        S.op('act', lambda: nc.scalar.activation(out=junk[:], in_=t1[:], func=AF.Square, accum_out=sm[:, 0:1]), reads=[t1], writes=[junk, sm])

                    def _c3(sm=sm):
                        S.op('dve', lambda: nc.vector.tensor_scalar(out=sm[:, 1:2], in0=sm[:, 0:1], scalar1=1.0 / 512, scalar2=EPS, op0=ALU.mult, op1=ALU.add),
                             reads=[sm], writes=[sm])
                        S.op('act', lambda: nc.scalar.activation(out=sm[:, 2:3], in_=sm[:, 1:2], func=AF.Sqrt), reads=[sm], writes=[sm])

                    def _c4(sm=sm, t1=t1, yn=yn):
                        S.op('dve', lambda: nc.vector.reciprocal(out=sm[:, 3:4], in_=sm[:, 2:3]), reads=[sm], writes=[sm])
                        S.op('act', lambda: nc.scalar.activation(out=yn[:], in_=t1[:], func=AF.Copy, scale=sm[:, 3:4]), reads=[t1, sm], writes=[yn])

                    def _tr(yn=yn, lt=lt):
                        ptr_ = p_tr.next()
                        for j in range(4):
                            S.op('pe', lambda: nc.tensor.transpose(out=ptr_[:, j, :], in_=yn[:, j * 128:(j + 1) * 128], identity=identb),
                                 reads=[yn], writes=[ptr_], signal=(j == 3))
                        yt = yts.next()
                        S.op('dve', lambda: nc.vector.tensor_tensor(out=yt[:], in0=ptr_[:], in1=cst[:, C_GNW + g * 4:C_GNW + g * 4 + 4].unsqueeze(2).broadcast_to([128, 4, 128]),
                                                                     op=ALU.mult), reads=[ptr_], writes=[yt])
                        S.dma('pool', ynT_s[g * 4:(g + 1) * 4, :, tok0 + lt * 128:tok0 + (lt + 1) * 128].rearrange("j p t -> p j t"), yt[:], reads=[yt], slot=('yts', yts.i))
                    pend.extend([_c1, _c2, _c3, _c4, _tr])
                xd = xdr.next()

                def _st(xd=xd):
                    S.op('dve', lambda: nc.vector.tensor_tensor(out=xd[:].rearrange("p j (e q) -> p j e q", e=8),
                                                                 in0=x2[:, :, g * 512:(g + 1) * 512].rearrange("p j (e q) -> p j e q", e=8),
                                                                 in1=dend[:, :, g * 8:(g + 1) * 8].unsqueeze(3).broadcast_to([128, 2, 8, 64]), op=ALU.mult),
                         reads=[x2, dend], writes=[xd])
                    pst = p_yo.next()
                    S.op('pe', lambda: nc.tensor.matmul(pst[:], lhsT=b2[:, 0, g * 128:(g + 1) * 128], rhs=xd[:, 0, :], start=True, stop=False), reads=[b2, xd], writes=[pst], signal=False)
                    S.op('pe', lambda: nc.tensor.matmul(pst[:], lhsT=b2[:, 1, g * 128:(g + 1) * 128], rhs=xd[:, 1, :], start=False, stop=True), reads=[b2, xd], writes=[pst])
                    S.op('dve', lambda: nc.vector.tensor_tensor(out=state[:, g, :].rearrange("p (e q) -> p e q", e=8), in0=state[:, g, :].rearrange("p (e q) -> p e q", e=8),
                                                                 in1=sdec[:, g * 8:(g + 1) * 8].unsqueeze(2).broadcast_to([128, 8, 64]), op=ALU.mult),
                         reads=[(state, g), sdec], writes=[(state, g)])
                    S.op('dve', lambda: nc.vector.tensor_tensor(out=state[:, g, :], in0=state[:, g, :], in1=pst[:], op=ALU.add), reads=[(state, g), pst], writes=[(state, g)])
                    S.op('act', lambda: nc.scalar.copy(out=stbf[:, g, :], in_=state[:, g, :]), reads=[(state, g)], writes=[(stbf, g)])
                pend.append(_st)

            hitems = [(g, e) for g in range(4) for e in range(8)]
            NH = len(hitems)
            for n in range(NH + 2):
                if n < NH:
                    stageA(*hitems[n])
                for _ in range(2):
                    if pend:
                        pend.pop(0)()
                if 0 <= n - 1 < NH:
                    stageA2(*hitems[n - 1])
                if n - 2 >= 0:
                    stageB(*hitems[n - 2])
            while pend:
                pend.pop(0)()
        S.barrier()
    chk('S')

    def gather(src, dst):
        nc.gpsimd.collective_compute("AllGather", ALU.bypass, replica_groups=pair_groups,
                                     ins=[src.opt()], outs=[dst.opt()]).then_inc(cc_sem)
        n_cc[0] += 1

    for j in range(16):
        for tq in range(0, T, 8192):
            gather(ynT_s[j], ynT_g[j])

    with ExitStack() as pes:
        qTr = ring(pes, 2, [128, T], BF16, "mq")
        kTr = ring(pes, 2, [128, T], BF16, "mk")
        var = ring(pes, 2, [128, NT, 130], BF16, "mv")
        gsr = ring(pes, 2, [128, NT, 128], BF16, "mg")
        kmf = sb(pes, [128, 32], F32, "kmf")
        kmb = sb(pes, [128, 32], BF16, "kmb")
        gbuf = ring(pes, 2, [128, 32], F32, "gbuf")
        t8r = ring(pes, 2, [128, 8], F32, "t8")
        selr = ring(pes, 6, [128, 32], F32, "sel")
        ptr_ = ring(pes, 4, [128, 512], BF16, "mp")
        oacc = ring(pes, 4, [128, 132], F32, "oacc")
        rdr = ring(pes, 4, [128, 2], F32, "rd")
        ogr = ring(pes, 2, [128, 128], BF16, "og")
        ogT = ring(pes, 2, [128, 256], BF16, "ogT")
        p_g = ring(pes, 1, [128, 512], F32, "pg", psum=True)
        p_s = ring(pes, 4, [128, 512], F32, "pS", psum=True)
        p_o = ring(pes, 2, [128, 2, 256], F32, "pO", psum=True)
        p_t = ring(pes, 1, [128, 2, 128], BF16, "pT", psum=True)
        scale = 128.0 ** -0.5
        NBLK = T // 256
        LA = 3
        cmask = cstb[:, B_CM0:B_CM0 + 512]
        for hd in range(8):
            qT = qTr.next(); kT = kTr.next(); va = var.next(); gs = gsr.next()
            S.dma('sp', qT[:], qT_s[hd * 128:(hd + 1) * 128, :], writes=[qT], slot=('mq', qTr.i))
            S.dma('sp', kT[:], kT_s[hd * 128:(hd + 1) * 128, :], writes=[kT], slot=('mk', kTr.i))
            S.dma('sp', va[:, :, 0:128], v_s[:, hd * 128:(hd + 1) * 128].rearrange("(j p) c -> p j c", p=128), writes=[(va, 'v')], slot=('mv', var.i))
            S.op('dve', lambda: nc.vector.memset(va[:, :, 128:130], 1.0), writes=[(va, 'o')])
            S.dma('sp', gs[:], gs_s[:, hd * 128:(hd + 1) * 128].rearrange("(j p) c -> p j c", p=128), writes=[gs], slot=('mg', gsr.i))
            S.op('dve', lambda: nc.vector.tensor_reduce(out=kmf[:, 0:NBLK], in_=kT[:].rearrange("p (n t) -> p n t", t=256), op=ALU.add, axis=AX.X),
                 reads=[kT], writes=[kmf])
            S.op('act', lambda: nc.scalar.activation(out=kmb[:, 0:NBLK], in_=kmf[:, 0:NBLK], func=AF.Copy, scale=1.0 / 256), reads=[kmf], writes=[kmb])
            gb = [gbuf.next(), gbuf.next()]
            for qt in range(2):
                S.op('dve', lambda: nc.vector.memset(gb[qt][:], NEGBIG), writes=[gb[qt]])
            items = [(i, j) for i in range(NBLK) for j in [i] + list(range(i))]
            selm = {}
            pSm = {}
            oam = {}

            def emit_qk(n):
                i, j = items[n]
                if j == i and i >= 1:
                    pg = p_g.next()
                    for qt in range(2):
                        q0 = i * 256 + qt * 128
                        S.op('pe', lambda: nc.tensor.matmul(pg[:, qt * 32:qt * 32 + NBLK], lhsT=qT[:, q0:q0 + 128], rhs=kmb[:, 0:NBLK], start=True, stop=True),
                             reads=[qT, kmb], writes=[pg], signal=(qt == 1))
                    sl = []
                    for qt in range(2):
                        S.op('dve', lambda: nc.vector.tensor_copy(out=gb[qt][:, 0:i], in_=pg[:, qt * 32:qt * 32 + i]), reads=[pg], writes=[gb[qt]])
                        t8 = t8r.next()
                        S.op('dve', lambda: nc.vector.max(out=t8[:], in_=gb[qt][:]), reads=[gb[qt]], writes=[t8])
                        sel = selr.next()
                        S.op('dve', lambda: nc.vector.tensor_scalar(out=sel[:], in0=gb[qt][:], scalar1=t8[:, 2:3], scalar2=None, op0=ALU.is_ge),
                             reads=[gb[qt], t8], writes=[sel])
                        sl.append(sel)
                    selm[i] = sl
                pS = p_s.next()
                for kt in range(2):
                    k0 = j * 256 + kt * 128
                    S.op('pe', lambda: nc.tensor.matmul(pS[:, kt * 256:(kt + 1) * 256], lhsT=kT[:, k0:k0 + 128], rhs=qT[:, i * 256:(i + 1) * 256], start=True, stop=True),
                         reads=[kT, qT], writes=[pS], signal=(kt == 1))
                pSm[n] = pS

            def emit_rest(n):
                i, j = items[n]
                pS = pSm.pop(n)
                if j == i:
                    oam[i] = [oacc.next(), oacc.next()]
                oa = oam[i]
                pT = ptr_.next()
                S.op('act', lambda: nc.scalar.activation(out=pT[:], in_=pS[:], func=AF.Exp, scale=scale), reads=[pS], writes=[pT])
                if j == i:
                    S.op('dve', lambda: nc.vector.tensor_tensor(out=pT[:], in0=pT[:], in1=cmask, op=ALU.mult), reads=[pT], writes=[pT])
                pO = p_o.next()
                for qt in range(2):
                    for kt in range(2):
                        S.op('pe', lambda: nc.tensor.matmul(pO[:, qt, 0:130], lhsT=pT[:, kt * 256 + qt * 128:kt * 256 + (qt + 1) * 128], rhs=va[:, j * 2 + kt, :],
                                                            start=(kt == 0), stop=(kt == 1)),
                             reads=[pT, (va, 'v'), (va, 'o')], writes=[pO], signal=(qt == 1 and kt == 1))
                for qt in range(2):
                    if j == i:
                        S.op('dve', lambda: nc.vector.tensor_copy(out=oa[qt][:, 0:130], in_=pO[:, qt, 0:130]), reads=[pO], writes=[oa[qt]])
                    else:
                        S.op('dve', lambda: nc.vector.scalar_tensor_tensor(out=oa[qt][:, 0:130], in0=pO[:, qt, 0:130], scalar=selm[i][qt][:, j:j + 1],
                                                                            in1=oa[qt][:, 0:130], op0=ALU.mult, op1=ALU.add),
                             reads=[pO, selm[i][qt], oa[qt]], writes=[oa[qt]])
                if (j == i - 1) or (i == 0):
                    pt_ = p_t.next()
                    for qt in range(2):
                        rd = rdr.next()
                        S.op('dve', lambda: nc.vector.reciprocal(out=rd[:, 0:1], in_=oa[qt][:, 128:129]), reads=[oa[qt]], writes=[rd])
                        og = ogr.next()
                        S.op('dve', lambda: nc.vector.scalar_tensor_tensor(out=og[:], in0=oa[qt][:, 0:128], scalar=rd[:, 0:1], in1=gs[:, i * 2 + qt, :],
                                                                            op0=ALU.mult, op1=ALU.mult), reads=[oa[qt], rd, gs], writes=[og])
                        S.op('pe', lambda: nc.tensor.transpose(out=pt_[:, qt, :], in_=og[:], identity=identb), reads=[og],
                             writes=[pt_], signal=(qt == 1))
                    ot = ogT.next()
                    S.op('act', lambda: nc.scalar.copy(out=ot[:], in_=pt_[:].rearrange("p a b -> p (a b)")), reads=[pt_], writes=[ot])
                    S.dma('pool', ogT_s[hd][:, i * 256:(i + 1) * 256], ot[:], reads=[ot], slot=('ogT', ogT.i))
                    oam.pop(i)
                    selm.pop(i, None)

            NI = len(items)
            for n in range(NI + LA):
                if n < NI:
                    emit_qk(n)
                if n - LA >= 0:
                    emit_rest(n - LA)
            S._wait('pool', [(k, v) for k, v in S.val.items() if not isinstance(k, str) and k[1] in (('ogT', 0), ('ogT', 1))])
            gather(ogT_s[hd], ogT_g[hd])
        S.barrier()
    nc.sync.wait_ge(cc_sem, n_cc[0])
    nc.gpsimd.wait_ge(cc_sem, n_cc[0])
    chk('M')

    with ExitStack() as pes:
        TB = 256
        hTf = sb(pes, [128, 16, TB], BF16, "fh")
        ynf = sb(pes, [128, 32, TB], BF16, "fy")
        ogf = sb(pes, [128, 16, TB], BF16, "fo")
        mixT = sb(pes, [128, 16, TB], BF16, "fm")
        stgr = ring(pes, 2, [128, 16, TB], BF16, "fst")
        wr1 = ring(pes, 6, [128, 16, 128], BF16, "fw1")
        wor = ring(pes, 2, [128, 16, 512], BF16, "fwo")
        g1r = ring(pes, 2, [128, TB], F32, "fg1")
        g2r = ring(pes, 2, [128, TB], F32, "fg2")
        xr = ring(pes, 2, [128, 2048], F32, "fx")
        xor_ = ring(pes, 2, [128, 2048], F32, "fxo")
        junk = sb(pes, [128, 2048], BF16, "fj")
        fnw = sb(pes, [128, 2048], F32, "fnw")
        smr = ring(pes, 4, [128, 4], F32, "fsm")
        psr = ring(pes, 6, [128, 512], F32, "fps", psum=True)
        pdr = ring(pes, 2, [128, 512], F32, "fpd", psum=True)
        S.dma('sp', fnw[:], fnw_d[:, :], writes=[fnw], slot='fnw')
        s_lo = cst[:, C_BL:C_BL + 1]
        s_hi = cst[:, C_BL + 1:C_BL + 2]

        def blend_load(dst, n, lo_src, hi_src, key):
            S.dma('sp', dst, lo_src, writes=[key], slot=key)
            st = stgr.next()
            S.dma('sp', st[:, :n, :], hi_src, writes=[st], slot=('fst', stgr.i))
            S.op('dve', lambda: nc.vector.tensor_scalar(out=dst, in0=dst, scalar1=s_lo, scalar2=None, op0=ALU.mult), reads=[key], writes=[key])
            S.op('dve', lambda: nc.vector.scalar_tensor_tensor(out=dst, in0=st[:, :n, :], scalar=s_hi, in1=dst, op0=ALU.mult, op1=ALU.add),
                 reads=[st, key], writes=[key])

        for bi in range(TH // TB):
            lo = bi * TB
            hi = TH + bi * TB
            blend_load(hTf[:], 16, hT_s[:, :, lo:lo + TB].rearrange("k p t -> p k t"), hT_s[:, :, hi:hi + TB].rearrange("k p t -> p k t"), ('fh', 0))
            for r in range(2):
                blend_load(ynf[:, r * 16:(r + 1) * 16, :], 16, ynT_g[:, r * 128:(r + 1) * 128, lo:lo + TB].rearrange("j p t -> p j t"),
                           ynT_g[:, r * 128:(r + 1) * 128, hi:hi + TB].rearrange("j p t -> p j t"), ('fy', r))
            for r in range(2):
                blend_load(ogf[:, r * 8:(r + 1) * 8, :], 8, ogT_g[:, r * 128:(r + 1) * 128, lo:lo + TB].rearrange("j p t -> p j t"),
                           ogT_g[:, r * 128:(r + 1) * 128, hi:hi + TB].rearrange("j p t -> p j t"), ('fo', r))
            for mc in range(16):
                srcs = [wgm_s[mc], wgm_s[16 + mc], wssm_s[mc][:, 0:2048], wssm_s[mc][:, 2048:4096], wattn_s[mc]]
                wch = []
                for s_ in srcs:
                    wt_ = wr1.next()
                    S.dma('sp', wt_[:], s_.rearrange("p (k c) -> p k c", k=16), writes=[wt_], slot=('fw1', wr1.i))
                    wch.append(wt_)
                pg1 = psr.next(); pg2 = psr.next(); pys = psr.next(); pya = psr.next()
                for k in range(16):
                    S.op('pe', lambda: nc.tensor.matmul(pg1[:, :TB], lhsT=wch[0][:, k, :], rhs=hTf[:, k, :], start=(k == 0), stop=(k == 15)),
                         reads=[wch[0], ('fh', 0)], writes=[pg1], signal=(k == 15))
                for k in range(16):
                    S.op('pe', lambda: nc.tensor.matmul(pg2[:, :TB], lhsT=wch[1][:, k, :], rhs=hTf[:, k, :], start=(k == 0), stop=(k == 15)),
                         reads=[wch[1], ('fh', 0)], writes=[pg2], signal=(k == 15))
                for k in range(32):
                    S.op('pe', lambda: nc.tensor.matmul(pys[:, :TB], lhsT=wch[2 + k // 16][:, k % 16, :], rhs=ynf[:, k, :], start=(k == 0), stop=(k == 31)),
                         reads=[wch[2 + k // 16], ('fy', k // 16)], writes=[pys], signal=(k == 31))
                for k in range(16):
                    S.op('pe', lambda: nc.tensor.matmul(pya[:, :TB], lhsT=wch[4][:, k, :], rhs=ogf[:, k, :], start=(k == 0), stop=(k == 15)),
                         reads=[wch[4], ('fo', k // 8)], writes=[pya], signal=(k == 15))
                g1 = g1r.next(); g2 = g2r.next()
                S.op('act', lambda: nc.scalar.activation(out=g1[:], in_=pg1[:, :TB], func=AF.Sigmoid, bias=cst[:, C_GB + mc:C_GB + mc + 1], scale=1.0),
                     reads=[pg1], writes=[g1])
                S.op('act', lambda: nc.scalar.activation(out=g2[:], in_=pg2[:, :TB], func=AF.Sigmoid, bias=cst[:, C_GB + 16 + mc:C_GB + 17 + mc], scale=1.0),
                     reads=[pg2], writes=[g2])
                S.op('dve', lambda: nc.vector.tensor_tensor(out=g1[:], in0=g1[:], in1=pys[:, :TB], op=ALU.mult), reads=[g1, pys], writes=[g1])
                S.op('dve', lambda: nc.vector.tensor_tensor(out=g2[:], in0=g2[:], in1=pya[:, :TB], op=ALU.mult), reads=[g2, pya], writes=[g2])
                S.op('dve', lambda: nc.vector.tensor_tensor(out=mixT[:, mc, :], in0=g1[:], in1=g2[:], op=ALU.add), reads=[g1, g2], writes=[mixT])
            xts = []
            xos = []
            for tt in range(TB // 128):
                xt = xr.next()
                S.dma('sp', xt[:], xf_d[lo + tt * 128:lo + (tt + 1) * 128, :], writes=[xt], slot=('fx', xr.i))
                xts.append(xt)
                xos.append(xor_.next())
            for dblk in range(4):
                wo = wor.next()
                S.dma('sp', wo[:], wout_s[dblk].rearrange("p (k c) -> p k c", k=16), writes=[wo], slot=('fwo', wor.i))
                for tt in range(TB // 128):
                    pd = pdr.next()
                    for k in range(16):
                        S.op('pe', lambda: nc.tensor.matmul(pd[:], lhsT=mixT[:, k, tt * 128:(tt + 1) * 128], rhs=wo[:, k, :], start=(k == 0), stop=(k == 15)),
                             reads=[wo, mixT], writes=[pd], signal=(k == 15))
                    S.op('dve', lambda: nc.vector.tensor_tensor(out=xos[tt][:, dblk * 512:(dblk + 1) * 512], in0=pd[:], in1=xts[tt][:, dblk * 512:(dblk + 1) * 512], op=ALU.add),
                         reads=[pd, xts[tt]], writes=[(xos[tt], dblk)])
            for tt in range(TB // 128):
                sm = smr.next()
                xo = xos[tt]
                S.op('act', lambda: nc.scalar.activation(out=junk[:], in_=xo[:], func=AF.Square, accum_out=sm[:, 0:1]),
                     reads=[(xo, q) for q in range(4)], writes=[junk, sm])
                S.op('dve', lambda: nc.vector.tensor_scalar(out=sm[:, 1:2], in0=sm[:, 0:1], scalar1=1.0 / D, scalar2=EPS, op0=ALU.mult, op1=ALU.add),
                     reads=[sm], writes=[sm])
                S.op('act', lambda: nc.scalar.activation(out=sm[:, 2:3], in_=sm[:, 1:2], func=AF.Sqrt), reads=[sm], writes=[sm])
                S.op('dve', lambda: nc.vector.reciprocal(out=sm[:, 3:4], in_=sm[:, 2:3]), reads=[sm], writes=[sm])
                S.op('dve', lambda: nc.vector.scalar_tensor_tensor(out=xts[tt][:], in0=xo[:], scalar=sm[:, 3:4], in1=fnw[:], op0=ALU.mult, op1=ALU.mult),
                     reads=[(xo, q) for q in range(4)] + [sm, fnw], writes=[xts[tt]])
                S.dma('pool', out_d[lo + tt * 128:lo + (tt + 1) * 128, :], xts[tt][:], reads=[xts[tt]], slot=('fxs', xr.t.index(xts[tt])))
        S.barrier()
    es.close()
    return nc


_OFFS = np.cumsum([0, 4096, 6144, 64, 2048, 2048, 2048, 2048, 4096])


def _chunk(w, kc, ncol_blk):
    nblk = w.shape[1] // ncol_blk
    return np.ascontiguousarray(w.reshape(kc, 128, nblk, ncol_blk).transpose(2, 1, 0, 3)).reshape(nblk, 128, kc * ncol_blk)


def _shared_consts():
    t = np.arange(128)
    cst = np.zeros((128, 1024), np.float32)
    U = (t[:, None] <= t[None, :]).astype(np.float32)
    cst[:, 256:384] = U
    cst[:, 384:512] = 1.0
    cst[:, 512:640] = (t[:, None] > t[None, :]).astype(np.float32)
    cst[:, 640:768] = 1.0
    cst[:, 768:896] = np.eye(128, dtype=np.float32)
    cstb = np.zeros((128, 1024), np.float32)
    cstb[:, 0:128] = np.eye(128)
    col = np.arange(256)
    cstb[:, 128:384] = (t[:, None] <= col[None, :])
    cstb[:, 384:640] = ((128 + t[:, None]) <= col[None, :])
    return cst, cstb.astype(ml_dtypes.bfloat16)


def _prep_core(inp, b, hh, T, cst0, cstb):
    f = np.float32
    x = np.ascontiguousarray(inp["x"][b, :T]).astype(f, copy=False)
    w_in = inp["w_in"][0]
    z0, xbc0, dt0, q0, k0, v0, g0, gm0 = [int(v) for v in _OFFS[:8]]

    def cols(a, n):
        return w_in[:, a:a + n]

    Wz = cols(z0 + hh * 2048, 2048)
    Wx = cols(xbc0 + hh * 2048, 2048)
    WB = cols(xbc0 + 4096 + hh * 512, 512)
    WC = cols(xbc0 + 5120 + hh * 512, 512)
    Wdt = cols(dt0 + hh * 32, 32)
    Wq = cols(q0 + hh * 1024, 1024)
    Wk = cols(k0 + hh * 1024, 1024)
    Wv = cols(v0 + hh * 1024, 1024)
    Wg = cols(g0 + hh * 1024, 1024)
    Wgm = cols(gm0, 4096)
    wfm = _chunk(np.concatenate([Wx, WB, WC, Wq, Wk], 1), 16, 128)
    wtm = _chunk(np.concatenate([Wz, Wv, Wg], 1), 16, 512)
    wdt = _chunk(Wdt, 16, 32)[0]
    wgm = _chunk(Wgm, 16, 128)
    wssm = _chunk(inp["w_ssm_proj"][0], 32, 128)
    wattn = _chunk(inp["w_attn_proj"][0], 16, 128)
    wout = _chunk(inp["w_out"][0], 16, 512)
    cst = cst0.copy()
    cst[:, 0:16] = inp["norm_w"][0].reshape(16, 128).T
    conv_w = inp["conv_w"][0]
    conv_b = inp["conv_b"][0]
    chans = np.concatenate([hh * 2048 + np.arange(2048), 4096 + hh * 512 + np.arange(512), 5120 + hh * 512 + np.arange(512)])
    cw = conv_w[:, chans]
    cst[:, 16:112] = cw.reshape(4, 24, 128).transpose(2, 1, 0).reshape(128, 96)
    cst[:, 112:136] = conv_b[chans].reshape(24, 128).T
    cst[:, 136:168] = np.broadcast_to(inp["dt_bias"][0][hh * 32:(hh + 1) * 32], (128, 32))
    cst[:, 168:200] = np.broadcast_to(inp["A_log"][0][hh * 32:(hh + 1) * 32], (128, 32))
    cst[:, 200:216] = inp["ssm_norm_w"][0][hh * 2048:(hh + 1) * 2048].reshape(16, 128).T
    cst[:, 216:248] = inp["gate_bias"][0].reshape(32, 128).T
    cst[:, 248] = 1.0 - hh
    cst[:, 249] = float(hh)
    dsk = np.ascontiguousarray(np.broadcast_to(np.repeat(inp["D_skip"][0][hh * 32:(hh + 1) * 32], 64), (128, 2048))).astype(f)
    fnw = np.ascontiguousarray(np.broadcast_to(inp["final_norm_w"], (128, 2048))).astype(f)
    TH = T // 2
    return {"x": x, "wfm": wfm, "wtm": wtm, "wdt": np.ascontiguousarray(wdt), "wgm": wgm, "wssm": wssm, "wattn": wattn,
            "wout": wout, "cst": cst, "dsk": dsk, "fnw": fnw, "xf": np.ascontiguousarray(x[hh * TH:(hh + 1) * TH]), "cstb": cstb}


def run(inputs, T, nb, stop=None, debug=False):
    ncores = 2 * nb
    groups = [[2 * i, 2 * i + 1] for i in range(nb)]
    nc = build(T, groups, stop, debug)
    cst0, cstb = _shared_consts()
    in_maps = [_prep_core(inputs, c // 2, c % 2, T, cst0, cstb) for c in range(ncores)]
    res = run_bass_kernel_spmd(nc, in_maps, core_ids=list(range(ncores)))
    if debug:
        return res.results, in_maps
    out = np.empty((nb, T, D), np.float32)
    TH = T // 2
    for c in range(ncores):
        out[c // 2, (c % 2) * TH:(c % 2 + 1) * TH] = res.results[c]["out"]
    return out


def kernel(**inputs):
    inputs = {k: np.asarray(v) for k, v in inputs.items()}
    return run(inputs, 8192, 4)
```

```python
import numpy as np
from contextlib import ExitStack
import concourse.bass as bass
import concourse.mybir as mybir
from concourse.bass_utils import run_bass_kernel_spmd
import ml_dtypes

F32 = mybir.dt.float32
BF16 = mybir.dt.bfloat16
AF = mybir.ActivationFunctionType
ALU = mybir.AluOpType
AX = mybir.AxisListType

D = 2048
EPS = 1e-6
NEGBIG = -1.0e30


class Sched:
    def __init__(self, nc, es):
        self.nc = nc
        self.es = es
        self.E = {'pe': nc.tensor, 'act': nc.scalar, 'dve': nc.vector, 'pool': nc.gpsimd, 'sp': nc.sync}
        self.sems = {}
        self.val = {}
        for k in ('pe', 'act', 'dve', 'pool'):
            self.sems[k] = es.enter_context(nc.semaphore('c_' + k))
            self.val[k] = 0
        self.seen = {e: {} for e in self.E}
        self.w = {}
        self.r = {}
        self.nsem = 4

    def _wait(self, e, toks):
        need = {}
        for (k, v) in toks:
            if not isinstance(k, str):
                v = self.val[k]
            if need.get(k, 0) < v:
                need[k] = v
        for k, v in need.items():
            if k == e and e == 'pe':
                continue
            if self.seen[e].get(k, 0) >= v:
                continue
            self.E[e].wait_ge(self.sems[k], v)
            self.seen[e][k] = v

    def _key(self, r):
        if isinstance(r, tuple):
            return tuple(self._key(x) for x in r)
        if isinstance(r, (str, int)):
            return r
        return ('id', id(r))

    def _deps(self, e, reads, writes):
        toks = []
        for r in reads:
            toks += self.w.get(r, [])
        for w_ in writes:
            toks += self.w.get(w_, [])
            toks += self.r.get(w_, [])
        return toks

    def op(self, e, fn, reads=(), writes=(), signal=True):
        reads = [self._key(r) for r in reads]
        writes = [self._key(r) for r in writes]
        self._wait(e, self._deps(e, reads, writes))
        inst = fn()
        if signal:
            self.val[e] += 1
            inst.then_inc(self.sems[e], 1)
            tok = (e, self.val[e])
            for w_ in writes:
                self.w[w_] = [tok]
                self.r[w_] = []
        else:
            tok = (e, self.val[e] + 1)
            for w_ in writes:
                self.w[w_] = [tok]
                self.r[w_] = []
        for r in reads:
            self.r.setdefault(r, []).append(tok)
        return tok

    def dma(self, q, out, in_, reads=(), writes=(), slot=None):
        reads = [self._key(r) for r in reads]
        writes = [self._key(r) for r in writes]
        self._wait(q, self._deps(q, reads, writes))
        key = ('d', slot)
        if key not in self.sems:
            self.sems[key] = self.es.enter_context(self.nc.semaphore('d%d' % self.nsem))
            self.nsem += 1
            self.val[key] = 0
        self.val[key] += 16
        self.E[q].dma_start(out=out, in_=in_).then_inc(self.sems[key], 16)
        tok = (key, self.val[key])
        for w_ in writes:
            self.w[w_] = [tok]
            self.r[w_] = []
        for r in reads:
            self.r.setdefault(r, []).append(tok)
        return tok

    def barrier(self):
        for e in ('pe', 'act', 'dve', 'pool', 'sp'):
            for k, v in self.val.items():
                if v > 0 and k != e and self.seen[e].get(k, 0) < v:
                    self.E[e].wait_ge(self.sems[k], v)
                    self.seen[e][k] = v
        self.w.clear()
        self.r.clear()


class Ring:
    def __init__(self, tiles):
        self.t = tiles
        self.i = -1

    def next(self):
        self.i = (self.i + 1) % len(self.t)
        return self.t[self.i]


class _Stop(Exception):
    pass


def build(T, pair_groups, stop=None, debug=False):
    holder = {}
    try:
        _build(T, pair_groups, stop, holder, debug)
    except _Stop:
        holder['es'].close()
    return holder['nc']


def _build(T, pair_groups, stop, holder, debug):
    assert T % 512 == 0
    NT = T // 128
    NCH = T // 256
    TS = min(T, 2048)
    NSB = T // TS
    TH = T // 2
    nc = bass.Bass("TRN2", target_bir_lowering=False)

    def din(name, shape, dt=F32):
        return nc.dram_tensor(name, shape, dt, kind="ExternalInput")

    x_d = din("x", [T, D])
    wfm_d = din("wfm", [40, 128, 2048])
    wtm_d = din("wtm", [8, 128, 8192])
    wdt_d = din("wdt", [128, 512])
    wgm_d = din("wgm", [32, 128, 2048])
    wssm_d = din("wssm", [16, 128, 4096])
    wattn_d = din("wattn", [16, 128, 2048])
    wout_d = din("wout", [4, 128, 8192])
    cst_d = din("cst", [128, 1024])
    dsk_d = din("dsk", [128, 2048])
    fnw_d = din("fnw", [128, 2048])
    xf_d = din("xf", [T // 2, D])
    cstb_d = din("cstb", [128, 1024], BF16)
    out_d = nc.dram_tensor("out", [TH, D], F32, kind="ExternalOutput")

    def scr(name, shape, dt=BF16):
        if debug and not name.startswith(("ynT", "ogT")):
            return nc.dram_tensor(name, shape, dt, kind="ExternalOutput")
        return nc.dram_tensor(name, shape, dt)

    hT_s = scr("hT_s", [16, 128, T])
    wfm_s = scr("wfm_s", [40, 128, 2048])
    wtm_s = scr("wtm_s", [8, 128, 8192])
    wdt_s = scr("wdt_s", [128, 512])
    wgm_s = scr("wgm_s", [32, 128, 2048])
    wssm_s = scr("wssm_s", [16, 128, 4096])
    wattn_s = scr("wattn_s", [16, 128, 2048])
    wout_s = scr("wout_s", [4, 128, 8192])
    xtm_s = scr("xtm_s", [T, 2048])
    btm_s = scr("btm_s", [T, 512])
    bT_s = scr("bT_s", [512, T])
    cT_s = scr("cT_s", [512, T])
    qT_s = scr("qT_s", [1024, T])
    kT_s = scr("kT_s", [1024, T])
    zs_s = scr("zs_s", [T, 2048])
    v_s = scr("v_s", [T, 1024])
    gs_s = scr("gs_s", [T, 1024])
    dt_s = scr("dt_s", [T, 32], F32)
    ynT_s = scr("ynT_s", [16, 128, T])
    ynT_g = scr("ynT_g", [16, 256, T])
    ogT_s = scr("ogT_s", [8, 128, T])
    ogT_g = scr("ogT_g", [8, 256, T])

    es = ExitStack()
    holder['nc'] = nc
    holder['es'] = es
    S = Sched(nc, es)

    def chk(name):
        if stop == name:
            S.barrier()
            raise _Stop()
    cc_sem = es.enter_context(nc.semaphore("cc"))
    n_cc = [0]

    uid = [0]

    def sb(es_, shape, dt, name=None):
        uid[0] += 1
        return es_.enter_context(nc.sbuf_tensor("%s%d" % (name or "t", uid[0]), shape, dt))

    def ps(es_, shape, dt, name=None):
        uid[0] += 1
        return es_.enter_context(nc.psum_tensor("%s%d" % (name or "p", uid[0]), shape, dt))

    def ring(es_, n, shape, dt, name=None, psum=False):
        return Ring([(ps if psum else sb)(es_, shape, dt, name) for _ in range(n)])

    cst = sb(es, [128, 1024], F32, "cst")
    cstb = sb(es, [128, 1024], BF16, "cstb")
    S.dma('sp', cst[:], cst_d[:, :], writes=[cst], slot='cst')
    S.dma('sp', cstb[:], cstb_d[:, :], writes=[cstb], slot='cstb')
    C_NORMW = 0
    C_CW = 16
    C_CB = 112
    C_DTB = 136
    C_ALOG = 168
    C_GNW = 200
    C_GB = 216
    C_BL = 248
    C_U = 256
    C_L0 = 512
    C_ONES = 640
    C_IDF = 768
    B_ID = 0
    B_CM0 = 128
    B_CM1 = 384
    identb = cstb[:, B_ID:B_ID + 128]
    identf = cst[:, C_IDF:C_IDF + 128]
    S.barrier()
    S.op('act', lambda: nc.scalar.activation(out=cst[:, C_ALOG:C_ALOG + 32], in_=cst[:, C_ALOG:C_ALOG + 32], func=AF.Exp),
         reads=[cst], writes=[cst])
    S.op('dve', lambda: nc.vector.tensor_scalar(out=cst[:, C_ALOG:C_ALOG + 32], in0=cst[:, C_ALOG:C_ALOG + 32],
                                                 scalar1=-1.0, scalar2=None, op0=ALU.mult),
         reads=[cst], writes=[cst])
    S.barrier()
    A_bc = cst[:, C_ALOG:C_ALOG + 32]

    normw_b = cst[:, C_NORMW:C_NORMW + 16]
    cnt = [0]

    def conv_tile(win, wo, src_ap, dst_ap, kcols, fold, k0=0, nk=16):
        a = win.next()
        o = wo.next()
        n = nk * kcols
        S.dma('sp', a[:, :n], src_ap, writes=[a], slot=('win', id(win), win.i))
        if fold:
            S.op('dve', lambda: nc.vector.tensor_tensor(
                out=o[:, :n].rearrange("p (k c) -> p k c", k=nk),
                in0=a[:, :n].rearrange("p (k c) -> p k c", k=nk),
                in1=normw_b[:, k0:k0 + nk].unsqueeze(2).broadcast_to([128, nk, kcols]), op=ALU.mult),
                reads=[a], writes=[o])
        else:
            cnt[0] += 1
            if cnt[0] % 2:
                S.op('act', lambda: nc.scalar.copy(out=o[:, :n], in_=a[:, :n]), reads=[a], writes=[o])
            else:
                S.op('dve', lambda: nc.vector.tensor_copy(out=o[:, :n], in_=a[:, :n]), reads=[a], writes=[o])
        S.dma('pool', dst_ap, o[:, :n], reads=[o], slot=('wo', id(wo), wo.i))

    with ExitStack() as pes:
        win = ring(pes, 3, [128, 2048], F32, "win")
        wo = ring(pes, 3, [128, 2048], BF16, "wo")
        for c in range(40):
            conv_tile(win, wo, wfm_d[c], wfm_s[c], 128, True)
        for c in range(8):
            for kq in range(4):
                conv_tile(win, wo, wtm_d[c][:, kq * 2048:(kq + 1) * 2048], wtm_s[c][:, kq * 2048:(kq + 1) * 2048], 512, True, k0=kq * 4, nk=4)
        conv_tile(win, wo, wdt_d[:, :], wdt_s[:, :], 32, True)
        S.barrier()
    chk('W')

    def late_conversions(win, wo):
        lst = []
        for c in range(32):
            lst.append(lambda c=c: conv_tile(win, wo, wgm_d[c], wgm_s[c], 128, True))
        for c in range(16):
            for hq in range(2):
                lst.append(lambda c=c, hq=hq: conv_tile(win, wo, wssm_d[c][:, hq * 2048:(hq + 1) * 2048], wssm_s[c][:, hq * 2048:(hq + 1) * 2048], 128, False))
        for c in range(16):
            lst.append(lambda c=c: conv_tile(win, wo, wattn_d[c], wattn_s[c], 128, False))
        for c in range(4):
            for kq in range(4):
                lst.append(lambda c=c, kq=kq: conv_tile(win, wo, wout_d[c][:, kq * 2048:(kq + 1) * 2048], wout_s[c][:, kq * 2048:(kq + 1) * 2048], 512, False, nk=4))
        return lst

    with ExitStack() as pes:
        xring = ring(pes, 2, [128, 2048], F32, "xt")
        junk = sb(pes, [128, 2048], BF16, "junk")
        hbr = ring(pes, 2, [128, 2048], BF16, "hb")
        stg = ring(pes, 2, [128, 16, 512], BF16, "hst")
        smr = ring(pes, 4, [128, 4], F32, "sm")
        ptr = ring(pes, 2, [128, 16, 128], BF16, "ptr", psum=True)
        for i4 in range(T // 512):
            stage = stg.next()
            for ii in range(4):
                i = i4 * 4 + ii
                xt = xring.next()
                S.dma('sp', xt[:], x_d[i * 128:(i + 1) * 128, :], writes=[xt], slot=('xt', xring.i))
                sm = smr.next()
                S.op('act', lambda: nc.scalar.activation(out=junk[:], in_=xt[:], func=AF.Square, accum_out=sm[:, 0:1]),
                     reads=[xt], writes=[junk, sm])
                S.op('dve', lambda: nc.vector.tensor_scalar(out=sm[:, 1:2], in0=sm[:, 0:1], scalar1=1.0 / D, scalar2=EPS,
                                                             op0=ALU.mult, op1=ALU.add), reads=[sm], writes=[sm])
                S.op('act', lambda: nc.scalar.activation(out=sm[:, 2:3], in_=sm[:, 1:2], func=AF.Sqrt), reads=[sm], writes=[sm])
                S.op('dve', lambda: nc.vector.reciprocal(out=sm[:, 3:4], in_=sm[:, 2:3]), reads=[sm], writes=[sm])
                hb = hbr.next()
                S.op('dve', lambda: nc.vector.tensor_scalar(out=hb[:], in0=xt[:], scalar1=sm[:, 3:4], scalar2=None, op0=ALU.mult),
                     reads=[xt, sm], writes=[hb])
                pt = ptr.next()
                for k in range(16):
                    S.op('pe', lambda: nc.tensor.transpose(out=pt[:, k, :], in_=hb[:, k * 128:(k + 1) * 128], identity=identb),
                         reads=[hb], writes=[pt], signal=(k == 15))
                S.op('act', lambda: nc.scalar.copy(out=stage[:, :, ii * 128:(ii + 1) * 128], in_=pt[:]), reads=[pt], writes=[stage])
            S.dma('pool', hT_s[:, :, i4 * 512:(i4 + 1) * 512].rearrange("k p t -> p k t"), stage[:], reads=[stage], slot=('hst', stg.i))
        S.barrier()
    chk('A')

    with ExitStack() as pes:
        hT = sb(pes, [128, 16, TS], BF16, "hT")
        wfr = ring(pes, 3, [128, 16, 128], BF16, "wf")
        wtr = ring(pes, 2, [128, 16, 512], BF16, "wt")
        wdt = sb(pes, [128, 16, 32], BF16, "wdt")
        halo = sb(pes, [128, 24, 4], F32, "halo")
        xsr = ring(pes, 2, [128, 516], F32, "xs")
        accr = ring(pes, 2, [128, 512], F32, "acc")
        fmo = ring(pes, 3, [128, 512], BF16, "fmo")
        tmo = ring(pes, 3, [128, 512], BF16, "tmo")
        tms = ring(pes, 2, [128, 4, 128], BF16, "tms")
        dtr = ring(pes, 2, [128, 32], F32, "dtr")
        pacc = ring(pes, 4, [128, 512], F32, "pacc", psum=True)
        ptp = ring(pes, 2, [128, 4, 128], BF16, "ptp", psum=True)
        S.op('dve', lambda: nc.vector.memset(halo[:], 0.0), writes=[halo])
        lwin = ring(pes, 2, [128, 2048], F32, "lwin")
        lwo = ring(pes, 2, [128, 2048], BF16, "lwo")
        late = late_conversions(lwin, lwo)
        S.dma('sp', wdt[:], wdt_s[:, :].rearrange("p (k c) -> p k c", k=16), writes=[wdt], slot='wdt')
        for sbi in range(NSB):
            t0 = sbi * TS
            for k in range(16):
                S.dma('sp', hT[:, k, :], hT_s[k][:, t0:t0 + TS], writes=[hT], slot='hT')
            pend = []
            for c in range(40):
                wf = wfr.next()
                S.dma('sp', wf[:], wfm_s[c].rearrange("p (k c) -> p k c", k=16), writes=[wf], slot=('wf', wfr.i))
                for tt in range(TS // 512):
                    tok0 = t0 + tt * 512
                    pa = pacc.next()
                    for k in range(16):
                        S.op('pe', lambda: nc.tensor.matmul(pa[:], lhsT=wf[:, k, :], rhs=hT[:, k, tt * 512:(tt + 1) * 512],
                                                            start=(k == 0), stop=(k == 15)),
                             reads=[wf, hT], writes=[pa], signal=(k == 15))
                    while pend:
                        pend.pop(0)()
                    if late:
                        late.pop(0)()
                    if c < 24:
                        xs = xsr.next()
                        S.op('act', lambda: nc.scalar.copy(out=xs[:, 3:515], in_=pa[:]), reads=[pa], writes=[(xs, 'b')])
                        S.op('dve', lambda: nc.vector.tensor_copy(out=xs[:, 0:3], in_=halo[:, c, 0:3]), reads=[halo], writes=[(xs, 'h')])
                        S.op('dve', lambda: nc.vector.tensor_copy(out=halo[:, c, 0:3], in_=xs[:, 512:515]), reads=[(xs, 'b')], writes=[halo])
                        acc = accr.next()
                        cw = C_CW + c * 4
                        S.op('dve', lambda: nc.vector.tensor_scalar(out=acc[:], in0=xs[:, 0:512], scalar1=cst[:, cw:cw + 1],
                                                                     scalar2=cst[:, C_CB + c:C_CB + c + 1], op0=ALU.mult, op1=ALU.add),
                             reads=[(xs, 'b'), (xs, 'h')], writes=[acc])
                        for k in range(1, 4):
                            S.op('dve', lambda: nc.vector.scalar_tensor_tensor(out=acc[:], in0=xs[:, k:k + 512], scalar=cst[:, cw + k:cw + k + 1],
                                                                                in1=acc[:], op0=ALU.mult, op1=ALU.add),
                                 reads=[(xs, 'b'), (xs, 'h'), acc], writes=[acc])
                        fo = fmo.next()
                        S.op('act', lambda: nc.scalar.activation(out=fo[:], in_=acc[:], func=AF.Silu), reads=[acc], writes=[fo])
                        if c >= 16:
                            dst = (bT_s if c < 20 else cT_s)[((c - 16) % 4) * 128:((c - 16) % 4 + 1) * 128, tok0:tok0 + 512]
                            S.dma('pool', dst, fo[:], reads=[fo], slot=('fmo', fmo.i))
                        if c < 20:
                            def _tr(fo=fo, c=c, tok0=tok0):
                                pt = ptp.next()
                                for j in range(4):
                                    S.op('pe', lambda: nc.tensor.transpose(out=pt[:, j, :], in_=fo[:, j * 128:(j + 1) * 128], identity=identb),
                                         reads=[fo], writes=[pt], signal=(j == 3))
                                ts_ = tms.next()
                                S.op('act', lambda: nc.scalar.copy(out=ts_[:], in_=pt[:]), reads=[pt], writes=[ts_])
                                if c < 16:
                                    dst = xtm_s[tok0:tok0 + 512, c * 128:(c + 1) * 128]
                                else:
                                    dst = btm_s[tok0:tok0 + 512, (c - 16) * 128:(c - 15) * 128]
                                S.dma('pool', dst.rearrange("(j p) c -> p j c", p=128), ts_[:], reads=[ts_], slot=('tms', tms.i))
                            pend.append(_tr)
                    else:
                        fo = fmo.next()
                        S.op('act', lambda: nc.scalar.copy(out=fo[:], in_=pa[:]), reads=[pa], writes=[fo])
                        cc_ = c - 24
                        dst = (qT_s if cc_ < 8 else kT_s)[(cc_ % 8) * 128:(cc_ % 8 + 1) * 128, tok0:tok0 + 512]
                        S.dma('pool', dst, fo[:], reads=[fo], slot=('fmo', fmo.i))
            while pend:
                pend.pop(0)()
            while late and sbi == NSB - 1:
                late.pop(0)()
            for blk in range(8):
                wt = wtr.next()
                S.dma('sp', wt[:], wtm_s[blk].rearrange("p (k c) -> p k c", k=16), writes=[wt], slot=('wt', wtr.i))
                for tt in range(TS // 128):
                    tok0 = t0 + tt * 128
                    pa = pacc.next()
                    for k in range(16):
                        S.op('pe', lambda: nc.tensor.matmul(pa[:], lhsT=hT[:, k, tt * 128:(tt + 1) * 128], rhs=wt[:, k, :],
                                                            start=(k == 0), stop=(k == 15)),
                             reads=[wt, hT], writes=[pa], signal=(k == 15))
                    to = tmo.next()
                    if blk < 4:
                        S.op('act', lambda: nc.scalar.activation(out=to[:], in_=pa[:], func=AF.Silu), reads=[pa], writes=[to])
                        dst = zs_s[tok0:tok0 + 128, blk * 512:(blk + 1) * 512]
                    elif blk < 6:
                        S.op('act', lambda: nc.scalar.copy(out=to[:], in_=pa[:]), reads=[pa], writes=[to])
                        dst = v_s[tok0:tok0 + 128, (blk - 4) * 512:(blk - 3) * 512]
                    else:
                        S.op('act', lambda: nc.scalar.activation(out=to[:], in_=pa[:], func=AF.Silu), reads=[pa], writes=[to])
                        dst = gs_s[tok0:tok0 + 128, (blk - 6) * 512:(blk - 5) * 512]
                    S.dma('pool', dst, to[:], reads=[to], slot=('tmo', tmo.i))
            for tt in range(TS // 128):
                tok0 = t0 + tt * 128
                pa = pacc.next()
                for k in range(16):
                    S.op('pe', lambda: nc.tensor.matmul(pa[:, 0:32], lhsT=hT[:, k, tt * 128:(tt + 1) * 128], rhs=wdt[:, k, :],
                                                        start=(k == 0), stop=(k == 15)),
                         reads=[wdt, hT], writes=[pa], signal=(k == 15))
                d_ = dtr.next()
                S.op('dve', lambda: nc.vector.tensor_tensor(out=d_[:], in0=pa[:, 0:32], in1=cst[:, C_DTB:C_DTB + 32], op=ALU.add),
                     reads=[pa], writes=[d_])
                S.op('act', lambda: nc.scalar.activation(out=d_[:], in_=d_[:], func=AF.Exp), reads=[d_], writes=[d_])
                S.op('act', lambda: nc.scalar.activation(out=d_[:], in_=d_[:], func=AF.Ln, bias=1.0, scale=1.0), reads=[d_], writes=[d_])
                S.dma('pool', dt_s[tok0:tok0 + 128, :], d_[:], reads=[d_], slot=('dtr', dtr.i))
            S.barrier()
    chk('P')

    with ExitStack() as pes:
        xr = ring(pes, 2, [128, 2, 2048], BF16, "sx")
        br = ring(pes, 2, [128, 2, 512], BF16, "sb")
        bTr = ring(pes, 2, [128, 4, 256], BF16, "sbT")
        cTr = ring(pes, 2, [128, 4, 256], BF16, "scT")
        dtr2 = ring(pes, 2, [128, 2, 32], F32, "sdt")
        zr = ring(pes, 2, [128, 2, 2048], BF16, "sz")
        state = sb(pes, [128, 4, 512], F32, "state")
        stbf = sb(pes, [128, 4, 512], BF16, "stbf")
        dtA = sb(pes, [128, 2, 32], F32, "dtA")
        cumT = sb(pes, [128, 2, 32], F32, "cumT")
        cend = sb(pes, [128, 32], F32, "cend")
        ecum = sb(pes, [128, 2, 32], F32, "ecum")
        dend = sb(pes, [128, 2, 32], F32, "dend")
        sdec = sb(pes, [128, 32], F32, "sdec")
        cbm = ring(pes, 2, [128, 384], F32, "cbm")
        lhr = ring(pes, 3, [128, 384], F32, "lh")
        er = ring(pes, 3, [128, 384], BF16, "er")
        wr = ring(pes, 3, [128, 384], BF16, "wr")
        t1r = ring(pes, 4, [128, 512], F32, "t1")
        t2r = ring(pes, 2, [128, 512], F32, "t2")
        smr = ring(pes, 4, [128, 4], F32, "ssm")
        ynr = ring(pes, 2, [128, 512], BF16, "yn")
        yts = ring(pes, 2, [128, 4, 128], BF16, "yts")
        xdr = ring(pes, 2, [128, 2, 512], BF16, "xd")
        p_cum = ps(pes, [128, 512], F32, "pcum")
        p_cb = ring(pes, 1, [128, 512], F32, "pcb", psum=True)
        p_seg = ring(pes, 2, [128, 512], F32, "pseg", psum=True)
        p_yd = ring(pes, 1, [128, 2, 512], F32, "pyd", psum=True)
        p_yo = ring(pes, 1, [128, 512], F32, "pyo", psum=True)
        p_tr = ring(pes, 1, [128, 4, 128], BF16, "ptr2", psum=True)
        junk = sb(pes, [128, 512], BF16, "junk2")
        dsk = sb(pes, [128, 2048], F32, "dsk")
        S.dma('sp', dsk[:], dsk_d[:, :], writes=[dsk], slot='dsk')
        U2 = cst[:, C_U:C_U + 256]
        U1 = cst[:, C_U:C_U + 128]
        L0 = cst[:, C_L0:C_L0 + 128]
        ONES = cst[:, C_ONES:C_ONES + 128]
        S.op('dve', lambda: nc.vector.memset(state[:], 0.0), writes=[(state, 0), (state, 1), (state, 2), (state, 3)])
        S.op('dve', lambda: nc.vector.memset(stbf[:], 0.0), writes=[(stbf, 0), (stbf, 1), (stbf, 2), (stbf, 3)])
        for c in range(NCH):
            tok0 = c * 256
            x2 = xr.next(); b2 = br.next(); bT = bTr.next(); cT = cTr.next(); dt2 = dtr2.next(); z2 = zr.next()
            cbs = {}; pyds = {}; wws = {}; ees = {}; pend = []
            S.dma('sp', x2[:], xtm_s[tok0:tok0 + 256, :].rearrange("(j p) c -> p j c", p=128), writes=[x2], slot=('sx', xr.i))
            S.dma('sp', b2[:], btm_s[tok0:tok0 + 256, :].rearrange("(j p) c -> p j c", p=128), writes=[b2], slot=('sb', br.i))
            S.dma('sp', bT[:], bT_s[:, tok0:tok0 + 256].rearrange("(g p) t -> p g t", p=128), writes=[bT], slot=('sbT', bTr.i))
            S.dma('sp', cT[:], cT_s[:, tok0:tok0 + 256].rearrange("(g p) t -> p g t", p=128), writes=[cT], slot=('scT', cTr.i))
            S.dma('sp', dt2[:], dt_s[tok0:tok0 + 256, :].rearrange("(j p) c -> p j c", p=128), writes=[dt2], slot=('sdt', dtr2.i))
            S.dma('sp', z2[:], zs_s[tok0:tok0 + 256, :].rearrange("(j p) c -> p j c", p=128), writes=[z2], slot=('sz', zr.i))
            S.op('dve', lambda: nc.vector.tensor_tensor(out=dtA[:], in0=dt2[:], in1=A_bc.unsqueeze(1).broadcast_to([128, 2, 32]), op=ALU.mult),
                 reads=[dt2], writes=[dtA])
            S.op('pe', lambda: nc.tensor.matmul(p_cum[:, 0:32], lhsT=U1, rhs=dtA[:, 0, :], start=True, stop=True), reads=[dtA], writes=[p_cum], signal=False)
            S.op('pe', lambda: nc.tensor.matmul(p_cum[:, 32:64], lhsT=ONES, rhs=dtA[:, 0, :], start=True, stop=False), reads=[dtA], writes=[p_cum], signal=False)
            S.op('pe', lambda: nc.tensor.matmul(p_cum[:, 32:64], lhsT=U1, rhs=dtA[:, 1, :], start=False, stop=True), reads=[dtA], writes=[p_cum], signal=False)
            S.op('pe', lambda: nc.tensor.matmul(p_cum[:, 64:96], lhsT=ONES, rhs=dtA[:, 0, :], start=True, stop=False), reads=[dtA], writes=[p_cum], signal=False)
            S.op('pe', lambda: nc.tensor.matmul(p_cum[:, 64:96], lhsT=ONES, rhs=dtA[:, 1, :], start=False, stop=True), reads=[dtA], writes=[p_cum])
            S.op('dve', lambda: nc.vector.tensor_copy(out=cumT[:].rearrange("p j h -> p (j h)"), in_=p_cum[:, 0:64]), reads=[p_cum], writes=[cumT])
            S.op('dve', lambda: nc.vector.tensor_copy(out=cend[:], in_=p_cum[:, 64:96]), reads=[p_cum], writes=[cend])
            S.op('act', lambda: nc.scalar.activation(out=ecum[:], in_=cumT[:], func=AF.Exp), reads=[cumT], writes=[ecum])
            S.op('act', lambda: nc.scalar.activation(out=sdec[:], in_=cend[:], func=AF.Exp), reads=[cend], writes=[sdec])
            S.op('dve', lambda: nc.vector.tensor_tensor(out=dend[:], in0=cend[:].unsqueeze(1).broadcast_to([128, 2, 32]), in1=cumT[:], op=ALU.subtract),
                 reads=[cend, cumT], writes=[dend])
            S.op('act', lambda: nc.scalar.activation(out=dend[:], in_=dend[:], func=AF.Exp), reads=[dend], writes=[dend])
            S.op('dve', lambda: nc.vector.tensor_tensor(out=dend[:], in0=dend[:], in1=dt2[:], op=ALU.mult), reads=[dend, dt2], writes=[dend])
            def prologue(g):
                pcb = p_cb.next()
                S.op('pe', lambda: nc.tensor.matmul(pcb[:, 0:256], lhsT=bT[:, g, 0:128], rhs=cT[:, g, 0:256], start=True, stop=True),
                     reads=[bT, cT], writes=[pcb], signal=False)
                S.op('pe', lambda: nc.tensor.matmul(pcb[:, 256:384], lhsT=bT[:, g, 128:256], rhs=cT[:, g, 128:256], start=True, stop=True),
                     reads=[bT, cT], writes=[pcb])
                cb_ = cbm.next()
                S.op('dve', lambda: nc.vector.tensor_tensor(out=cb_[:, 0:256], in0=pcb[:, 0:256], in1=U2, op=ALU.mult), reads=[pcb], writes=[(cb_, 0)])
                S.op('dve', lambda: nc.vector.tensor_tensor(out=cb_[:, 256:384], in0=pcb[:, 256:384], in1=U1, op=ALU.mult), reads=[pcb], writes=[(cb_, 1)])
                cbs[g] = cb_
                pyds[g] = p_yd.next()

            def stageA(g, e):
                if e == 0:
                    prologue(g)
                cb_ = cbs[g]
                h = g * 8 + e
                lh = lhr.next()
                S.op('dve', lambda: nc.vector.tensor_scalar(out=lh[:, 0:128], in0=L0, scalar1=dtA[:, 0, h:h + 1], scalar2=None, op0=ALU.mult),
                     reads=[dtA], writes=[(lh, 0)])
                S.op('dve', lambda: nc.vector.tensor_scalar(out=lh[:, 128:256], in0=ONES, scalar1=dtA[:, 1, h:h + 1], scalar2=None, op0=ALU.mult),
                     reads=[dtA], writes=[(lh, 1)])
                S.op('dve', lambda: nc.vector.tensor_scalar(out=lh[:, 256:384], in0=L0, scalar1=dtA[:, 1, h:h + 1], scalar2=None, op0=ALU.mult),
                     reads=[dtA], writes=[(lh, 2)])
                pseg = p_seg.next()
                S.op('pe', lambda: nc.tensor.matmul(pseg[:, 0:256], lhsT=lh[:, 0:128], rhs=U2, start=True, stop=False),
                     reads=[(lh, 0)], writes=[pseg], signal=False)
                S.op('pe', lambda: nc.tensor.matmul(pseg[:, 128:256], lhsT=lh[:, 128:256], rhs=U1, start=False, stop=True),
                     reads=[(lh, 1)], writes=[pseg], signal=False)
                S.op('pe', lambda: nc.tensor.matmul(pseg[:, 256:384], lhsT=lh[:, 256:384], rhs=U1, start=True, stop=True),
                     reads=[(lh, 2)], writes=[pseg])
                ee = er.next()
                S.op('act', lambda: nc.scalar.activation(out=ee[:], in_=pseg[:, 0:384], func=AF.Exp), reads=[pseg], writes=[ee])
                ees[(g, e)] = ee

            def stageA2(g, e):
                cb_ = cbs[g]
                h = g * 8 + e
                ee = ees.pop((g, e))
                ww = wr.next()
                S.op('dve', lambda: nc.vector.scalar_tensor_tensor(out=ww[:, 0:256], in0=ee[:, 0:256], scalar=dt2[:, 0, h:h + 1], in1=cb_[:, 0:256],
                                                                    op0=ALU.mult, op1=ALU.mult), reads=[ee, dt2, (cb_, 0)], writes=[(ww, 0)])
                S.op('dve', lambda: nc.vector.scalar_tensor_tensor(out=ww[:, 256:384], in0=ee[:, 256:384], scalar=dt2[:, 1, h:h + 1], in1=cb_[:, 256:384],
                                                                    op0=ALU.mult, op1=ALU.mult), reads=[ee, dt2, (cb_, 1)], writes=[(ww, 1)])
                wws[(g, e)] = ww

            def stageB(g, e):
                h = g * 8 + e
                ww = wws.pop((g, e))
                pyd = pyds[g]
                xs0 = x2[:, 0, h * 64:(h + 1) * 64]
                xs1 = x2[:, 1, h * 64:(h + 1) * 64]
                S.op('pe', lambda: nc.tensor.matmul(pyd[:, 0, e * 64:(e + 1) * 64], lhsT=ww[:, 0:128], rhs=xs0, start=True, stop=True),
                     reads=[(ww, 0), x2], writes=[pyd], signal=False)
                S.op('pe', lambda: nc.tensor.matmul(pyd[:, 1, e * 64:(e + 1) * 64], lhsT=ww[:, 128:256], rhs=xs0, start=True, stop=False),
                     reads=[(ww, 0), x2], writes=[pyd], signal=False)
                S.op('pe', lambda: nc.tensor.matmul(pyd[:, 1, e * 64:(e + 1) * 64], lhsT=ww[:, 256:384], rhs=xs1, start=False, stop=True),
                     reads=[(ww, 1), x2], writes=[pyd], signal=(e == 7))
                if e == 7:
                    epilogue(g)

            def epilogue(g):
                pyd = pyds.pop(g)
                t1s = [t1r.next(), t1r.next()]

                def _c0():
                    for lt in range(2):
                        S.op('act', lambda: nc.scalar.copy(out=t1s[lt][:], in_=pyd[:, lt, :]), reads=[pyd], writes=[t1s[lt]])
                _c0()
                for lt in range(2):
                    t1 = t1s[lt]
                    t2 = t2r.next()
                    sm = smr.next()
                    yn = ynr.next()

                    def _c1(lt=lt, t1=t1, t2=t2):
                        pyo = p_yo.next()
                        S.op('pe', lambda: nc.tensor.matmul(pyo[:], lhsT=cT[:, g, lt * 128:(lt + 1) * 128], rhs=stbf[:, g, :], start=True, stop=True),
                             reads=[cT, (stbf, g)], writes=[pyo])
                        S.op('dve', lambda: nc.vector.tensor_tensor(out=t2[:].rearrange("p (e q) -> p e q", e=8), in0=pyo[:].rearrange("p (e q) -> p e q", e=8),
                                                                     in1=ecum[:, lt, g * 8:(g + 1) * 8].unsqueeze(2).broadcast_to([128, 8, 64]), op=ALU.mult),
                             reads=[pyo, ecum], writes=[t2])
                        S.op('dve', lambda: nc.vector.tensor_tensor(out=t1[:], in0=t1[:], in1=t2[:], op=ALU.add), reads=[t1, t2], writes=[t1])

                    def _c2(lt=lt, t1=t1, t2=t2, sm=sm):
                        S.op('dve', lambda: nc.vector.tensor_tensor(out=t2[:], in0=x2[:, lt, g * 512:(g + 1) * 512],
                                                                     in1=dsk[:, g * 512:(g + 1) * 512], op=ALU.mult), reads=[x2, dsk, t1], writes=[t2])
                        S.op('dve', lambda: nc.vector.tensor_tensor(out=t1[:], in0=t1[:], in1=t2[:], op=ALU.add), reads=[t1, t2], writes=[t1])
                        S.op('dve', lambda: nc.vector.tensor_tensor(out=t1[:], in0=t1[:], in1=z2[:, lt, g * 512:(g + 1) * 512], op=ALU.mult),
                             reads=[t1, z2], writes=[t1])
                        S.op('act', lambda: nc.scalar.activation(out=junk[:], in_=t1[:], func=AF.Square, accum_out=sm[:, 0:1]), reads=[t1], writes=[junk, sm])

                    def _c3(sm=sm):
                        S.op('dve', lambda: nc.vector.tensor_scalar(out=sm[:, 1:2], in0=sm[:, 0:1], scalar1=1.0 / 512, scalar2=EPS, op0=ALU.mult, op1=ALU.add),
                             reads=[sm], writes=[sm])
                        S.op('act', lambda: nc.scalar.activation(out=sm[:, 2:3], in_=sm[:, 1:2], func=AF.Sqrt), reads=[sm], writes=[sm])

                    def _c4(sm=sm, t1=t1, yn=yn):
                        S.op('dve', lambda: nc.vector.reciprocal(out=sm[:, 3:4], in_=sm[:, 2:3]), reads=[sm], writes=[sm])
                        S.op('act', lambda: nc.scalar.activation(out=yn[:], in_=t1[:], func=AF.Copy, scale=sm[:, 3:4]), reads=[t1, sm], writes=[yn])

                    def _tr(yn=yn, lt=lt):
                        ptr_ = p_tr.next()
                        for j in range(4):
                            S.op('pe', lambda: nc.tensor.transpose(out=ptr_[:, j, :], in_=yn[:, j * 128:(j + 1) * 128], identity=identb),
                                 reads=[yn], writes=[ptr_], signal=(j == 3))
                        yt = yts.next()
                        S.op('dve', lambda: nc.vector.tensor_tensor(out=yt[:], in0=ptr_[:], in1=cst[:, C_GNW + g * 4:C_GNW + g * 4 + 4].unsqueeze(2).broadcast_to([128, 4, 128]),
                                                                     op=ALU.mult), reads=[ptr_], writes=[yt])
                        S.dma('pool', ynT_s[g * 4:(g + 1) * 4, :, tok0 + lt * 128:tok0 + (lt + 1) * 128].rearrange("j p t -> p j t"), yt[:], reads=[yt], slot=('yts', yts.i))
                    pend.extend([_c1, _c2, _c3, _c4, _tr])
                xd = xdr.next()

                def _st(xd=xd):
                    S.op('dve', lambda: nc.vector.tensor_tensor(out=xd[:].rearrange("p j (e q) -> p j e q", e=8),
                                                                 in0=x2[:, :, g * 512:(g + 1) * 512].rearrange("p j (e q) -> p j e q", e=8),
                                                                 in1=dend[:, :, g * 8:(g + 1) * 8].unsqueeze(3).broadcast_to([128, 2, 8, 64]), op=ALU.mult),
                         reads=[x2, dend], writes=[xd])
                    pst = p_yo.next()
                    S.op('pe', lambda: nc.tensor.matmul(pst[:], lhsT=b2[:, 0, g * 128:(g + 1) * 128], rhs=xd[:, 0, :], start=True, stop=False), reads=[b2, xd], writes=[pst], signal=False)
                    S.op('pe', lambda: nc.tensor.matmul(pst[:], lhsT=b2[:, 1, g * 128:(g + 1) * 128], rhs=xd[:, 1, :], start=False, stop=True), reads=[b2, xd], writes=[pst])
                    S.op('dve', lambda: nc.vector.tensor_tensor(out=state[:, g, :].rearrange("p (e q) -> p e q", e=8), in0=state[:, g, :].rearrange("p (e q) -> p e q", e=8),
                                                                 in1=sdec[:, g * 8:(g + 1) * 8].unsqueeze(2).broadcast_to([128, 8, 64]), op=ALU.mult),
                         reads=[(state, g), sdec], writes=[(state, g)])
                    S.op('dve', lambda: nc.vector.tensor_tensor(out=state[:, g, :], in0=state[:, g, :], in1=pst[:], op=ALU.add), reads=[(state, g), pst], writes=[(state, g)])
                    S.op('act', lambda: nc.scalar.copy(out=stbf[:, g, :], in_=state[:, g, :]), reads=[(state, g)], writes=[(stbf, g)])
                pend.append(_st)

            hitems = [(g, e) for g in range(4) for e in range(8)]
            NH = len(hitems)
            for n in range(NH + 2):
                if n < NH:
                    stageA(*hitems[n])
                for _ in range(2):
                    if pend:
                        pend.pop(0)()
                if 0 <= n - 1 < NH:
                    stageA2(*hitems[n - 1])
                if n - 2 >= 0:
                    stageB(*hitems[n - 2])
            while pend:
                pend.pop(0)()
        S.barrier()
    chk('S')

    def gather(src, dst):
        nc.gpsimd.collective_compute("AllGather", ALU.bypass, replica_groups=pair_groups,
                                     ins=[src.opt()], outs=[dst.opt()]).then_inc(cc_sem)
        n_cc[0] += 1

    for j in range(16):
        for tq in range(0, T, 8192):
            gather(ynT_s[j], ynT_g[j])

    with ExitStack() as pes:
        qTr = ring(pes, 2, [128, T], BF16, "mq")
        kTr = ring(pes, 2, [128, T], BF16, "mk")
        var = ring(pes, 2, [128, NT, 130], BF16, "mv")
        gsr = ring(pes, 2, [128, NT, 128], BF16, "mg")
        kmf = sb(pes, [128, 32], F32, "kmf")
        kmb = sb(pes, [128, 32], BF16, "kmb")
        gbuf = ring(pes, 2, [128, 32], F32, "gbuf")
        t8r = ring(pes, 2, [128, 8], F32, "t8")
        selr = ring(pes, 6, [128, 32], F32, "sel")
        ptr_ = ring(pes, 4, [128, 512], BF16, "mp")
        oacc = ring(pes, 4, [128, 132], F32, "oacc")
        rdr = ring(pes, 4, [128, 2], F32, "rd")
        ogr = ring(pes, 2, [128, 128], BF16, "og")
        ogT = ring(pes, 2, [128, 256], BF16, "ogT")
        p_g = ring(pes, 1, [128, 512], F32, "pg", psum=True)
        p_s = ring(pes, 4, [128, 512], F32, "pS", psum=True)
        p_o = ring(pes, 2, [128, 2, 256], F32, "pO", psum=True)
        p_t = ring(pes, 1, [128, 2, 128], BF16, "pT", psum=True)
        scale = 128.0 ** -0.5
        NBLK = T // 256
        LA = 3
        cmask = cstb[:, B_CM0:B_CM0 + 512]
        for hd in range(8):
            qT = qTr.next(); kT = kTr.next(); va = var.next(); gs = gsr.next()
            S.dma('sp', qT[:], qT_s[hd * 128:(hd + 1) * 128, :], writes=[qT], slot=('mq', qTr.i))
            S.dma('sp', kT[:], kT_s[hd * 128:(hd + 1) * 128, :], writes=[kT], slot=('mk', kTr.i))
            S.dma('sp', va[:, :, 0:128], v_s[:, hd * 128:(hd + 1) * 128].rearrange("(j p) c -> p j c", p=128), writes=[(va, 'v')], slot=('mv', var.i))
            S.op('dve', lambda: nc.vector.memset(va[:, :, 128:130], 1.0), writes=[(va, 'o')])
            S.dma('sp', gs[:], gs_s[:, hd * 128:(hd + 1) * 128].rearrange("(j p) c -> p j c", p=128), writes=[gs], slot=('mg', gsr.i))
            S.op('dve', lambda: nc.vector.tensor_reduce(out=kmf[:, 0:NBLK], in_=kT[:].rearrange("p (n t) -> p n t", t=256), op=ALU.add, axis=AX.X),
                 reads=[kT], writes=[kmf])
            S.op('act', lambda: nc.scalar.activation(out=kmb[:, 0:NBLK], in_=kmf[:, 0:NBLK], func=AF.Copy, scale=1.0 / 256), reads=[kmf], writes=[kmb])
            gb = [gbuf.next(), gbuf.next()]
            for qt in range(2):
                S.op('dve', lambda: nc.vector.memset(gb[qt][:], NEGBIG), writes=[gb[qt]])
            items = [(i, j) for i in range(NBLK) for j in [i] + list(range(i))]
            selm = {}
            pSm = {}
            oam = {}

            def emit_qk(n):
                i, j = items[n]
                if j == i and i >= 1:
                    pg = p_g.next()
                    for qt in range(2):
                        q0 = i * 256 + qt * 128
                        S.op('pe', lambda: nc.tensor.matmul(pg[:, qt * 32:qt * 32 + NBLK], lhsT=qT[:, q0:q0 + 128], rhs=kmb[:, 0:NBLK], start=True, stop=True),
                             reads=[qT, kmb], writes=[pg], signal=(qt == 1))
                    sl = []
                    for qt in range(2):
                        S.op('dve', lambda: nc.vector.tensor_copy(out=gb[qt][:, 0:i], in_=pg[:, qt * 32:qt * 32 + i]), reads=[pg], writes=[gb[qt]])
                        t8 = t8r.next()
                        S.op('dve', lambda: nc.vector.max(out=t8[:], in_=gb[qt][:]), reads=[gb[qt]], writes=[t8])
                        sel = selr.next()
                        S.op('dve', lambda: nc.vector.tensor_scalar(out=sel[:], in0=gb[qt][:], scalar1=t8[:, 2:3], scalar2=None, op0=ALU.is_ge),
                             reads=[gb[qt], t8], writes=[sel])
                        sl.append(sel)
                    selm[i] = sl
                pS = p_s.next()
                for kt in range(2):
                    k0 = j * 256 + kt * 128
                    S.op('pe', lambda: nc.tensor.matmul(pS[:, kt * 256:(kt + 1) * 256], lhsT=kT[:, k0:k0 + 128], rhs=qT[:, i * 256:(i + 1) * 256], start=True, stop=True),
                         reads=[kT, qT], writes=[pS], signal=(kt == 1))
                pSm[n] = pS

            def emit_rest(n):
                i, j = items[n]
                pS = pSm.pop(n)
                if j == i:
                    oam[i] = [oacc.next(), oacc.next()]
                oa = oam[i]
                pT = ptr_.next()
                S.op('act', lambda: nc.scalar.activation(out=pT[:], in_=pS[:], func=AF.Exp, scale=scale), reads=[pS], writes=[pT])
                if j == i:
                    S.op('dve', lambda: nc.vector.tensor_tensor(out=pT[:], in0=pT[:], in1=cmask, op=ALU.mult), reads=[pT], writes=[pT])
                pO = p_o.next()
                for qt in range(2):
                    for kt in range(2):
                        S.op('pe', lambda: nc.tensor.matmul(pO[:, qt, 0:130], lhsT=pT[:, kt * 256 + qt * 128:kt * 256 + (qt + 1) * 128], rhs=va[:, j * 2 + kt, :],
                                                            start=(kt == 0), stop=(kt == 1)),
                             reads=[pT, (va, 'v'), (va, 'o')], writes=[pO], signal=(qt == 1 and kt == 1))
                for qt in range(2):
                    if j == i:
                        S.op('dve', lambda: nc.vector.tensor_copy(out=oa[qt][:, 0:130], in_=pO[:, qt, 0:130]), reads=[pO], writes=[oa[qt]])
                    else:
                        S.op('dve', lambda: nc.vector.scalar_tensor_tensor(out=oa[qt][:, 0:130], in0=pO[:, qt, 0:130], scalar=selm[i][qt][:, j:j + 1],
                                                                            in1=oa[qt][:, 0:130], op0=ALU.mult, op1=ALU.add),
                             reads=[pO, selm[i][qt], oa[qt]], writes=[oa[qt]])
                if (j == i - 1) or (i == 0):
                    pt_ = p_t.next()
                    for qt in range(2):
                        rd = rdr.next()
                        S.op('dve', lambda: nc.vector.reciprocal(out=rd[:, 0:1], in_=oa[qt][:, 128:129]), reads=[oa[qt]], writes=[rd])
                        og = ogr.next()
                        S.op('dve', lambda: nc.vector.scalar_tensor_tensor(out=og[:], in0=oa[qt][:, 0:128], scalar=rd[:, 0:1], in1=gs[:, i * 2 + qt, :],
                                                                            op0=ALU.mult, op1=ALU.mult), reads=[oa[qt], rd, gs], writes=[og])
                        S.op('pe', lambda: nc.tensor.transpose(out=pt_[:, qt, :], in_=og[:], identity=identb), reads=[og],
                             writes=[pt_], signal=(qt == 1))
                    ot = ogT.next()
                    S.op('act', lambda: nc.scalar.copy(out=ot[:], in_=pt_[:].rearrange("p a b -> p (a b)")), reads=[pt_], writes=[ot])
                    S.dma('pool', ogT_s[hd][:, i * 256:(i + 1) * 256], ot[:], reads=[ot], slot=('ogT', ogT.i))
                    oam.pop(i)
                    selm.pop(i, None)

            NI = len(items)
            for n in range(NI + LA):
                if n < NI:
                    emit_qk(n)
                if n - LA >= 0:
                    emit_rest(n - LA)
            S._wait('pool', [(k, v) for k, v in S.val.items() if not isinstance(k, str) and k[1] in (('ogT', 0), ('ogT', 1))])
            gather(ogT_s[hd], ogT_g[hd])
        S.barrier()
    nc.sync.wait_ge(cc_sem, n_cc[0])
    nc.gpsimd.wait_ge(cc_sem, n_cc[0])
    chk('M')

    with ExitStack() as pes:
        TB = 512
        hTf = sb(pes, [128, 16, TB], BF16, "fh")
        ynf = sb(pes, [128, 32, TB], BF16, "fy")
        ogf = sb(pes, [128, 16, TB], BF16, "fo")
        mixT = sb(pes, [128, 16, TB], BF16, "fm")
        stgr = ring(pes, 2, [128, 8, TB], BF16, "fst")
        wr1 = ring(pes, 5, [128, 16, 128], BF16, "fw1")
        wor = ring(pes, 2, [128, 16, 512], BF16, "fwo")
        g1r = ring(pes, 2, [128, TB], F32, "fg1")
        g2r = ring(pes, 2, [128, TB], F32, "fg2")
        xr = ring(pes, 2, [128, 2048], F32, "fx")
        xor_ = ring(pes, 2, [128, 2048], F32, "fxo")
        fnw = sb(pes, [128, 2048], F32, "fnw")
        smr = ring(pes, 4, [128, 4], F32, "fsm")
        psr = ring(pes, 6, [128, 512], F32, "fps", psum=True)
        pdr = ring(pes, 2, [128, 512], F32, "fpd", psum=True)
        S.dma('sp', fnw[:], fnw_d[:, :], writes=[fnw], slot='fnw')
        s_lo = cst[:, C_BL:C_BL + 1]
        s_hi = cst[:, C_BL + 1:C_BL + 2]

        def blend_load(dst, lo_src, hi_src, key):
            S.dma('sp', dst, lo_src, writes=[key], slot=key)
            st = stgr.next()
            S.dma('sp', st[:], hi_src, writes=[st], slot=('fst', stgr.i))
            S.op('dve', lambda: nc.vector.tensor_scalar(out=dst, in0=dst, scalar1=s_lo, scalar2=None, op0=ALU.mult), reads=[key], writes=[key])
            S.op('dve', lambda: nc.vector.scalar_tensor_tensor(out=dst, in0=st[:], scalar=s_hi, in1=dst, op0=ALU.mult, op1=ALU.add),
                 reads=[st, key], writes=[key])

        for bi in range(TH // TB):
            lo = bi * TB
            hi = TH + bi * TB
            for q in range(2):
                blend_load(hTf[:, q * 8:(q + 1) * 8, :], hT_s[q * 8:(q + 1) * 8, :, lo:lo + TB].rearrange("k p t -> p k t"),
                           hT_s[q * 8:(q + 1) * 8, :, hi:hi + TB].rearrange("k p t -> p k t"), ('fh', q))
            for q in range(4):
                r, jq = q // 2, (q % 2) * 8
                blend_load(ynf[:, q * 8:(q + 1) * 8, :], ynT_g[jq:jq + 8, r * 128:(r + 1) * 128, lo:lo + TB].rearrange("j p t -> p j t"),
                           ynT_g[jq:jq + 8, r * 128:(r + 1) * 128, hi:hi + TB].rearrange("j p t -> p j t"), ('fy', q))
            for r in range(2):
                blend_load(ogf[:, r * 8:(r + 1) * 8, :], ogT_g[:, r * 128:(r + 1) * 128, lo:lo + TB].rearrange("j p t -> p j t"),
                           ogT_g[:, r * 128:(r + 1) * 128, hi:hi + TB].rearrange("j p t -> p j t"), ('fo', r))
            for mc in range(16):
                srcs = [wgm_s[mc], wgm_s[16 + mc], wssm_s[mc][:, 0:2048], wssm_s[mc][:, 2048:4096], wattn_s[mc]]
                wch = []
                for s_ in srcs:
                    wt_ = wr1.next()
                    S.dma('sp', wt_[:], s_.rearrange("p (k c) -> p k c", k=16), writes=[wt_], slot=('fw1', wr1.i))
                    wch.append(wt_)
                pg1 = psr.next(); pg2 = psr.next(); pys = psr.next(); pya = psr.next()
                for k in range(16):
                    S.op('pe', lambda: nc.tensor.matmul(pg1[:], lhsT=wch[0][:, k, :], rhs=hTf[:, k, :], start=(k == 0), stop=(k == 15)),
                         reads=[wch[0], ('fh', k // 8)], writes=[pg1], signal=(k == 15))
                for k in range(16):
                    S.op('pe', lambda: nc.tensor.matmul(pg2[:], lhsT=wch[1][:, k, :], rhs=hTf[:, k, :], start=(k == 0), stop=(k == 15)),
                         reads=[wch[1], ('fh', k // 8)], writes=[pg2], signal=(k == 15))
                for k in range(32):
                    S.op('pe', lambda: nc.tensor.matmul(pys[:], lhsT=wch[2 + k // 16][:, k % 16, :], rhs=ynf[:, k, :], start=(k == 0), stop=(k == 31)),
                         reads=[wch[2 + k // 16], ('fy', k // 8)], writes=[pys], signal=(k == 31))
                for k in range(16):
                    S.op('pe', lambda: nc.tensor.matmul(pya[:], lhsT=wch[4][:, k, :], rhs=ogf[:, k, :], start=(k == 0), stop=(k == 15)),
                         reads=[wch[4], ('fo', k // 8)], writes=[pya], signal=(k == 15))
                g1 = g1r.next(); g2 = g2r.next()
                S.op('act', lambda: nc.scalar.activation(out=g1[:], in_=pg1[:], func=AF.Sigmoid, bias=cst[:, C_GB + mc:C_GB + mc + 1], scale=1.0),
                     reads=[pg1], writes=[g1])
                S.op('act', lambda: nc.scalar.activation(out=g2[:], in_=pg2[:], func=AF.Sigmoid, bias=cst[:, C_GB + 16 + mc:C_GB + 17 + mc], scale=1.0),
                     reads=[pg2], writes=[g2])
                S.op('dve', lambda: nc.vector.tensor_tensor(out=g1[:], in0=g1[:], in1=pys[:], op=ALU.mult), reads=[g1, pys], writes=[g1])
                S.op('dve', lambda: nc.vector.tensor_tensor(out=g2[:], in0=g2[:], in1=pya[:], op=ALU.mult), reads=[g2, pya], writes=[g2])
                S.op('dve', lambda: nc.vector.tensor_tensor(out=mixT[:, mc, :], in0=g1[:], in1=g2[:], op=ALU.add), reads=[g1, g2], writes=[mixT])
            for half in range(TB // 256):
                xts = []
                xos = []
                for t2_ in range(2):
                    tt = half * 2 + t2_
                    xt = xr.next()
                    S.dma('sp', xt[:], xf_d[lo + tt * 128:lo + (tt + 1) * 128, :], writes=[xt], slot=('fx', xr.i))
                    xts.append(xt)
                    xos.append(xor_.next())
                for dblk in range(4):
                    wo = wor.next()
                    S.dma('sp', wo[:], wout_s[dblk].rearrange("p (k c) -> p k c", k=16), writes=[wo], slot=('fwo', wor.i))
                    for t2_ in range(2):
                        tt = half * 2 + t2_
                        pd = pdr.next()
                        for k in range(16):
                            S.op('pe', lambda: nc.tensor.matmul(pd[:], lhsT=mixT[:, k, tt * 128:(tt + 1) * 128], rhs=wo[:, k, :], start=(k == 0), stop=(k == 15)),
                                 reads=[wo, mixT], writes=[pd], signal=(k == 15))
                        S.op('dve', lambda: nc.vector.tensor_tensor(out=xos[t2_][:, dblk * 512:(dblk + 1) * 512], in0=pd[:], in1=xts[t2_][:, dblk * 512:(dblk + 1) * 512], op=ALU.add),
                             reads=[pd, xts[t2_]], writes=[(xos[t2_], dblk)])
                for t2_ in range(2):
                    tt = half * 2 + t2_
                    sm = smr.next()
                    xo = xos[t2_]
                    xt = xts[t2_]
                    S.op('act', lambda: nc.scalar.activation(out=xt[:], in_=xo[:], func=AF.Square, accum_out=sm[:, 0:1]),
                         reads=[(xo, q) for q in range(4)], writes=[xt, sm])
                    S.op('dve', lambda: nc.vector.tensor_scalar(out=sm[:, 1:2], in0=sm[:, 0:1], scalar1=1.0 / D, scalar2=EPS, op0=ALU.mult, op1=ALU.add),
                         reads=[sm], writes=[sm])
                    S.op('act', lambda: nc.scalar.activation(out=sm[:, 2:3], in_=sm[:, 1:2], func=AF.Sqrt), reads=[sm], writes=[sm])
                    S.op('dve', lambda: nc.vector.reciprocal(out=sm[:, 3:4], in_=sm[:, 2:3]), reads=[sm], writes=[sm])
                    S.op('dve', lambda: nc.vector.scalar_tensor_tensor(out=xt[:], in0=xo[:], scalar=sm[:, 3:4], in1=fnw[:], op0=ALU.mult, op1=ALU.mult),
                         reads=[(xo, q) for q in range(4)] + [sm, fnw], writes=[xt])
                    S.dma('pool', out_d[lo + tt * 128:lo + (tt + 1) * 128, :], xt[:], reads=[xt], slot=('fxs', xr.t.index(xt)))
        S.barrier()
    es.close()
    return nc


_OFFS = np.cumsum([0, 4096, 6144, 64, 2048, 2048, 2048, 2048, 4096])


def _chunk(w, kc, ncol_blk):
    nblk = w.shape[1] // ncol_blk
    return np.ascontiguousarray(w.reshape(kc, 128, nblk, ncol_blk).transpose(2, 1, 0, 3)).reshape(nblk, 128, kc * ncol_blk)


def _shared_consts():
    t = np.arange(128)
    cst = np.zeros((128, 1024), np.float32)
    U = (t[:, None] <= t[None, :]).astype(np.float32)
    cst[:, 256:384] = U
    cst[:, 384:512] = 1.0
    cst[:, 512:640] = (t[:, None] > t[None, :]).astype(np.float32)
    cst[:, 640:768] = 1.0
    cst[:, 768:896] = np.eye(128, dtype=np.float32)
    cstb = np.zeros((128, 1024), np.float32)
    cstb[:, 0:128] = np.eye(128)
    col = np.arange(256)
    cstb[:, 128:384] = (t[:, None] <= col[None, :])
    cstb[:, 384:640] = ((128 + t[:, None]) <= col[None, :])
    return cst, cstb.astype(ml_dtypes.bfloat16)


def _prep_core(inp, b, hh, T, cst0, cstb):
    f = np.float32
    x = np.ascontiguousarray(inp["x"][b, :T]).astype(f, copy=False)
    w_in = inp["w_in"][0]
    z0, xbc0, dt0, q0, k0, v0, g0, gm0 = [int(v) for v in _OFFS[:8]]

    def cols(a, n):
        return w_in[:, a:a + n]

    Wz = cols(z0 + hh * 2048, 2048)
    Wx = cols(xbc0 + hh * 2048, 2048)
    WB = cols(xbc0 + 4096 + hh * 512, 512)
    WC = cols(xbc0 + 5120 + hh * 512, 512)
    Wdt = cols(dt0 + hh * 32, 32)
    Wq = cols(q0 + hh * 1024, 1024)
    Wk = cols(k0 + hh * 1024, 1024)
    Wv = cols(v0 + hh * 1024, 1024)
    Wg = cols(g0 + hh * 1024, 1024)
    Wgm = cols(gm0, 4096)
    wfm = _chunk(np.concatenate([Wx, WB, WC, Wq, Wk], 1), 16, 128)
    wtm = _chunk(np.concatenate([Wz, Wv, Wg], 1), 16, 512)
    wdt = _chunk(Wdt, 16, 32)[0]
    wgm = _chunk(Wgm, 16, 128)
    wssm = _chunk(inp["w_ssm_proj"][0], 32, 128)
    wattn = _chunk(inp["w_attn_proj"][0], 16, 128)
    wout = _chunk(inp["w_out"][0], 16, 512)
    cst = cst0.copy()
    cst[:, 0:16] = inp["norm_w"][0].reshape(16, 128).T
    conv_w = inp["conv_w"][0]
    conv_b = inp["conv_b"][0]
    chans = np.concatenate([hh * 2048 + np.arange(2048), 4096 + hh * 512 + np.arange(512), 5120 + hh * 512 + np.arange(512)])
    cw = conv_w[:, chans]
    cst[:, 16:112] = cw.reshape(4, 24, 128).transpose(2, 1, 0).reshape(128, 96)
    cst[:, 112:136] = conv_b[chans].reshape(24, 128).T
    cst[:, 136:168] = np.broadcast_to(inp["dt_bias"][0][hh * 32:(hh + 1) * 32], (128, 32))
    cst[:, 168:200] = np.broadcast_to(inp["A_log"][0][hh * 32:(hh + 1) * 32], (128, 32))
    cst[:, 200:216] = inp["ssm_norm_w"][0][hh * 2048:(hh + 1) * 2048].reshape(16, 128).T
    cst[:, 216:248] = inp["gate_bias"][0].reshape(32, 128).T
    cst[:, 248] = 1.0 - hh
    cst[:, 249] = float(hh)
    dsk = np.ascontiguousarray(np.broadcast_to(np.repeat(inp["D_skip"][0][hh * 32:(hh + 1) * 32], 64), (128, 2048))).astype(f)
    fnw = np.ascontiguousarray(np.broadcast_to(inp["final_norm_w"], (128, 2048))).astype(f)
    TH = T // 2
    return {"x": x, "wfm": wfm, "wtm": wtm, "wdt": np.ascontiguousarray(wdt), "wgm": wgm, "wssm": wssm, "wattn": wattn,
            "wout": wout, "cst": cst, "dsk": dsk, "fnw": fnw, "xf": np.ascontiguousarray(x[hh * TH:(hh + 1) * TH]), "cstb": cstb}


def run(inputs, T, nb, stop=None, debug=False):
    ncores = 2 * nb
    groups = [[2 * i, 2 * i + 1] for i in range(nb)]
    nc = build(T, groups, stop, debug)
    cst0, cstb = _shared_consts()
    in_maps = [_prep_core(inputs, c // 2, c % 2, T, cst0, cstb) for c in range(ncores)]
    res = run_bass_kernel_spmd(nc, in_maps, core_ids=list(range(ncores)))
    if debug:
        return res.results, in_maps
    out = np.empty((nb, T, D), np.float32)
    TH = T // 2
    for c in range(ncores):
        out[c // 2, (c % 2) * TH:(c % 2 + 1) * TH] = res.results[c]["out"]
    return out


def kernel(**inputs):
    inputs = {k: np.asarray(v) for k, v in inputs.items()}
    return run(inputs, 8192, 4)
```
